# Optimizing a Trainium2 kernel written in Bass

```python
import jax, jax.numpy as jnp
from jax import lax
import numpy as np

D_MODEL = 2048
BATCH = 2
SEQ = 4096
DEPTH = 4

ATTN_WIDTH = D_MODEL // 2
HEAD_DIM_ATTN = 128
N_HEADS_ATTN = ATTN_WIDTH // HEAD_DIM_ATTN
DILATION_PATTERNS = ((128, 1), (512, 4), (2048, 16))
RET_WIDTH = D_MODEL // 2
N_HEADS_RET = 4
RET_V_DIM = RET_WIDTH // N_HEADS_RET
RET_QK_DIM = RET_V_DIM // 2
RET_CHUNK = 128
MIX_WIDTH = ATTN_WIDTH + RET_WIDTH
IN_SPLIT_SIZES = (ATTN_WIDTH, ATTN_WIDTH, ATTN_WIDTH, ATTN_WIDTH,
                  N_HEADS_RET * RET_QK_DIM, N_HEADS_RET * RET_QK_DIM, RET_WIDTH, RET_WIDTH)
IN_PROJ_WIDTH = sum(IN_SPLIT_SIZES)
IN_SPLIT_IDX = tuple(int(i) for i in np.cumsum(IN_SPLIT_SIZES)[:-1])
NORM_EPS = 1e-6
MASK_VALUE = -1e30

kernel_name = "hybrid_dilated_attn_retention_encoder"


def rms_norm(x, g):
    xf = x.astype(jnp.float32)
    y = xf * lax.rsqrt(jnp.mean(xf * xf, axis=-1, keepdims=True) + NORM_EPS)
    return (y * g.astype(jnp.float32)).astype(x.dtype)


def alibi_slopes(n_heads):
    return jnp.exp2(-8.0 * (jnp.arange(n_heads, dtype=jnp.float32) + 1.0) / n_heads)


def dilated_window_attention(q, k, v, window, dilation, slopes):
    b, s, h, e = q.shape
    radius = window // (2 * dilation)
    blk = radius
    sub_len = s // dilation
    n_blk = -(-sub_len // blk)
    pad_len = n_blk * blk
    def to_sub(t):
        return t.reshape(b, sub_len, dilation, h, e)
    qs = jnp.pad(to_sub(q), ((0, 0), (0, pad_len - sub_len), (0, 0), (0, 0), (0, 0)))
    qb = qs.reshape(b, n_blk, blk, dilation, h, e)
    def key_windows(t):
        tp = jnp.pad(to_sub(t), ((0, 0), (blk, blk + pad_len - sub_len), (0, 0), (0, 0), (0, 0)))
        tb = tp.reshape(b, n_blk + 2, blk, dilation, h, e)
        return jnp.concatenate([tb[:, :-2], tb[:, 1:-1], tb[:, 2:]], axis=2)
    kw = key_windows(k)
    vw = key_windows(v)
    scores = jnp.einsum('bnirhe,bnjrhe->bnrhij', qb, kw).astype(jnp.float32)
    i_idx = jnp.arange(blk)[:, None]
    j_idx = jnp.arange(3 * blk)[None, :]
    rel = j_idx - blk - i_idx
    key_pos = jnp.arange(n_blk)[:, None] * blk + jnp.arange(3 * blk)[None, :] - blk
    valid = (key_pos >= 0) & (key_pos < sub_len)
    mask = (jnp.abs(rel) <= radius)[None] & valid[:, None, :]
    dist = (jnp.abs(rel) * dilation).astype(jnp.float32)
    bias = -slopes[:, None, None] * dist[None]
    scores = jnp.where(mask[None, :, None, None], scores + bias[None, None, None], MASK_VALUE)
    m = jnp.max(scores, axis=-1, keepdims=True)
    p = jnp.exp(scores - m)
    den = jnp.sum(p, axis=-1)
    o = jnp.einsum('bnrhij,bnjrhe->bnirhe', p, vw.astype(jnp.float32))
    o = o / den.transpose(0, 1, 4, 2, 3)[..., None]
    lse = (m[..., 0] + jnp.log(den)).transpose(0, 1, 4, 2, 3)
    o = o.reshape(b, pad_len, dilation, h, e)[:, :sub_len].reshape(b, s, h, e)
    lse = lse.reshape(b, pad_len, dilation, h)[:, :sub_len].reshape(b, s, h)
    return o, lse


def dilated_attention_mixture(q, k, v):
    slopes = alibi_slopes(q.shape[2])
    outs, lses = [], []
    for window, dilation in DILATION_PATTERNS:
        o, lse = dilated_window_attention(q, k, v, window, dilation, slopes)
        outs.append(o)
        lses.append(lse)
    w = jax.nn.softmax(jnp.stack(lses, axis=0), axis=0)
    o = jnp.sum(w[..., None] * jnp.stack(outs, axis=0), axis=0)
    return o.astype(v.dtype)


def retention_one_direction(q, k, v, log_gamma):
    b, s, h, dk = q.shape
    dv = v.shape[-1]
    c = RET_CHUNK
    n = s // c
    def chunk(t):
        return t.reshape(b, n, c, h, t.shape[-1]).transpose(1, 0, 3, 2, 4)
    qc, kc, vc = chunk(q), chunk(k), chunk(v)
    idx = jnp.arange(c, dtype=jnp.float32)
    rel = idx[:, None] - idx[None, :]
    lg = log_gamma[:, None, None]
    decay = jnp.where(rel[None] >= 0, jnp.exp(jnp.maximum(rel, 0.0)[None] * lg), 0.0).astype(q.dtype)
    xi = jnp.exp((idx[None] + 1.0) * log_gamma[:, None]).astype(q.dtype)[..., None]
    zeta = jnp.exp((c - 1.0 - idx[None]) * log_gamma[:, None]).astype(q.dtype)[..., None]
    g_chunk = jnp.exp(c * log_gamma).astype(q.dtype)[:, None, None]

    def step(state, inp):
        qi, ki, vi = inp
        inner = jnp.einsum('bhid,bhjd->bhij', qi, ki) * decay
        o = jnp.einsum('bhij,bhje->bhie', inner, vi) + jnp.einsum('bhid,bhde->bhie', qi, state) * xi
        state = state * g_chunk + jnp.einsum('bhjd,bhje->bhde', ki * zeta, vi)
        return state, o

    state0 = jnp.zeros((b, h, dk, dv), dtype=q.dtype)
    _, oc = lax.scan(step, state0, (qc, kc, vc))
    return oc.transpose(1, 0, 3, 2, 4).reshape(b, s, h, dv)


def bidirectional_retention(q, k, v, decay_logit_f, decay_logit_b):
    lg_f = jax.nn.log_sigmoid(decay_logit_f.astype(jnp.float32))
    lg_b = jax.nn.log_sigmoid(decay_logit_b.astype(jnp.float32))
    o_f = retention_one_direction(q, k, v, lg_f)
    o_b = jnp.flip(retention_one_direction(jnp.flip(q, 1), jnp.flip(k, 1), jnp.flip(v, 1), lg_b), 1)
    o = (o_f + o_b).astype(jnp.float32)
    o = o * lax.rsqrt(jnp.mean(o * o, axis=-1, keepdims=True) + NORM_EPS)
    return o.astype(v.dtype)


def hybrid_layer(x, c_act, g, w_ada, b_ada, w_in, w_out, dec_f, dec_b):
    b, s, _ = x.shape
    mod = c_act @ w_ada + b_ada
    shift, scale, gate = jnp.split(mod, 3, axis=-1)
    h = rms_norm(x, g) * (1.0 + scale[:, None]) + shift[:, None]
    proj = jnp.einsum('bsd,df->bsf', h, w_in)
    q_a, k_a, v_a, z_a, q_r, k_r, v_r, z_r = jnp.split(proj, IN_SPLIT_IDX, axis=-1)
    q_a = q_a.reshape(b, s, N_HEADS_ATTN, HEAD_DIM_ATTN) * (HEAD_DIM_ATTN ** -0.5)
    k_a = k_a.reshape(b, s, N_HEADS_ATTN, HEAD_DIM_ATTN)
    v_a = v_a.reshape(b, s, N_HEADS_ATTN, HEAD_DIM_ATTN)
    y_a = dilated_attention_mixture(q_a, k_a, v_a).reshape(b, s, ATTN_WIDTH)
    q_r = q_r.reshape(b, s, N_HEADS_RET, RET_QK_DIM)
    k_r = k_r.reshape(b, s, N_HEADS_RET, RET_QK_DIM) * (RET_QK_DIM ** -0.5)
    v_r = v_r.reshape(b, s, N_HEADS_RET, RET_V_DIM)
    y_r = bidirectional_retention(q_r, k_r, v_r, dec_f, dec_b).reshape(b, s, RET_WIDTH)
    y = jnp.concatenate([y_a * jax.nn.silu(z_a), y_r * jax.nn.silu(z_r)], axis=-1)
    out = jnp.einsum('bsf,fd->bsd', y, w_out)
    return x + gate[:, None] * out


def setup_inputs(seed: int = 0) -> dict:
    key = jax.random.key(seed)
    ks = jax.random.split(key, 10)
    f32 = jnp.float32
    x = jax.random.normal(ks[0], (BATCH, SEQ, D_MODEL), f32)
    c = jax.random.normal(ks[1], (BATCH, D_MODEL), f32)
    norm_gain = 1.0 + 0.02 * jax.random.normal(ks[2], (DEPTH, D_MODEL), f32)
    w_ada = 0.5 * D_MODEL ** -0.5 * jax.random.normal(ks[3], (DEPTH, D_MODEL, 3 * D_MODEL), f32)
    b_ada = 0.02 * jax.random.normal(ks[4], (DEPTH, 3 * D_MODEL), f32)
    w_in = D_MODEL ** -0.5 * jax.random.normal(ks[5], (DEPTH, D_MODEL, IN_PROJ_WIDTH), f32)
    w_out = MIX_WIDTH ** -0.5 * jax.random.normal(ks[6], (DEPTH, MIX_WIDTH, D_MODEL), f32)
    gamma = 1.0 - jnp.exp2(-5.0 - jnp.arange(N_HEADS_RET, dtype=f32))
    base_logit = jnp.log(gamma) - jnp.log1p(-gamma)
    ret_decay_logit_f = base_logit[None] + 0.1 * jax.random.normal(ks[7], (DEPTH, N_HEADS_RET), f32)
    ret_decay_logit_b = base_logit[None] + 0.1 * jax.random.normal(ks[8], (DEPTH, N_HEADS_RET), f32)
    final_gain = 1.0 + 0.02 * jax.random.normal(ks[9], (D_MODEL,), f32)
    return {"x": x, "c": c, "norm_gain": norm_gain, "w_ada": w_ada, "b_ada": b_ada,
            "w_in": w_in, "w_out": w_out, "ret_decay_logit_f": ret_decay_logit_f,
            "ret_decay_logit_b": ret_decay_logit_b, "final_gain": final_gain}


def reference(x, c, norm_gain, w_ada, b_ada, w_in, w_out, ret_decay_logit_f, ret_decay_logit_b, final_gain):
    c_act = jax.nn.silu(c)
    h = x
    for layer in range(DEPTH):
        h = hybrid_layer(h, c_act, norm_gain[layer], w_ada[layer], b_ada[layer], w_in[layer],
                         w_out[layer], ret_decay_logit_f[layer], ret_decay_logit_b[layer])
    return rms_norm(h, final_gain)
```

```python
import numpy as np
import ml_dtypes
from contextlib import ExitStack
import concourse.bass as bass
import concourse.mybir as mybir
from concourse.bass_utils import run_bass_kernel_spmd

F32 = mybir.dt.float32
BF16 = mybir.dt.bfloat16
AF = mybir.ActivationFunctionType
ALU = mybir.AluOpType

D = 2048
T = 4096
NL = 4
TT = 512
NTT = T // TT
EPS = 1e-6
PATTERNS = (1, 4, 16)
ENGS = ("pe", "act", "dve", "pool", "sp")


def ssl(start, n, step):
    return slice(start, start + (n - 1) * step + 1, step)
NRING = 8
RECIP = "reciprocal"


class Sched:
    def __init__(self, nc, es):
        self.nc, self.es = nc, es
        self.q = {e: [] for e in ENGS}
        self.sems, self.cnt = {}, {}
        self.waited = {e: {} for e in ENGS}
        self.lastw, self.readers = {}, {}
        self.ring_idx = {"sp": 0, "pool": 0, "act": 0}
        self.pend_r, self.pend_w = set(), set()

    def sem(self, key):
        if key not in self.sems:
            self.sems[key] = self.es.enter_context(self.nc.semaphore("sem_" + str(key)))
            self.cnt[key] = 0
        return self.sems[key]

    def _deps(self, eng, reads, writes):
        evs = []
        for k in list(reads) + list(writes):
            if eng != "pe" and (k in self.pend_r or k in self.pend_w):
                raise RuntimeError(f"dependency on unsignalled PE op for key {k}")
        for k in reads:
            w = self.lastw.get(k)
            if w:
                evs.append(w)
        for k in writes:
            w = self.lastw.get(k)
            if w:
                evs.append(w)
            for sk, v in self.readers.get(k, {}).items():
                evs.append((sk, v))
        return evs

    def _waits(self, eng, evs):
        out = []
        for sk, v in evs:
            if eng == "pe" and sk == "pe":
                continue
            if self.waited[eng].get(sk, 0) >= v:
                continue
            self.waited[eng][sk] = v
            out.append((sk, v))
        return out

    def _register(self, ev, reads, writes):
        for k in reads:
            self.readers.setdefault(k, {})[ev[0]] = ev[1]
        for k in writes:
            self.lastw[k] = ev
            self.readers[k] = {}

    @staticmethod
    def _is_psum(k):
        return k == "pm" or k == "pt" or (isinstance(k, tuple) and k[0] == "pb")

    def op(self, eng, name, args=(), kw=None, reads=(), writes=(), sig=True, extra=()):
        kw = kw or {}
        self.sem(eng)
        writes = list(writes) + [k for k in reads if self._is_psum(k)]
        reads = [k for k in reads if not self._is_psum(k)]
        evs = self._deps(eng, reads, writes) + list(extra)
        waits = self._waits(eng, evs)
        if sig:
            self.cnt[eng] += 1
            ev = (eng, self.cnt[eng])
            if eng == "pe":
                reads = set(reads) | self.pend_r
                writes = set(writes) | self.pend_w
                self.pend_r, self.pend_w = set(), set()
            self._register(ev, reads, writes)
        else:
            assert eng == "pe"
            ev = None
            self.pend_r |= set(reads)
            self.pend_w |= set(writes)
        self.q[eng].append((waits, name, args, kw, ("eng", eng) if sig else None))
        return ev

    def dma(self, qeng, out, in_, reads=(), writes=(), extra=(), **kw):
        ring = self.ring_idx[qeng]
        self.ring_idx[qeng] += 1
        sk = f"d{qeng}{ring % NRING}"
        self.sem(sk)
        evs = self._deps(qeng, reads, writes) + list(extra)
        if self.cnt[sk] > 0:
            evs.append((sk, self.cnt[sk]))
        waits = self._waits(qeng, evs)
        self.cnt[sk] += 16
        ev = (sk, self.cnt[sk])
        self._register(ev, reads, writes)
        kw2 = dict(out=out, in_=in_)
        kw2.update(kw)
        self.q[qeng].append((waits, "dma_start", (), kw2, ("dma", sk)))
        return ev

    def cc(self, ins, outs, groups, reads=(), writes=()):
        self.sem("cc")
        evs = self._deps("pool", reads, writes)
        waits = self._waits("pool", evs)
        self.cnt["cc"] += 1
        ev = ("cc", self.cnt["cc"])
        self._register(ev, reads, writes)
        kw = dict(replica_groups=groups, ins=ins, outs=outs)
        self.q["pool"].append((waits, "collective_compute", ("AllGather", ALU.bypass), kw, ("cc", "cc")))
        return ev

    def barrier(self):
        evs = [(k, v) for k, v in self.cnt.items() if v > 0]
        assert not self.pend_r and not self.pend_w
        for e in ENGS:
            self.wait_only(e, [ev for ev in evs if ev[0] != e])

    def wait_only(self, eng, evs):
        waits = self._waits(eng, evs)
        self.q[eng].append((waits, None, (), {}, None))

    def emit(self):
        nc = self.nc
        block = self.es.enter_context(nc.Block())

        def mk(engname):
            def f(e):
                for waits, name, args, kw, sig in self.q[engname]:
                    for sk, v in waits:
                        e.wait_ge(self.sems[sk], v)
                    if name is None:
                        continue
                    ins = getattr(e, name)(*args, **kw)
                    if sig is None:
                        continue
                    if sig[0] == "eng":
                        ins.then_inc(self.sems[sig[1]], 1)
                    elif sig[0] == "dma":
                        ins.then_inc(self.sems[sig[1]], 16)
                    else:
                        ins.then_inc(self.sems[sig[1]])
            return f

        block.tensor(mk("pe"))
        block.scalar(mk("act"))
        block.vector(mk("dve"))
        block.gpsimd(mk("pool"))
        block.sync(mk("sp"))


def build(n_layers=NL, mix=True, dbg=False, stop=None):
    nc = bass.Bass("TRN2", target_bir_lowering=False)
    L = n_layers

    def din(name, shape, dt=F32):
        return nc.dram_tensor(name, shape, dt, kind="ExternalInput").ap()

    xT_d = din("xT", [512, T])
    cT_d = din("cT", [128, 16])
    wada_d = din("wada", [L, D, 1536])
    bada_d = din("bada", [128, NL * 12])
    gain_d = din("gain", [128, NL * 4])
    fgain_d = din("fgain", [128, 4])
    win_d = din("win", [L, D, 1792])
    wout_d = din("wout", [L, D, 512])
    dlog_d = din("dlog", [128, NL * 2])
    ident_d = din("ident", [128, 128])
    amask_d = din("amask", [128, 6 * 256])
    rconst_d = din("rconst", [128, 6 * 128 + 4])
    out_d = nc.dram_tensor("outT", [512, T], F32, kind="ExternalOutput").ap()
    dbg_d = {}
    if dbg:
        dbg_d["h"] = nc.dram_tensor("dbg_h", [D, T], BF16, kind="ExternalOutput").ap()
        dbg_d["y"] = nc.dram_tensor("dbg_y", [D, T], BF16, kind="ExternalOutput").ap()
        dbg_d["mods"] = nc.dram_tensor("dbg_mods", [128, 12], F32, kind="ExternalOutput").ap()

    ag1_in = nc.dram_tensor("ag1_in", [1, T], F32).ap()
    ag1_out = nc.dram_tensor("ag1_out", [4, T], F32).ap()
    ag2_in = nc.dram_tensor("ag2_in", [NTT, 512, TT], BF16).ap()
    ag2_out = nc.dram_tensor("ag2_out", [NTT, D, TT], BF16).ap()
    ag3a_in = nc.dram_tensor("ag3a_in", [NTT, 256, TT], BF16).ap()
    ag3a_out = nc.dram_tensor("ag3a_out", [NTT, 1024, TT], BF16).ap()
    ag3r_in = nc.dram_tensor("ag3r_in", [NTT, 256, TT], BF16).ap()
    ag3r_out = nc.dram_tensor("ag3r_out", [NTT, 1024, TT], BF16).ap()
    GROUPS = [[0, 1, 2, 3], [4, 5, 6, 7]]

    with ExitStack() as es:
        def sb(name, shape, dt):
            return es.enter_context(nc.sbuf_tensor("sb_" + name, shape, dt))

        def ps(name, shape, dt):
            return es.enter_context(nc.psum_tensor("ps_" + name, shape, dt))

        X = sb("X", [128, 4, T], F32)
        HY = sb("HY", [128, 2, 16, TT], BF16)
        WB = sb("WB", [128, 16 * 512], BF16)
        PO = sb("PO", [128, 4, T], BF16)
        MT = sb("MT", [128, 16384], BF16)
        ident = sb("ident", [128, 128], BF16)
        ones_bf = sb("ones_bf", [128, 128], BF16)
        ones_f = sb("ones_f", [128, 128], F32)
        amask = sb("amask", [128, 6, 256], F32)
        rconst = sb("rconst", [128, 6 * 128 + 4], F32)
        cT = sb("cT", [128, 16], F32)
        cA = sb("cA", [128, 16], BF16)
        bada = sb("bada", [128, NL * 12], F32)
        gain = sb("gain", [128, NL * 4], F32)
        fgain = sb("fgain", [128, 4], F32)
        dlog = sb("dlog", [128, NL * 2], F32)
        lg = sb("lg", [128, NL * 2], F32)
        modA = sb("modA", [128, NL * 4], F32)
        modB = sb("modB", [128, NL * 4], F32)
        modG = sb("modG", [128, NL * 4], F32)
        modrow = sb("modrow", [1, 1536], F32)
        rM = sb("rM", [128, 128], F32)
        rtmp = sb("rtmp", [128, 128], F32)
        rxi = sb("rxi", [128, 2, 128], F32)
        rcol = sb("rcol", [128, 4], F32)
        tA = sb("tA", [128, TT], F32)
        rstd = sb("rstd", [128, TT], F32)
        ssq_st = sb("ssq_st", [1, TT], F32)
        ssq4 = sb("ssq4", [4, TT], F32)

        PB = [ps(f"pb{i}", [128, 512], F32) for i in range(6)]
        PT = ps("pt", [128, 8, 128], BF16)
        PM = ps("pm", [128, 512], F32)

        K = Sched(nc, es)

        K.dma("pool", ident[:], ident_d, writes=["ident"])
        K.dma("sp", amask[:], amask_d.rearrange("p (a b) -> p a b", b=256), writes=["amask"])
        K.dma("sp", rconst[:], rconst_d, writes=["rconst"])
        K.dma("sp", cT[:], cT_d, writes=["cT"])
        K.dma("sp", bada[:], bada_d, writes=["bada"])
        K.dma("sp", gain[:], gain_d, writes=["gain"])
        K.dma("sp", fgain[:], fgain_d, writes=["fgain"])
        K.dma("sp", dlog[:], dlog_d, writes=["dlog"])
        for c in range(4):
            K.dma("sp", X[:, c, :], xT_d[c * 128:(c + 1) * 128, :], writes=[("x", c, tt) for tt in range(NTT)])
        K.op("dve", "memset", (ones_bf[:], 1.0), writes=["ones_bf"])
        K.op("dve", "memset", (ones_f[:], 1.0), writes=["ones_f"])
        K.op("act", "activation", (), dict(out=cA[:], in_=cT[:], func=AF.Silu), reads=["cT"], writes=["cA"])
        K.op("act", "activation", (), dict(out=lg[:], in_=dlog[:], func=AF.Exp, scale=-1.0), reads=["dlog"], writes=["lg"])
        K.op("act", "activation", (), dict(out=lg[:], in_=lg[:], func=AF.Ln, bias=1.0), reads=["lg"], writes=["lg"])
        K.op("dve", "tensor_scalar", (lg[:], lg[:], -1.0, None, ALU.mult), reads=["lg"], writes=["lg"])

        SQD = float(np.sqrt(D))

        def mods_load(l2, grp, stage):
            for q4 in range(4):
                K.dma("pool", stage[:, q4 * 4:(q4 + 1) * 4, :],
                      wada_d[l2, q4 * 512:(q4 + 1) * 512, grp * 512:(grp + 1) * 512].rearrange("(k p) n -> p k n", p=128),
                      writes=[("hy", 1)])

        def mods_mm(l2, grp, stage):
            for kc in range(16):
                K.op("pe", "matmul", (PM[0:1, :], cA[:, kc:kc + 1], stage[:, kc, :]),
                     dict(start=(kc == 0), stop=(kc == 15)), reads=[("hy", 1), "cA"], writes=["pm"], sig=(kc == 15))
            K.op("act", "activation", (), dict(out=modrow[0:1, grp * 512:(grp + 1) * 512], in_=PM[0:1, :], func=AF.Copy),
                 reads=["pm"], writes=["modrow"])

        def mods_finish(l2):
            for j in range(12):
                K.op("pe", "matmul", (PM[:, j:j + 1], modrow[0:1, j * 128:(j + 1) * 128], ones_f[0:1, 0:1]),
                     dict(start=True, stop=True), reads=["modrow", "ones_f"], writes=["pm"], sig=(j == 11))
            sl4 = slice(l2 * 4, l2 * 4 + 4)
            K.op("dve", "tensor_tensor", (modB[:, sl4], PM[:, 0:4], bada[:, l2 * 12:l2 * 12 + 4], ALU.add),
                 reads=["pm", "bada"], writes=[("modB", l2)])
            K.op("dve", "tensor_tensor", (modA[:, sl4], PM[:, 4:8], bada[:, l2 * 12 + 4:l2 * 12 + 8], ALU.add),
                 reads=["pm", "bada"], writes=[("modA", l2)])
            K.op("dve", "scalar_tensor_tensor", (modA[:, sl4], modA[:, sl4], 1.0, gain[:, sl4], ALU.add, ALU.mult),
                 reads=[("modA", l2), "gain"], writes=[("modA", l2)])
            K.op("dve", "tensor_scalar", (modA[:, sl4], modA[:, sl4], SQD, None, ALU.mult), reads=[("modA", l2)], writes=[("modA", l2)])
            K.op("dve", "tensor_tensor", (modG[:, sl4], PM[:, 8:12], bada[:, l2 * 12 + 8:l2 * 12 + 12], ALU.add),
                 reads=["pm", "bada"], writes=[("modG", l2)])

        STG = HY[:, 1, :, :]
        for grp in range(3):
            mods_load(0, grp, STG)
            mods_mm(0, grp, STG)
        mods_finish(0)
        K.op("dve", "tensor_scalar", (fgain[:], fgain[:], SQD, None, ALU.mult), reads=["fgain"], writes=["fgain"])
        if dbg:
            K.op("dve", "tensor_copy", (tA[:, 0:4], modB[:, 0:4]), reads=[("modB", 0)], writes=["tA"])
            K.op("dve", "tensor_copy", (tA[:, 4:8], modA[:, 0:4]), reads=[("modA", 0)], writes=["tA"])
            K.op("dve", "tensor_copy", (tA[:, 8:12], modG[:, 0:4]), reads=[("modG", 0)], writes=["tA"])
            K.dma("sp", dbg_d["mods"], tA[:, 0:12], reads=["tA"])

        def finish_raw():
            outs = []
            for c in range(4):
                outs.append(K.dma("sp", out_d[c * 128:(c + 1) * 128, :], X[:, c, :], reads=[("x", c, tt) for tt in range(NTT)]))
            K.wait_only("sp", outs)
            K.emit()

        if stop == "pro":
            finish_raw()
            return nc

        def norm_stats():
            for tt in range(NTT):
                tsl = slice(tt * TT, (tt + 1) * TT)
                hs = HY[:, tt % 2, 0:4, :]
                for c in range(4):
                    K.op("act", "activation", (), dict(out=hs[:, c, :], in_=X[:, c, tsl], func=AF.Square),
                         reads=[("x", c, tt)], writes=[("hy", tt % 2)])
                for c in range(4):
                    K.op("pe", "matmul", (PM[0:1, :], ones_bf[:, 0:1], hs[:, c, :]), dict(start=(c == 0), stop=(c == 3)),
                         reads=[("hy", tt % 2), "ones_bf"], writes=["pm"], sig=(c == 3))
                K.op("act", "activation", (), dict(out=ssq_st[0:1, :], in_=PM[0:1, :], func=AF.Copy),
                     reads=["pm"], writes=["ssq_st"])
                K.dma("sp", ag1_in[0:1, tsl], ssq_st[0:1, :], reads=["ssq_st"], writes=["ag1_in"])
            K.cc([ag1_in], [ag1_out], GROUPS, reads=["ag1_in"], writes=["ag1_out"])

        def rstd_tile(tt):
            tsl = slice(tt * TT, (tt + 1) * TT)
            K.dma("sp", ssq4[0:4, :], ag1_out[0:4, tsl], reads=["ag1_out"], writes=["ssq4"])
            K.op("pe", "matmul", (PM[:, :], ones_f[0:4, :], ssq4[0:4, :]), dict(start=True, stop=True),
                 reads=["ssq4", "ones_f"], writes=["pm"])
            K.op("act", "activation", (), dict(out=rstd[:], in_=PM[:, :], func=AF.Sqrt, bias=epsc[:, 0:1]),
                 reads=["pm", "epsc"], writes=["rstd"])
            K.op("dve", RECIP, (rstd[:], rstd[:]), reads=["rstd"], writes=["rstd"])

        epsc = sb("epsc", [128, 2], F32)
        K.op("dve", "memset", (epsc[:, 0:1], D * EPS), writes=["epsc"])
        K.op("dve", "memset", (epsc[:, 1:2], 256 * EPS), writes=["epsc"])

        def load_tile(src, tt, slot):
            if src is ag2_out:
                parts = [(ag2_out, "ag2_out", 0, 16)]
            else:
                parts = [(ag3a_out, "ag3a_out", 0, 8), (ag3r_out, "ag3r_out", 8, 8)]
            for sap, skey, k0, nk in parts:
                v = sap[tt].rearrange("(k p) t -> p k t", p=128)
                for q4 in range(nk // 4):
                    K.dma("sp", HY[:, slot, k0 + q4 * 4:k0 + (q4 + 1) * 4, :], v[:, q4 * 4:(q4 + 1) * 4, :],
                          reads=[(skey, tt)], writes=[("hy", slot)])

        def load_w(src2d, ncols, dst=None, key="W"):
            base = WB.ap() if dst is None else dst
            Wv = base[:, 0:16 * ncols].rearrange("p (k n) -> p k n", k=16)
            for q4 in range(4):
                K.dma("pool", Wv[:, q4 * 4:(q4 + 1) * 4, :],
                      src2d[q4 * 512:(q4 + 1) * 512, :].rearrange("(k p) n -> p k n", p=128), writes=[key])
            return Wv

        pbi = [0]

        def next_pb(n=2):
            i = pbi[0] % n
            pbi[0] += 1
            return i

        def proj_pass(l, col0, nch, evac, Wv=None, post_tile=None):
            if Wv is None:
                Wv = load_w(win_d[l, :, col0 * 128:(col0 + nch) * 128], nch * 128)
            for tt in range(NTT):
                slot = tt % 2
                load_tile(ag2_out, tt, slot)
                for ch in range(nch):
                    b = next_pb()
                    for kc in range(16):
                        K.op("pe", "matmul", (PB[b][:, :], Wv[:, kc, ch * 128:(ch + 1) * 128], HY[:, slot, kc, :]),
                             dict(start=(kc == 0), stop=(kc == 15)), reads=["W", ("hy", slot)], writes=[("pb", b)],
                             sig=(kc == 15))
                    evac(ch, tt, PB[b][:, :], ("pb", b))
                if post_tile is not None:
                    post_tile(tt)


        WBa = WB.ap()
        MTa = MT.ap()
        acc_o = MTa[:, 0:8192].bitcast(F32)
        acc_d = MTa[:, 8192:16384].bitcast(F32)
        VT = WBa[:, 0:4096].rearrange("p (i e) -> p i e", e=128)
        Es = [WBa[:, 4096 + 512 * i:4096 + 512 * (i + 1)].bitcast(F32) for i in range(2)]
        Ps = [WBa[:, 5120 + 256 * i:5120 + 256 * (i + 1)] for i in range(2)]
        SCALE = 128.0 ** -0.5
        PT2 = PM.ap().bitcast(BF16).rearrange("p (i e) -> p i e", e=128)
        PTS = [(PT, "pt"), (PT2, "pm")]

        def attention_head(l, hh):
            def evac(ch, tt, pap, bk):
                tsl = slice(tt * TT, (tt + 1) * TT)
                if ch == 0:
                    K.op("act", "activation", (), dict(out=PO[:, 0, tsl], in_=pap, func=AF.Copy, scale=SCALE),
                         reads=[bk], writes=[("po", 0)])
                elif ch == 3:
                    K.op("act", "activation", (), dict(out=PO[:, 3, tsl], in_=pap, func=AF.Silu),
                         reads=[bk], writes=[("po", 3)])
                else:
                    K.op("dve", "tensor_copy", (PO[:, ch, tsl], pap), reads=[bk], writes=[("po", ch)])
            proj_pass(l, hh * 4, 4, evac, Wv=(W_A0 if hh == 0 else None))
            K.barrier()
            if l + 1 < L:
                mods_load(l + 1, hh, STG)
            for pi, d in enumerate(PATTERNS):
                Wm = amask[:, hh * 3 + pi, :]
                Ls = T // d
                nkt = Ls // 128
                for idx in range(32):
                    r, m = idx // nkt, idx % nkt
                    tok0 = r + d * 128 * m
                    half = (idx // 4) % 2
                    PTb, ptk = PTS[half]
                    K.op("pe", "transpose", (PTb[:, idx % 4, :], PO[:, 2, ssl(tok0, 128, d)], ident[:]),
                         reads=[("po", 2), "ident"], writes=[ptk], sig=(idx % 4 == 3))
                    if idx % 4 == 3:
                        if half == 0:
                            K.op("act", "activation", (), dict(out=VT[:, idx - 3:idx + 1, :], in_=PTb[:, 0:4, :], func=AF.Copy),
                                 reads=[ptk], writes=[("vt", idx // 4)])
                        else:
                            K.op("dve", "tensor_copy", (VT[:, idx - 3:idx + 1, :], PTb[:, 0:4, :]),
                                 reads=[ptk], writes=[("vt", idx // 4)])
                tiles = [(r, m) for r in range(d) for m in range(nkt)]

                def geom(r, m):
                    c_lo = 64 if m == 0 else 0
                    c_hi = 192 if m == nkt - 1 else 256
                    return c_lo, c_hi

                def emit_S(i):
                    r, m = tiles[i]
                    c_lo, c_hi = geom(r, m)
                    nq = c_hi - c_lo
                    tq0 = r + d * (128 * m - 64 + c_lo)
                    kt0 = r + d * 128 * m
                    sbk = i % 2
                    K.op("pe", "matmul", (PB[sbk][:, 0:nq], PO[:, 1, ssl(kt0, 128, d)], PO[:, 0, ssl(tq0, nq, d)]),
                         dict(start=True, stop=True), reads=[("po", 0), ("po", 1)], writes=[("pb", sbk)])

                def emit_EP(i):
                    r, m = tiles[i]
                    c_lo, c_hi = geom(r, m)
                    nq = c_hi - c_lo
                    sbk = i % 2
                    K.op("act", "activation", (), dict(out=Es[sbk][:, 0:nq], in_=PB[sbk][:, 0:nq], func=AF.Exp),
                         reads=[("pb", sbk)], writes=[("E", sbk)])
                    K.op("dve", "tensor_tensor", (Ps[sbk][:, c_lo:c_hi], Es[sbk][:, 0:nq], Wm[:, c_lo:c_hi], ALU.mult),
                         reads=[("E", sbk), "amask"], writes=[("P", sbk)])

                def emit_PV(i):
                    r, m = tiles[i]
                    c_lo, c_hi = geom(r, m)
                    sbk = i % 2
                    vt = VT[:, r * nkt + m, :]
                    vtk = ("vt", (r * nkt + m) // 4)
                    bka = (m // 4) % 2
                    ca = (m % 4) * 128
                    for which, lhs, base in (("o", vt, 2), ("d", ones_bf[:, :], 4)):
                        K.op("pe", "matmul", (PB[base + bka][:, ca + c_lo:ca + 128], lhs, Ps[sbk][:, c_lo:128]),
                             dict(start=(m == 0), stop=True), reads=[("P", sbk), vtk, "ones_bf"],
                             writes=[("pb", base + bka)], sig=(which == "d"))
                    bkb = ((m + 1) // 4) % 2
                    cb = ((m + 1) % 4) * 128
                    for which, lhs, base in (("o", vt, 2), ("d", ones_bf[:, :], 4)):
                        K.op("pe", "matmul", (PB[base + bkb][:, cb:cb + c_hi - 128], lhs, Ps[sbk][:, 128:c_hi]),
                             dict(start=True, stop=(m == nkt - 1)), reads=[("P", sbk), vtk, "ones_bf"],
                             writes=[("pb", base + bkb)], sig=(which == "d"))
                    groups = []
                    if m % 4 == 3:
                        groups.append(m // 4)
                    if m == nkt - 1:
                        groups.append(nkt // 4)
                    for k in groups:
                        jmax = min(4 * k + 3, nkt)
                        lo = 64 if k == 0 else 0
                        hi = (jmax % 4) * 128 + (64 if jmax == nkt else 128)
                        sub0 = 128 * 4 * k - 64 + lo
                        n = hi - lo
                        t0 = r + d * sub0
                        bk = k % 2
                        for acc, base, key in ((acc_o, 2, "acco"), (acc_d, 4, "accd")):
                            dst = acc[:, ssl(t0, n, d)]
                            if pi == 0:
                                K.op("dve", "tensor_copy", (dst, PB[base + bk][:, lo:hi]), reads=[("pb", base + bk)], writes=[key])
                            else:
                                K.op("dve", "tensor_tensor", (dst, dst, PB[base + bk][:, lo:hi], ALU.add),
                                     reads=[("pb", base + bk), key], writes=[key])

                emit_S(0)
                for i in range(len(tiles)):
                    emit_EP(i)
                    if i + 1 < len(tiles):
                        emit_S(i + 1)
                    emit_PV(i)
            if l + 1 < L:
                mods_mm(l + 1, hh, STG)
            for tt in range(NTT):
                tsl = slice(tt * TT, (tt + 1) * TT)
                K.op("dve", "reciprocal", (acc_d[:, tsl], acc_d[:, tsl]), reads=["accd"], writes=[("accd", tt)])
                K.op("pool", "tensor_tensor", (acc_o[:, tsl], acc_o[:, tsl], acc_d[:, tsl], ALU.mult), reads=["acco", ("accd", tt)], writes=[("acco", tt)])
                K.op("pool", "tensor_tensor", (ystage[:, tsl], acc_o[:, tsl], PO[:, 3, tsl], ALU.mult),
                     reads=[("acco", tt), ("po", 3)], writes=[("hy", 0)])
            K.dma("sp", ag3a_in[:, hh * 128:(hh + 1) * 128, :].rearrange("n p t -> p n t"),
                  ystage[:, 0:T].rearrange("p (n t) -> p n t", t=TT), reads=[("hy", 0)], writes=[("ag3a_in", tt) for tt in range(NTT)])
            if hh == 1:
                for tt in range(NTT):
                    K.cc([ag3a_in[tt]], [ag3a_out[tt]], GROUPS, reads=[("ag3a_in", tt)], writes=[("ag3a_out", tt)])
            K.barrier()

        Sb_all = MTa[:, 0:8192].rearrange("p (n e) -> p n e", e=256)
        VTr = MTa[:, 8192:16384].rearrange("p (n e) -> p n e", e=256)
        kzs = [WBa[:, 128 * i:128 * (i + 1)] for i in range(2)]
        Prs = [WBa[:, 256 + 128 * i:256 + 128 * (i + 1)] for i in range(2)]
        qxi = [WBa[:, 512 + 512 * i:512 + 512 * (i + 1)] for i in range(2)]
        Sst = [WBa[:, 1536 + 512 * i:1536 + 512 * (i + 1)].bitcast(F32) for i in range(2)]
        Sfb = [WBa[:, 2560 + 256 * i:2560 + 256 * (i + 1)] for i in range(2)]
        sqs = WBa[:, 3072:4096].rearrange("p (c t) -> p c t", c=2)
        rs = WBa[:, 4096:5120].bitcast(F32)
        RC = 768
        ysr = [MTa[:, 8192 + 1024 * i:8192 + 1024 * (i + 1)].rearrange("p (c t) -> p c t", c=2) for i in range(2)]
        Wo = [None]

        def retention_head(l):
            lgf = lg[:, 2 * l:2 * l + 1]
            lgb = lg[:, 2 * l + 1:2 * l + 2]
            K.op("act", "activation", (), dict(out=rM[:], in_=rconst[:, 0:128], func=AF.Exp, scale=lgf), reads=["rconst", "lg"], writes=["rM"])
            K.op("dve", "tensor_tensor", (rM[:], rM[:], rconst[:, 128:256], ALU.mult), reads=["rM", "rconst"], writes=["rM"])
            K.op("act", "activation", (), dict(out=rtmp[:], in_=rconst[:, 256:384], func=AF.Exp, scale=lgb), reads=["rconst", "lg"], writes=["rtmp"])
            K.op("dve", "tensor_tensor", (rtmp[:], rtmp[:], rconst[:, 384:512], ALU.mult), reads=["rtmp", "rconst"], writes=["rtmp"])
            K.op("dve", "tensor_tensor", (rM[:], rM[:], rtmp[:], ALU.add), reads=["rM", "rtmp"], writes=["rM"])
            K.op("act", "activation", (), dict(out=rxi[:, 0, :], in_=rconst[:, 512:640], func=AF.Exp, scale=lgf), reads=["rconst", "lg"], writes=["rxi"])
            K.op("act", "activation", (), dict(out=rxi[:, 1, :], in_=rconst[:, 640:768], func=AF.Exp, scale=lgb), reads=["rconst", "lg"], writes=["rxi"])
            K.op("act", "activation", (), dict(out=rcol[:, 0:1], in_=rconst[:, RC:RC + 1], func=AF.Exp, scale=lgf), reads=["rconst", "lg"], writes=["rcol"])
            K.op("act", "activation", (), dict(out=rcol[:, 1:2], in_=rconst[:, RC + 1:RC + 2], func=AF.Exp, scale=lgb), reads=["rconst", "lg"], writes=["rcol"])
            K.op("act", "activation", (), dict(out=rcol[:, 2:3], in_=rconst[:, RC + 2:RC + 3], func=AF.Exp, scale=lgf), reads=["rconst", "lg"], writes=["rcol"])
            K.op("act", "activation", (), dict(out=rcol[:, 3:4], in_=rconst[:, RC + 2:RC + 3], func=AF.Exp, scale=lgb), reads=["rconst", "lg"], writes=["rcol"])

            def evac(ch, tt, pap, bk):
                tsl = slice(tt * TT, (tt + 1) * TT)
                if ch == 1:
                    K.op("act", "activation", (), dict(out=PO[:, 1, tsl], in_=pap, func=AF.Copy, scale=SCALE),
                         reads=[bk], writes=[("po", 1)])
                elif ch == 0:
                    K.op("act", "activation", (), dict(out=PO[:, 0, tsl], in_=pap, func=AF.Copy), reads=[bk], writes=[("po", 0)])
                else:
                    K.op("dve", "tensor_copy", (PO[:, ch, tsl], pap), reads=[bk], writes=[("po", ch)])
            proj_pass(l, 8, 4, evac)
            K.barrier()
            if l + 1 < L:
                mods_load(l + 1, 2, STG)
            K.op("dve", "memset", (Sst[0], 0.0), writes=["Sf"])
            K.op("dve", "memset", (Sst[1], 0.0), writes=["Sb"])
            for n in range(31, -1, -1):
                csl = slice(n * 128, (n + 1) * 128)
                half = n % 2
                PTb, ptk = PTS[half]
                for c2 in range(2):
                    K.op("pe", "transpose", (PTb[:, c2, :], PO[:, 2 + c2, csl], ident[:]),
                         reads=[("po", 2 + c2), "ident"], writes=[ptk], sig=False)
                K.op("pe", "transpose", (PTb[:, 2, :], PO[:, 1, csl], ident[:]),
                     reads=[("po", 1), "ident"], writes=[ptk], sig=True)
                K.op("act", "activation", (), dict(out=VTr[:, n, :].rearrange("p (c e) -> p c e", c=2), in_=PTb[:, 0:2, :], func=AF.Copy),
                     reads=[ptk], writes=[("vtr", n)])
                K.op("dve", "tensor_scalar", (kzs[half], PTb[:, 2, :], rcol[:, 1:2], None, ALU.mult),
                     reads=[ptk, "rcol"], writes=[("kz", half)])
                K.op("act", "activation", (), dict(out=Sb_all[:, n, :], in_=Sst[1], func=AF.Copy), reads=["Sb"], writes=[("sball", n)])
                b = next_pb()
                K.op("pe", "matmul", (PB[b][:, 0:256], kzs[half], VTr[:, n, :]), dict(start=True, stop=True),
                     reads=[("kz", half), ("vtr", n)], writes=[("pb", b)])
                K.op("dve", "scalar_tensor_tensor", (Sst[1], Sst[1], rcol[:, 3:4], PB[b][:, 0:256], ALU.mult, ALU.add),
                     reads=[("pb", b), "Sb", "rcol"], writes=["Sb"])
            for gq in range(NTT):
                tsl = slice(gq * TT, (gq + 1) * TT)
                for ci in range(4):
                    csl = slice(gq * TT + ci * 128, gq * TT + (ci + 1) * 128)
                    K.op("pool", "tensor_tensor", (qxi[0][:, ci * 128:(ci + 1) * 128], PO[:, 0, csl], rxi[:, 0, :], ALU.mult),
                         reads=[("po", 0), "rxi"], writes=["qxi0"])
                    K.op("pool", "tensor_tensor", (qxi[1][:, ci * 128:(ci + 1) * 128], PO[:, 0, csl], rxi[:, 1, :], ALU.mult),
                         reads=[("po", 0), "rxi"], writes=["qxi1"])
                ob = 2 + 2 * (gq % 2)
                for ci in range(4):
                    n = gq * 4 + ci
                    csl = slice(n * 128, (n + 1) * 128)
                    half = n % 2
                    K.op("pe", "transpose", (PT[:, 0, :], PO[:, 1, csl], ident[:]),
                         reads=[("po", 1), "ident"], writes=["pt"], sig=True)
                    K.op("dve", "tensor_scalar", (kzs[half], PT[:, 0, :], rcol[:, 0:1], None, ALU.mult),
                         reads=["pt", "rcol"], writes=[("kz", half)])
                    b = next_pb()
                    K.op("pe", "matmul", (PB[b][:, 0:128], PO[:, 1, csl], PO[:, 0, csl]), dict(start=True, stop=True),
                         reads=[("po", 0), ("po", 1)], writes=[("pb", b)])
                    K.op("dve", "tensor_tensor", (Prs[half], PB[b][:, 0:128], rM[:], ALU.mult), reads=[("pb", b), "rM"], writes=[("Pr", half)])
                    for c2 in range(2):
                        dst = PB[ob + c2][:, ci * 128:(ci + 1) * 128]
                        terms = [(VTr[:, n, c2 * 128:(c2 + 1) * 128], Prs[half], [("vtr", n), ("Pr", half)])]
                        if n > 0:
                            terms.append((Sfb[n % 2][:, c2 * 128:(c2 + 1) * 128], qxi[0][:, ci * 128:(ci + 1) * 128], [("Sfb", n % 2), "qxi0"]))
                        if n < 31:
                            terms.append((Sb_all[:, n, c2 * 128:(c2 + 1) * 128], qxi[1][:, ci * 128:(ci + 1) * 128], [("sball", n), "qxi1"]))
                        for ti, (lhs, rhs, rk) in enumerate(terms):
                            K.op("pe", "matmul", (dst, lhs, rhs), dict(start=(ti == 0), stop=(ti == len(terms) - 1)),
                                 reads=rk, writes=[("pb", ob + c2)], sig=(ti == len(terms) - 1))
                    b = next_pb()
                    K.op("pe", "matmul", (PB[b][:, 0:256], kzs[half], VTr[:, n, :]), dict(start=True, stop=True),
                         reads=[("kz", half), ("vtr", n)], writes=[("pb", b)])
                    K.op("dve", "scalar_tensor_tensor", (Sst[0], Sst[0], rcol[:, 2:3], PB[b][:, 0:256], ALU.mult, ALU.add),
                         reads=[("pb", b), "Sf", "rcol"], writes=["Sf"])
                    K.op("act", "activation", (), dict(out=Sfb[(n + 1) % 2], in_=Sst[0], func=AF.Copy), reads=["Sf"], writes=[("Sfb", (n + 1) % 2)])
                for c2 in range(2):
                    K.op("act", "activation", (), dict(out=sqs[:, c2, :], in_=PB[ob + c2][:, :], func=AF.Square),
                         reads=[("pb", ob + c2)], writes=[("sq", c2)])
                for c2 in range(2):
                    K.op("pe", "matmul", (PM[:, :], ones_bf[:, :], sqs[:, c2, :]), dict(start=(c2 == 0), stop=(c2 == 1)),
                         reads=[("sq", c2), "ones_bf"], writes=["pm"], sig=(c2 == 1))
                K.op("act", "activation", (), dict(out=rs, in_=PM[:, :], func=AF.Sqrt, bias=epsc[:, 1:2]),
                     reads=["pm", "epsc"], writes=["rs"])
                K.op("dve", RECIP, (rs, rs), reads=["rs"], writes=["rs"])
                for c2 in range(2):
                    K.op("dve", "scalar_tensor_tensor", (PO[:, 2 + c2, tsl], PB[ob + c2][:, :], 16.0, rs, ALU.mult, ALU.mult),
                         reads=[("pb", ob + c2), "rs"], writes=[("po", 2 + c2)])
            if l + 1 < L:
                mods_mm(l + 1, 2, STG)
                mods_finish(l + 1)
            K.barrier()
            Wo[0] = load_w(wout_d[l, :, :], 512, dst=MTa, key="Wo")

            def evac2(ch, tt, pap, bk):
                tsl = slice(tt * TT, (tt + 1) * TT)
                K.op("act", "activation", (), dict(out=PO[:, ch, tsl], in_=pap, func=AF.Silu), reads=[bk], writes=[("po", ch)])

            def post(tt):
                tsl = slice(tt * TT, (tt + 1) * TT)
                ys = ysr[tt % 2]
                for c2 in range(2):
                    K.op("pool", "tensor_tensor", (ys[:, c2, :], PO[:, 2 + c2, tsl], PO[:, c2, tsl], ALU.mult),
                         reads=[("po", 2 + c2), ("po", c2)], writes=[("ysr", tt % 2)])
                K.dma("sp", ag3r_in[tt].rearrange("(c p) t -> p c t", p=128), ys, reads=[("ysr", tt % 2)], writes=[("ag3r_in", tt)])
                K.cc([ag3r_in[tt]], [ag3r_out[tt]], GROUPS, reads=[("ag3r_in", tt)], writes=[("ag3r_out", tt)])
            proj_pass(l, 12, 2, evac2, post_tile=post)
            K.barrier()

        ystage = HY[:, 0, :, :].rearrange("p k t -> p (k t)")
        for l in range(L):
            A_ = modA[:, l * 4:l * 4 + 4]
            B_ = modB[:, l * 4:l * 4 + 4]
            G_ = modG[:, l * 4:l * 4 + 4]
            W_A0 = load_w(win_d[l, :, 0:512], 512)
            norm_stats()
            if stop == "n1":
                finish_raw()
                return nc
            for tt in range(NTT):
                tsl = slice(tt * TT, (tt + 1) * TT)
                rstd_tile(tt)
                slot = tt % 2
                for c in range(4):
                    K.op("dve", "scalar_tensor_tensor", (tA[:], X[:, c, tsl], A_[:, c:c + 1], rstd[:], ALU.mult, ALU.mult),
                         reads=[("x", c, tt), "rstd", ("modA", l)], writes=["tA"])
                    K.op("act", "activation", (), dict(out=HY[:, slot, c, :], in_=tA[:], func=AF.Identity, bias=B_[:, c:c + 1]),
                         reads=["tA", ("modB", l)], writes=[("hy", slot)])
                K.dma("sp", ag2_in[tt].rearrange("(c p) t -> p c t", p=128), HY[:, slot, 0:4, :],
                      reads=[("hy", slot)], writes=[("ag2_in", tt)])
                K.cc([ag2_in[tt]], [ag2_out[tt]], GROUPS, reads=[("ag2_in", tt)], writes=[("ag2_out", tt)])
            if dbg and l == 0:
                for tt in range(NTT):
                    load_tile(ag2_out, tt, tt % 2)
                    K.dma("sp", dbg_d["h"].rearrange("(k p) t -> p k t", p=128)[:, :, tt * TT:(tt + 1) * TT], HY[:, tt % 2, :, :],
                          reads=[("hy", tt % 2)])

            if stop == "n2":
                finish_raw()
                return nc
            if mix:
                attention_head(l, 0)
                attention_head(l, 1)
                retention_head(l)
            if dbg and l == 0:
                for tt in range(NTT):
                    load_tile(None, tt, tt % 2)
                    K.dma("sp", dbg_d["y"].rearrange("(k p) t -> p k t", p=128)[:, :, tt * TT:(tt + 1) * TT], HY[:, tt % 2, :, :],
                          reads=[("hy", tt % 2)])

            if stop == "m":
                finish_raw()
                return nc
            Wv = Wo[0]
            for tt in range(NTT):
                tsl = slice(tt * TT, (tt + 1) * TT)
                slot = tt % 2
                load_tile(None, tt, slot)
                for c in range(4):
                    b = next_pb()
                    for kc in range(16):
                        K.op("pe", "matmul", (PB[b][:, :], Wv[:, kc, c * 128:(c + 1) * 128], HY[:, slot, kc, :]),
                             dict(start=(kc == 0), stop=(kc == 15)), reads=["Wo", ("hy", slot)], writes=[("pb", b)],
                             sig=(kc == 15))
                    K.op("dve", "scalar_tensor_tensor", (X[:, c, tsl], PB[b][:, :], G_[:, c:c + 1], X[:, c, tsl], ALU.mult, ALU.add),
                         reads=[("pb", b), ("x", c, tt), ("modG", l)], writes=[("x", c, tt)])

        if stop == "o":
            finish_raw()
            return nc
        norm_stats()
        outs = []
        for tt in range(NTT):
            tsl = slice(tt * TT, (tt + 1) * TT)
            rstd_tile(tt)
            for c in range(4):
                K.op("dve", "scalar_tensor_tensor", (tA[:], X[:, c, tsl], fgain[:, c:c + 1], rstd[:], ALU.mult, ALU.mult),
                     reads=[("x", c, tt), "rstd", "fgain"], writes=["tA"])
                outs.append(K.dma("sp", out_d[c * 128:(c + 1) * 128, tsl], tA[:], reads=["tA"]))
        K.wait_only("sp", outs)
        K.emit()
    return nc


def _host_consts():
    p = np.arange(128)[:, None]
    c = np.arange(256)[None, :]
    rel = np.abs(c - 64 - p).astype(np.float64)
    return rel


def prep_inputs(x, c, norm_gain, w_ada, b_ada, w_in, w_out, ret_decay_logit_f, ret_decay_logit_b, final_gain, L=NL):
    f32 = np.float32
    x = np.asarray(x, f32); c = np.asarray(c, f32)
    norm_gain = np.asarray(norm_gain, f32); w_ada = np.asarray(w_ada, f32)[:L]; b_ada = np.asarray(b_ada, f32)
    w_in = np.asarray(w_in, f32)[:L]; w_out = np.asarray(w_out, f32)[:L]
    dlf = np.asarray(ret_decay_logit_f, f32); dlb = np.asarray(ret_decay_logit_b, f32)
    final_gain = np.asarray(final_gain, f32)
    rel = _host_consts()
    ident = np.eye(128, dtype=f32)
    j = np.arange(128)[:, None].astype(np.float64)
    i = np.arange(128)[None, :].astype(np.float64)
    Rf = np.maximum(i - j, 0); Uf = (i >= j).astype(np.float64)
    Rb = np.maximum(j - i, 0); Ub = (j >= i).astype(np.float64)
    I1 = np.broadcast_to(i + 1.0, (128, 128)); I2 = np.broadcast_to(128.0 - i, (128, 128))
    cols = np.concatenate([127.0 - j, j, np.full((128, 1), 128.0), np.zeros((128, 1))], axis=1)
    rconst = np.concatenate([Rf, Uf, Rb, Ub, I1, I2, cols], axis=1).astype(f32)
    in_maps = []
    for core in range(8):
        b, g = core // 4, core % 4
        dsl = slice(512 * g, 512 * g + 512)
        m = {}
        m["xT"] = np.ascontiguousarray(x[b].T[dsl, :])
        m["cT"] = np.ascontiguousarray(c[b].reshape(16, 128).T)
        m["wada"] = np.ascontiguousarray(np.concatenate([w_ada[:, :, dsl], w_ada[:, :, 2048:4096][:, :, dsl],
                                                         w_ada[:, :, 4096:6144][:, :, dsl]], axis=2))
        ba = np.concatenate([b_ada[:, dsl], b_ada[:, 2048:4096][:, dsl], b_ada[:, 4096:6144][:, dsl]], axis=1)
        m["bada"] = np.ascontiguousarray(ba.reshape(NL, 12, 128).transpose(2, 0, 1).reshape(128, NL * 12))
        m["gain"] = np.ascontiguousarray(norm_gain[:, dsl].reshape(NL, 4, 128).transpose(2, 0, 1).reshape(128, NL * 4))
        m["fgain"] = np.ascontiguousarray(final_gain[dsl].reshape(4, 128).T)
        cols_in = []
        for hh in (2 * g, 2 * g + 1):
            for base in (0, 1024, 2048, 3072):
                cols_in.append(np.arange(base + 128 * hh, base + 128 * hh + 128))
        cols_in.append(np.arange(4096 + 128 * g, 4096 + 128 * g + 128))
        cols_in.append(np.arange(4608 + 128 * g, 4608 + 128 * g + 128))
        cols_in.append(np.arange(5120 + 256 * g, 5120 + 256 * g + 256))
        cols_in.append(np.arange(6144 + 256 * g, 6144 + 256 * g + 256))
        cols_in = np.concatenate(cols_in)
        m["win"] = np.ascontiguousarray(w_in[:, :, cols_in])
        rows = []
        for g2 in range(4):
            rows.append(np.arange(128 * (2 * g2), 128 * (2 * g2) + 128))
            rows.append(np.arange(128 * (2 * g2 + 1), 128 * (2 * g2 + 1) + 128))
            rows.append(np.arange(1024 + 256 * g2, 1024 + 256 * g2 + 256))
        rows = np.concatenate(rows)
        m["wout"] = np.ascontiguousarray(w_out[:, :, dsl])
        dl = np.stack([dlf[:, g], dlb[:, g]], axis=1).reshape(1, NL * 2)
        m["dlog"] = np.ascontiguousarray(np.broadcast_to(dl, (128, NL * 2))).astype(f32)
        m["ident"] = ident
        am = []
        for hh in (2 * g, 2 * g + 1):
            slope = 2.0 ** (-(hh + 1.0))
            for d in PATTERNS:
                am.append(np.where(rel <= 64, np.exp(-slope * d * rel), 0.0))
        m["amask"] = np.ascontiguousarray(np.concatenate(am, axis=1)).astype(f32)
        m["rconst"] = rconst
        in_maps.append(m)
    return in_maps


def assemble(results):
    out = np.empty((2, T, D), np.float32)
    for core in range(8):
        b, g = core // 4, core % 4
        out[b, :, 512 * g:512 * g + 512] = results[core]["outT"].T
    return out


_NC_CACHE = {}


def kernel(x, c, norm_gain, w_ada, b_ada, w_in, w_out, ret_decay_logit_f, ret_decay_logit_b, final_gain):
    in_maps = prep_inputs(x, c, norm_gain, w_ada, b_ada, w_in, w_out, ret_decay_logit_f, ret_decay_logit_b, final_gain)
    nc = build()
    res = run_bass_kernel_spmd(nc, in_maps, core_ids=list(range(8)))
    return assemble(res.results)
```

```python
import numpy as np
import ml_dtypes
from contextlib import ExitStack
import concourse.bass as bass
import concourse.mybir as mybir
from concourse.bass_utils import run_bass_kernel_spmd

F32 = mybir.dt.float32
BF16 = mybir.dt.bfloat16
AF = mybir.ActivationFunctionType
ALU = mybir.AluOpType

D = 2048
T = 4096
NL = 4
TT = 512
NTT = T // TT
EPS = 1e-6
PATTERNS = (1, 4, 16)
ENGS = ("pe", "act", "dve", "pool", "sp")


def ssl(start, n, step):
    return slice(start, start + (n - 1) * step + 1, step)
NRING = 8
RECIP = "reciprocal"


class Sched:
    def __init__(self, nc, es):
        self.nc, self.es = nc, es
        self.q = {e: [] for e in ENGS}
        self.sems, self.cnt = {}, {}
        self.waited = {e: {} for e in ENGS}
        self.lastw, self.readers = {}, {}
        self.ring_idx = {"sp": 0, "pool": 0, "act": 0}
        self.pend_r, self.pend_w = set(), set()

    def sem(self, key):
        if key not in self.sems:
            self.sems[key] = self.es.enter_context(self.nc.semaphore("sem_" + str(key)))
            self.cnt[key] = 0
        return self.sems[key]

    def _deps(self, eng, reads, writes):
        evs = []
        for k in list(reads) + list(writes):
            if eng != "pe" and (k in self.pend_r or k in self.pend_w):
                raise RuntimeError(f"dependency on unsignalled PE op for key {k}")
        for k in reads:
            w = self.lastw.get(k)
            if w:
                evs.append(w)
        for k in writes:
            w = self.lastw.get(k)
            if w:
                evs.append(w)
            for sk, v in self.readers.get(k, {}).items():
                evs.append((sk, v))
        return evs

    def _waits(self, eng, evs):
        out = []
        for sk, v in evs:
            if eng == "pe" and sk == "pe":
                continue
            if self.waited[eng].get(sk, 0) >= v:
                continue
            self.waited[eng][sk] = v
            out.append((sk, v))
        return out

    def _register(self, ev, reads, writes):
        for k in reads:
            self.readers.setdefault(k, {})[ev[0]] = ev[1]
        for k in writes:
            self.lastw[k] = ev
            self.readers[k] = {}

    @staticmethod
    def _is_psum(k):
        return k == "pm" or k == "pt" or (isinstance(k, tuple) and k[0] == "pb")

    def op(self, eng, name, args=(), kw=None, reads=(), writes=(), sig=True, extra=()):
        kw = kw or {}
        self.sem(eng)
        writes = list(writes) + [k for k in reads if self._is_psum(k)]
        reads = [k for k in reads if not self._is_psum(k)]
        evs = self._deps(eng, reads, writes) + list(extra)
        waits = self._waits(eng, evs)
        if sig:
            self.cnt[eng] += 1
            ev = (eng, self.cnt[eng])
            if eng == "pe":
                reads = set(reads) | self.pend_r
                writes = set(writes) | self.pend_w
                self.pend_r, self.pend_w = set(), set()
            self._register(ev, reads, writes)
        else:
            assert eng == "pe"
            ev = None
            self.pend_r |= set(reads)
            self.pend_w |= set(writes)
        self.q[eng].append((waits, name, args, kw, ("eng", eng) if sig else None))
        return ev

    def dma(self, qeng, out, in_, reads=(), writes=(), extra=(), **kw):
        ring = self.ring_idx[qeng]
        self.ring_idx[qeng] += 1
        sk = f"d{qeng}{ring % NRING}"
        self.sem(sk)
        evs = self._deps(qeng, reads, writes) + list(extra)
        if self.cnt[sk] > 0:
            evs.append((sk, self.cnt[sk]))
        waits = self._waits(qeng, evs)
        self.cnt[sk] += 16
        ev = (sk, self.cnt[sk])
        self._register(ev, reads, writes)
        kw2 = dict(out=out, in_=in_)
        kw2.update(kw)
        self.q[qeng].append((waits, "dma_start", (), kw2, ("dma", sk)))
        return ev

    def cc(self, ins, outs, groups, reads=(), writes=()):
        self.sem("cc")
        evs = self._deps("pool", reads, writes)
        waits = self._waits("pool", evs)
        self.cnt["cc"] += 1
        ev = ("cc", self.cnt["cc"])
        self._register(ev, reads, writes)
        kw = dict(replica_groups=groups, ins=ins, outs=outs)
        self.q["pool"].append((waits, "collective_compute", ("AllGather", ALU.bypass), kw, ("cc", "cc")))
        return ev

    def barrier(self):
        evs = [(k, v) for k, v in self.cnt.items() if v > 0]
        assert not self.pend_r and not self.pend_w
        for e in ENGS:
            self.wait_only(e, [ev for ev in evs if ev[0] != e])

    def wait_only(self, eng, evs):
        waits = self._waits(eng, evs)
        self.q[eng].append((waits, None, (), {}, None))

    def emit(self):
        nc = self.nc
        block = self.es.enter_context(nc.Block())

        def mk(engname):
            def f(e):
                for waits, name, args, kw, sig in self.q[engname]:
                    for sk, v in waits:
                        e.wait_ge(self.sems[sk], v)
                    if name is None:
                        continue
                    ins = getattr(e, name)(*args, **kw)
                    if sig is None:
                        continue
                    if sig[0] == "eng":
                        ins.then_inc(self.sems[sig[1]], 1)
                    elif sig[0] == "dma":
                        ins.then_inc(self.sems[sig[1]], 16)
                    else:
                        ins.then_inc(self.sems[sig[1]])
            return f

        block.tensor(mk("pe"))
        block.scalar(mk("act"))
        block.vector(mk("dve"))
        block.gpsimd(mk("pool"))
        block.sync(mk("sp"))


def build(n_layers=NL, mix=True, dbg=False, stop=None):
    nc = bass.Bass("TRN2", target_bir_lowering=False)
    L = n_layers

    def din(name, shape, dt=F32):
        return nc.dram_tensor(name, shape, dt, kind="ExternalInput").ap()

    xT_d = din("xT", [512, T])
    cT_d = din("cT", [128, 16])
    wada_d = din("wada", [L, D, 1536])
    bada_d = din("bada", [128, NL * 12])
    gain_d = din("gain", [128, NL * 4])
    fgain_d = din("fgain", [128, 4])
    win_d = din("win", [L, D, 1792])
    wout_d = din("wout", [L, D, 512])
    dlog_d = din("dlog", [128, NL * 2])
    ident_d = din("ident", [128, 128])
    amask_d = din("amask", [128, 6 * 256])
    rconst_d = din("rconst", [128, 6 * 128 + 4])
    out_d = nc.dram_tensor("outT", [512, T], F32, kind="ExternalOutput").ap()
    dbg_d = {}
    if dbg:
        dbg_d["h"] = nc.dram_tensor("dbg_h", [D, T], BF16, kind="ExternalOutput").ap()
        dbg_d["y"] = nc.dram_tensor("dbg_y", [D, T], BF16, kind="ExternalOutput").ap()
        dbg_d["mods"] = nc.dram_tensor("dbg_mods", [128, 12], F32, kind="ExternalOutput").ap()

    ag1_in = nc.dram_tensor("ag1_in", [1, T], F32).ap()
    ag1_out = nc.dram_tensor("ag1_out", [4, T], F32).ap()
    ag2_in = nc.dram_tensor("ag2_in", [NTT, 512, TT], BF16).ap()
    ag2_out = nc.dram_tensor("ag2_out", [NTT, D, TT], BF16).ap()
    ag3a_in = nc.dram_tensor("ag3a_in", [NTT, 256, TT], BF16).ap()
    ag3a_out = nc.dram_tensor("ag3a_out", [NTT, 1024, TT], BF16).ap()
    ag3r_in = nc.dram_tensor("ag3r_in", [NTT, 256, TT], BF16).ap()
    ag3r_out = nc.dram_tensor("ag3r_out", [NTT, 1024, TT], BF16).ap()
    GROUPS = [[0, 1, 2, 3], [4, 5, 6, 7]]

    with ExitStack() as es:
        def sb(name, shape, dt):
            return es.enter_context(nc.sbuf_tensor("sb_" + name, shape, dt))

        def ps(name, shape, dt):
            return es.enter_context(nc.psum_tensor("ps_" + name, shape, dt))

        X = sb("X", [128, 4, T], F32)
        HY = sb("HY", [128, 2, 16, TT], BF16)
        WB = sb("WB", [128, 16 * 512], BF16)
        PO = sb("PO", [128, 4, T], BF16)
        MT = sb("MT", [128, 16384], BF16)
        ident = sb("ident", [128, 128], BF16)
        ones_bf = sb("ones_bf", [128, 128], BF16)
        ones_f = sb("ones_f", [128, 128], F32)
        amask = sb("amask", [128, 6, 256], F32)
        rconst = sb("rconst", [128, 6 * 128 + 4], F32)
        cT = sb("cT", [128, 16], F32)
        cA = sb("cA", [128, 16], BF16)
        bada = sb("bada", [128, NL * 12], F32)
        gain = sb("gain", [128, NL * 4], F32)
        fgain = sb("fgain", [128, 4], F32)
        dlog = sb("dlog", [128, NL * 2], F32)
        lg = sb("lg", [128, NL * 2], F32)
        modA = sb("modA", [128, NL * 4], F32)
        modB = sb("modB", [128, NL * 4], F32)
        modG = sb("modG", [128, NL * 4], F32)
        modrow = sb("modrow", [1, 1536], F32)
        rM = sb("rM", [128, 128], F32)
        rtmp = sb("rtmp", [128, 128], F32)
        rxi = sb("rxi", [128, 2, 128], F32)
        rcol = sb("rcol", [128, 4], F32)
        tA = sb("tA", [128, TT], F32)
        rstd = sb("rstd", [128, TT], F32)
        ssq_st = sb("ssq_st", [1, TT], F32)
        ssq4 = sb("ssq4", [4, TT], F32)

        PB = [ps(f"pb{i}", [128, 512], F32) for i in range(6)]
        PT = ps("pt", [128, 8, 128], BF16)
        PM = ps("pm", [128, 512], F32)

        K = Sched(nc, es)

        K.dma("pool", ident[:], ident_d, writes=["ident"])
        K.dma("sp", amask[:], amask_d.rearrange("p (a b) -> p a b", b=256), writes=["amask"])
        K.dma("sp", rconst[:], rconst_d, writes=["rconst"])
        K.dma("sp", cT[:], cT_d, writes=["cT"])
        K.dma("sp", bada[:], bada_d, writes=["bada"])
        K.dma("sp", gain[:], gain_d, writes=["gain"])
        K.dma("sp", fgain[:], fgain_d, writes=["fgain"])
        K.dma("sp", dlog[:], dlog_d, writes=["dlog"])
        for c in range(4):
            K.dma("sp", X[:, c, :], xT_d[c * 128:(c + 1) * 128, :], writes=[("x", c, tt) for tt in range(NTT)])
        K.op("dve", "memset", (ones_bf[:], 1.0), writes=["ones_bf"])
        K.op("dve", "memset", (ones_f[:], 1.0), writes=["ones_f"])
        K.op("act", "activation", (), dict(out=cA[:], in_=cT[:], func=AF.Silu), reads=["cT"], writes=["cA"])
        K.op("act", "activation", (), dict(out=lg[:], in_=dlog[:], func=AF.Exp, scale=-1.0), reads=["dlog"], writes=["lg"])
        K.op("act", "activation", (), dict(out=lg[:], in_=lg[:], func=AF.Ln, bias=1.0), reads=["lg"], writes=["lg"])
        K.op("dve", "tensor_scalar", (lg[:], lg[:], -1.0, None, ALU.mult), reads=["lg"], writes=["lg"])

        SQD = float(np.sqrt(D))

        def mods_load(l2, grp, stage):
            for q4 in range(4):
                K.dma("pool", stage[:, q4 * 4:(q4 + 1) * 4, :],
                      wada_d[l2, q4 * 512:(q4 + 1) * 512, grp * 512:(grp + 1) * 512].rearrange("(k p) n -> p k n", p=128),
                      writes=[("hy", 1)])

        def mods_mm(l2, grp, stage):
            for kc in range(16):
                K.op("pe", "matmul", (PM[0:1, :], cA[:, kc:kc + 1], stage[:, kc, :]),
                     dict(start=(kc == 0), stop=(kc == 15)), reads=[("hy", 1), "cA"], writes=["pm"], sig=(kc == 15))
            K.op("act", "activation", (), dict(out=modrow[0:1, grp * 512:(grp + 1) * 512], in_=PM[0:1, :], func=AF.Copy),
                 reads=["pm"], writes=["modrow"])

        def mods_finish(l2):
            for j in range(12):
                K.op("pe", "matmul", (PM[:, j:j + 1], modrow[0:1, j * 128:(j + 1) * 128], ones_f[0:1, 0:1]),
                     dict(start=True, stop=True), reads=["modrow", "ones_f"], writes=["pm"], sig=(j == 11))
            sl4 = slice(l2 * 4, l2 * 4 + 4)
            K.op("dve", "tensor_tensor", (modB[:, sl4], PM[:, 0:4], bada[:, l2 * 12:l2 * 12 + 4], ALU.add),
                 reads=["pm", "bada"], writes=[("modB", l2)])
            K.op("dve", "tensor_tensor", (modA[:, sl4], PM[:, 4:8], bada[:, l2 * 12 + 4:l2 * 12 + 8], ALU.add),
                 reads=["pm", "bada"], writes=[("modA", l2)])
            K.op("dve", "scalar_tensor_tensor", (modA[:, sl4], modA[:, sl4], 1.0, gain[:, sl4], ALU.add, ALU.mult),
                 reads=[("modA", l2), "gain"], writes=[("modA", l2)])
            K.op("dve", "tensor_scalar", (modA[:, sl4], modA[:, sl4], SQD, None, ALU.mult), reads=[("modA", l2)], writes=[("modA", l2)])
            K.op("dve", "tensor_tensor", (modG[:, sl4], PM[:, 8:12], bada[:, l2 * 12 + 8:l2 * 12 + 12], ALU.add),
                 reads=["pm", "bada"], writes=[("modG", l2)])

        STG = HY[:, 1, :, :]
        for grp in range(3):
            mods_load(0, grp, STG)
            mods_mm(0, grp, STG)
        mods_finish(0)
        K.op("dve", "tensor_scalar", (fgain[:], fgain[:], SQD, None, ALU.mult), reads=["fgain"], writes=["fgain"])
        if dbg:
            K.op("dve", "tensor_copy", (tA[:, 0:4], modB[:, 0:4]), reads=[("modB", 0)], writes=["tA"])
            K.op("dve", "tensor_copy", (tA[:, 4:8], modA[:, 0:4]), reads=[("modA", 0)], writes=["tA"])
            K.op("dve", "tensor_copy", (tA[:, 8:12], modG[:, 0:4]), reads=[("modG", 0)], writes=["tA"])
            K.dma("sp", dbg_d["mods"], tA[:, 0:12], reads=["tA"])

        def finish_raw():
            outs = []
            for c in range(4):
                outs.append(K.dma("sp", out_d[c * 128:(c + 1) * 128, :], X[:, c, :], reads=[("x", c, tt) for tt in range(NTT)]))
            K.wait_only("sp", outs)
            K.emit()

        if stop == "pro":
            finish_raw()
            return nc

        def norm_stats():
            for tt in range(NTT):
                tsl = slice(tt * TT, (tt + 1) * TT)
                hs = HY[:, tt % 2, 0:4, :]
                for c in range(4):
                    K.op("act", "activation", (), dict(out=hs[:, c, :], in_=X[:, c, tsl], func=AF.Square),
                         reads=[("x", c, tt)], writes=[("hy", tt % 2)])
                for c in range(4):
                    K.op("pe", "matmul", (PM[0:1, :], ones_bf[:, 0:1], hs[:, c, :]), dict(start=(c == 0), stop=(c == 3)),
                         reads=[("hy", tt % 2), "ones_bf"], writes=["pm"], sig=(c == 3))
                K.op("act", "activation", (), dict(out=ssq_st[0:1, :], in_=PM[0:1, :], func=AF.Copy),
                     reads=["pm"], writes=["ssq_st"])
                K.dma("sp", ag1_in[0:1, tsl], ssq_st[0:1, :], reads=["ssq_st"], writes=["ag1_in"])
            K.cc([ag1_in], [ag1_out], GROUPS, reads=["ag1_in"], writes=["ag1_out"])
            for tt in range(NTT):
                rstd_tile(tt)

        rstd_all = MT.ap()[:, 0:8192].bitcast(F32)

        def rstd_tile(tt):
            tsl = slice(tt * TT, (tt + 1) * TT)
            rstd = rstd_all[:, tsl]
            K.dma("sp", ssq4[0:4, :], ag1_out[0:4, tsl], reads=["ag1_out"], writes=["ssq4"])
            K.op("pe", "matmul", (PM[:, :], ones_f[0:4, :], ssq4[0:4, :]), dict(start=True, stop=True),
                 reads=["ssq4", "ones_f"], writes=["pm"])
            K.op("act", "activation", (), dict(out=rstd, in_=PM[:, :], func=AF.Sqrt, bias=epsc[:, 0:1]),
                 reads=["pm", "epsc"], writes=[("rstd", tt)])
            K.op("dve", RECIP, (rstd, rstd), reads=[("rstd", tt)], writes=[("rstd", tt)])

        epsc = sb("epsc", [128, 2], F32)
        K.op("dve", "memset", (epsc[:, 0:1], D * EPS), writes=["epsc"])
        K.op("dve", "memset", (epsc[:, 1:2], 256 * EPS), writes=["epsc"])

        def load_tile(src, tt, slot):
            if src is ag2_out:
                parts = [(ag2_out, "ag2_out", 0, 16)]
            else:
                parts = [(ag3a_out, "ag3a_out", 0, 8), (ag3r_out, "ag3r_out", 8, 8)]
            for sap, skey, k0, nk in parts:
                v = sap[tt].rearrange("(k p) t -> p k t", p=128)
                for q4 in range(nk // 4):
                    K.dma("sp", HY[:, slot, k0 + q4 * 4:k0 + (q4 + 1) * 4, :], v[:, q4 * 4:(q4 + 1) * 4, :],
                          reads=[(skey, tt)], writes=[("hy", slot)])

        def load_w(src2d, ncols, dst=None, key="W"):
            base = WB.ap() if dst is None else dst
            Wv = base[:, 0:16 * ncols].rearrange("p (k n) -> p k n", k=16)
            for q4 in range(4):
                K.dma("pool", Wv[:, q4 * 4:(q4 + 1) * 4, :],
                      src2d[q4 * 512:(q4 + 1) * 512, :].rearrange("(k p) n -> p k n", p=128), writes=[key])
            return Wv

        pbi = [0]

        def next_pb(n=2):
            i = pbi[0] % n
            pbi[0] += 1
            return i

        def proj_pass(l, col0, nch, evac, Wv=None, post_tile=None):
            if Wv is None:
                Wv = load_w(win_d[l, :, col0 * 128:(col0 + nch) * 128], nch * 128)
            for tt in range(NTT):
                slot = tt % 2
                load_tile(ag2_out, tt, slot)
                for ch in range(nch):
                    b = next_pb()
                    for kc in range(16):
                        K.op("pe", "matmul", (PB[b][:, :], Wv[:, kc, ch * 128:(ch + 1) * 128], HY[:, slot, kc, :]),
                             dict(start=(kc == 0), stop=(kc == 15)), reads=["W", ("hy", slot)], writes=[("pb", b)],
                             sig=(kc == 15))
                    evac(ch, tt, PB[b][:, :], ("pb", b))
                if post_tile is not None:
                    post_tile(tt)


        WBa = WB.ap()
        MTa = MT.ap()
        acc_o = MTa[:, 0:8192].bitcast(F32)
        acc_d = MTa[:, 8192:16384].bitcast(F32)
        VT = WBa[:, 0:4096].rearrange("p (i e) -> p i e", e=128)
        Es = [WBa[:, 4096 + 512 * i:4096 + 512 * (i + 1)].bitcast(F32) for i in range(2)]
        Ps = [WBa[:, 5120 + 256 * i:5120 + 256 * (i + 1)] for i in range(2)]
        SCALE = 128.0 ** -0.5
        PT2 = PM.ap().bitcast(BF16).rearrange("p (i e) -> p i e", e=128)
        PTS = [(PT, "pt"), (PT2, "pm")]

        def attention_head(l, hh):
            def evac(ch, tt, pap, bk):
                tsl = slice(tt * TT, (tt + 1) * TT)
                if ch == 0:
                    K.op("act", "activation", (), dict(out=PO[:, 0, tsl], in_=pap, func=AF.Copy, scale=SCALE),
                         reads=[bk], writes=[("po", 0)])
                elif ch == 3:
                    K.op("act", "activation", (), dict(out=PO[:, 3, tsl], in_=pap, func=AF.Silu),
                         reads=[bk], writes=[("po", 3)])
                else:
                    K.op("dve", "tensor_copy", (PO[:, ch, tsl], pap), reads=[bk], writes=[("po", ch)])
            proj_pass(l, hh * 4, 4, evac, Wv=(W_A0 if hh == 0 else None))
            K.barrier()
            if l + 1 < L:
                mods_load(l + 1, hh, STG)
            for pi, d in enumerate(PATTERNS):
                Wm = amask[:, hh * 3 + pi, :]
                Ls = T // d
                nkt = Ls // 128
                for idx in range(32):
                    r, m = idx // nkt, idx % nkt
                    tok0 = r + d * 128 * m
                    half = (idx // 4) % 2
                    PTb, ptk = PTS[half]
                    K.op("pe", "transpose", (PTb[:, idx % 4, :], PO[:, 2, ssl(tok0, 128, d)], ident[:]),
                         reads=[("po", 2), "ident"], writes=[ptk], sig=(idx % 4 == 3))
                    if idx % 4 == 3:
                        if half == 0:
                            K.op("act", "activation", (), dict(out=VT[:, idx - 3:idx + 1, :], in_=PTb[:, 0:4, :], func=AF.Copy),
                                 reads=[ptk], writes=[("vt", idx // 4)])
                        else:
                            K.op("dve", "tensor_copy", (VT[:, idx - 3:idx + 1, :], PTb[:, 0:4, :]),
                                 reads=[ptk], writes=[("vt", idx // 4)])
                tiles = [(r, m) for r in range(d) for m in range(nkt)]

                def geom(r, m):
                    c_lo = 64 if m == 0 else 0
                    c_hi = 192 if m == nkt - 1 else 256
                    return c_lo, c_hi

                def emit_S(i):
                    r, m = tiles[i]
                    c_lo, c_hi = geom(r, m)
                    nq = c_hi - c_lo
                    tq0 = r + d * (128 * m - 64 + c_lo)
                    kt0 = r + d * 128 * m
                    sbk = i % 2
                    K.op("pe", "matmul", (PB[sbk][:, 0:nq], PO[:, 1, ssl(kt0, 128, d)], PO[:, 0, ssl(tq0, nq, d)]),
                         dict(start=True, stop=True), reads=[("po", 0), ("po", 1)], writes=[("pb", sbk)])

                def emit_EP(i):
                    r, m = tiles[i]
                    c_lo, c_hi = geom(r, m)
                    nq = c_hi - c_lo
                    sbk = i % 2
                    K.op("act", "activation", (), dict(out=Es[sbk][:, 0:nq], in_=PB[sbk][:, 0:nq], func=AF.Exp),
                         reads=[("pb", sbk)], writes=[("E", sbk)])
                    K.op("dve", "tensor_tensor", (Ps[sbk][:, c_lo:c_hi], Es[sbk][:, 0:nq], Wm[:, c_lo:c_hi], ALU.mult),
                         reads=[("E", sbk), "amask"], writes=[("P", sbk)])

                def emit_PV(i):
                    r, m = tiles[i]
                    c_lo, c_hi = geom(r, m)
                    sbk = i % 2
                    vt = VT[:, r * nkt + m, :]
                    vtk = ("vt", (r * nkt + m) // 4)
                    bka = (m // 4) % 2
                    ca = (m % 4) * 128
                    for which, lhs, base in (("o", vt, 2), ("d", ones_bf[:, :], 4)):
                        K.op("pe", "matmul", (PB[base + bka][:, ca + c_lo:ca + 128], lhs, Ps[sbk][:, c_lo:128]),
                             dict(start=(m == 0), stop=True), reads=[("P", sbk), vtk, "ones_bf"],
                             writes=[("pb", base + bka)], sig=(which == "d"))
                    bkb = ((m + 1) // 4) % 2
                    cb = ((m + 1) % 4) * 128
                    for which, lhs, base in (("o", vt, 2), ("d", ones_bf[:, :], 4)):
                        K.op("pe", "matmul", (PB[base + bkb][:, cb:cb + c_hi - 128], lhs, Ps[sbk][:, 128:c_hi]),
                             dict(start=True, stop=(m == nkt - 1)), reads=[("P", sbk), vtk, "ones_bf"],
                             writes=[("pb", base + bkb)], sig=(which == "d"))
                    groups = []
                    if m % 4 == 3:
                        groups.append(m // 4)
                    if m == nkt - 1:
                        groups.append(nkt // 4)
                    for k in groups:
                        jmax = min(4 * k + 3, nkt)
                        lo = 64 if k == 0 else 0
                        hi = (jmax % 4) * 128 + (64 if jmax == nkt else 128)
                        sub0 = 128 * 4 * k - 64 + lo
                        n = hi - lo
                        t0 = r + d * sub0
                        bk = k % 2
                        for acc, base, key in ((acc_o, 2, "acco"), (acc_d, 4, "accd")):
                            dst = acc[:, ssl(t0, n, d)]
                            if pi == 0:
                                K.op("dve", "tensor_copy", (dst, PB[base + bk][:, lo:hi]), reads=[("pb", base + bk)], writes=[key])
                            else:
                                K.op("dve", "tensor_tensor", (dst, dst, PB[base + bk][:, lo:hi], ALU.add),
                                     reads=[("pb", base + bk), key], writes=[key])

                emit_S(0)
                for i in range(len(tiles)):
                    emit_EP(i)
                    if i + 1 < len(tiles):
                        emit_S(i + 1)
                    emit_PV(i)
            if l + 1 < L:
                mods_mm(l + 1, hh, STG)
            for tt in range(NTT):
                tsl = slice(tt * TT, (tt + 1) * TT)
                K.op("dve", "reciprocal", (acc_d[:, tsl], acc_d[:, tsl]), reads=["accd"], writes=[("accd", tt)])
                K.op("pool", "tensor_tensor", (acc_o[:, tsl], acc_o[:, tsl], acc_d[:, tsl], ALU.mult), reads=["acco", ("accd", tt)], writes=[("acco", tt)])
                K.op("pool", "tensor_tensor", (ystage[:, tsl], acc_o[:, tsl], PO[:, 3, tsl], ALU.mult),
                     reads=[("acco", tt), ("po", 3)], writes=[("hy", 0)])
            K.dma("sp", ag3a_in[:, hh * 128:(hh + 1) * 128, :].rearrange("n p t -> p n t"),
                  ystage[:, 0:T].rearrange("p (n t) -> p n t", t=TT), reads=[("hy", 0)], writes=[("ag3a_in", tt) for tt in range(NTT)])
            K.barrier()

        Sb_all = MTa[:, 0:8192].rearrange("p (n e) -> p n e", e=256)
        VTr = MTa[:, 8192:16384].rearrange("p (n e) -> p n e", e=256)
        kzs = [WBa[:, 128 * i:128 * (i + 1)] for i in range(2)]
        Prs = [WBa[:, 256 + 128 * i:256 + 128 * (i + 1)] for i in range(2)]
        qxi = [WBa[:, 512 + 512 * i:512 + 512 * (i + 1)] for i in range(2)]
        Sst = [WBa[:, 1536 + 512 * i:1536 + 512 * (i + 1)].bitcast(F32) for i in range(2)]
        Sfb = [WBa[:, 2560 + 256 * i:2560 + 256 * (i + 1)] for i in range(2)]
        sqs = WBa[:, 3072:4096].rearrange("p (c t) -> p c t", c=2)
        rs = WBa[:, 4096:5120].bitcast(F32)
        RC = 768
        ysr = [MTa[:, 8192 + 1024 * i:8192 + 1024 * (i + 1)].rearrange("p (c t) -> p c t", c=2) for i in range(2)]
        Wo = [None]

        def retention_head(l):
            lgf = lg[:, 2 * l:2 * l + 1]
            lgb = lg[:, 2 * l + 1:2 * l + 2]
            K.op("act", "activation", (), dict(out=rM[:], in_=rconst[:, 0:128], func=AF.Exp, scale=lgf), reads=["rconst", "lg"], writes=["rM"])
            K.op("dve", "tensor_tensor", (rM[:], rM[:], rconst[:, 128:256], ALU.mult), reads=["rM", "rconst"], writes=["rM"])
            K.op("act", "activation", (), dict(out=rtmp[:], in_=rconst[:, 256:384], func=AF.Exp, scale=lgb), reads=["rconst", "lg"], writes=["rtmp"])
            K.op("dve", "tensor_tensor", (rtmp[:], rtmp[:], rconst[:, 384:512], ALU.mult), reads=["rtmp", "rconst"], writes=["rtmp"])
            K.op("dve", "tensor_tensor", (rM[:], rM[:], rtmp[:], ALU.add), reads=["rM", "rtmp"], writes=["rM"])
            K.op("act", "activation", (), dict(out=rxi[:, 0, :], in_=rconst[:, 512:640], func=AF.Exp, scale=lgf), reads=["rconst", "lg"], writes=["rxi"])
            K.op("act", "activation", (), dict(out=rxi[:, 1, :], in_=rconst[:, 640:768], func=AF.Exp, scale=lgb), reads=["rconst", "lg"], writes=["rxi"])
            K.op("act", "activation", (), dict(out=rcol[:, 0:1], in_=rconst[:, RC:RC + 1], func=AF.Exp, scale=lgf), reads=["rconst", "lg"], writes=["rcol"])
            K.op("act", "activation", (), dict(out=rcol[:, 1:2], in_=rconst[:, RC + 1:RC + 2], func=AF.Exp, scale=lgb), reads=["rconst", "lg"], writes=["rcol"])
            K.op("act", "activation", (), dict(out=rcol[:, 2:3], in_=rconst[:, RC + 2:RC + 3], func=AF.Exp, scale=lgf), reads=["rconst", "lg"], writes=["rcol"])
            K.op("act", "activation", (), dict(out=rcol[:, 3:4], in_=rconst[:, RC + 2:RC + 3], func=AF.Exp, scale=lgb), reads=["rconst", "lg"], writes=["rcol"])

            def evac(ch, tt, pap, bk):
                tsl = slice(tt * TT, (tt + 1) * TT)
                if ch == 1:
                    K.op("act", "activation", (), dict(out=PO[:, 1, tsl], in_=pap, func=AF.Copy, scale=SCALE),
                         reads=[bk], writes=[("po", 1)])
                elif ch == 0:
                    K.op("act", "activation", (), dict(out=PO[:, 0, tsl], in_=pap, func=AF.Copy), reads=[bk], writes=[("po", 0)])
                else:
                    K.op("dve", "tensor_copy", (PO[:, ch, tsl], pap), reads=[bk], writes=[("po", ch)])
            Wv_b1 = load_w(win_d[l, :, 8 * 128:12 * 128], 512)
            for tt in range(NTT):
                K.cc([ag3a_in[tt]], [ag3a_out[tt]], GROUPS, reads=[("ag3a_in", tt)], writes=[("ag3a_out", tt)])
            proj_pass(l, 8, 4, evac, Wv=Wv_b1)
            K.barrier()
            if l + 1 < L:
                mods_load(l + 1, 2, STG)
            K.op("dve", "memset", (Sst[0], 0.0), writes=["Sf"])
            K.op("dve", "memset", (Sst[1], 0.0), writes=["Sb"])
            for n in range(31, -1, -1):
                csl = slice(n * 128, (n + 1) * 128)
                half = n % 2
                PTb, ptk = PTS[half]
                for c2 in range(2):
                    K.op("pe", "transpose", (PTb[:, c2, :], PO[:, 2 + c2, csl], ident[:]),
                         reads=[("po", 2 + c2), "ident"], writes=[ptk], sig=False)
                K.op("pe", "transpose", (PTb[:, 2, :], PO[:, 1, csl], ident[:]),
                     reads=[("po", 1), "ident"], writes=[ptk], sig=True)
                K.op("act", "activation", (), dict(out=VTr[:, n, :].rearrange("p (c e) -> p c e", c=2), in_=PTb[:, 0:2, :], func=AF.Copy),
                     reads=[ptk], writes=[("vtr", n)])
                K.op("dve", "tensor_scalar", (kzs[half], PTb[:, 2, :], rcol[:, 1:2], None, ALU.mult),
                     reads=[ptk, "rcol"], writes=[("kz", half)])
                K.op("act", "activation", (), dict(out=Sb_all[:, n, :], in_=Sst[1], func=AF.Copy), reads=["Sb"], writes=[("sball", n)])
                b = next_pb()
                K.op("pe", "matmul", (PB[b][:, 0:256], kzs[half], VTr[:, n, :]), dict(start=True, stop=True),
                     reads=[("kz", half), ("vtr", n)], writes=[("pb", b)])
                K.op("dve", "scalar_tensor_tensor", (Sst[1], Sst[1], rcol[:, 3:4], PB[b][:, 0:256], ALU.mult, ALU.add),
                     reads=[("pb", b), "Sb", "rcol"], writes=["Sb"])
            for gq in range(NTT):
                tsl = slice(gq * TT, (gq + 1) * TT)
                for ci in range(4):
                    csl = slice(gq * TT + ci * 128, gq * TT + (ci + 1) * 128)
                    K.op("pool", "tensor_tensor", (qxi[0][:, ci * 128:(ci + 1) * 128], PO[:, 0, csl], rxi[:, 0, :], ALU.mult),
                         reads=[("po", 0), "rxi"], writes=["qxi0"])
                    K.op("pool", "tensor_tensor", (qxi[1][:, ci * 128:(ci + 1) * 128], PO[:, 0, csl], rxi[:, 1, :], ALU.mult),
                         reads=[("po", 0), "rxi"], writes=["qxi1"])
                ob = 2 + 2 * (gq % 2)
                for ci in range(4):
                    n = gq * 4 + ci
                    csl = slice(n * 128, (n + 1) * 128)
                    half = n % 2
                    K.op("pe", "transpose", (PT[:, 0, :], PO[:, 1, csl], ident[:]),
                         reads=[("po", 1), "ident"], writes=["pt"], sig=True)
                    K.op("dve", "tensor_scalar", (kzs[half], PT[:, 0, :], rcol[:, 0:1], None, ALU.mult),
                         reads=["pt", "rcol"], writes=[("kz", half)])
                    b = next_pb()
                    K.op("pe", "matmul", (PB[b][:, 0:128], PO[:, 1, csl], PO[:, 0, csl]), dict(start=True, stop=True),
                         reads=[("po", 0), ("po", 1)], writes=[("pb", b)])
                    K.op("dve", "tensor_tensor", (Prs[half], PB[b][:, 0:128], rM[:], ALU.mult), reads=[("pb", b), "rM"], writes=[("Pr", half)])
                    for c2 in range(2):
                        dst = PB[ob + c2][:, ci * 128:(ci + 1) * 128]
                        terms = [(VTr[:, n, c2 * 128:(c2 + 1) * 128], Prs[half], [("vtr", n), ("Pr", half)])]
                        if n > 0:
                            terms.append((Sfb[n % 2][:, c2 * 128:(c2 + 1) * 128], qxi[0][:, ci * 128:(ci + 1) * 128], [("Sfb", n % 2), "qxi0"]))
                        if n < 31:
                            terms.append((Sb_all[:, n, c2 * 128:(c2 + 1) * 128], qxi[1][:, ci * 128:(ci + 1) * 128], [("sball", n), "qxi1"]))
                        for ti, (lhs, rhs, rk) in enumerate(terms):
                            K.op("pe", "matmul", (dst, lhs, rhs), dict(start=(ti == 0), stop=(ti == len(terms) - 1)),
                                 reads=rk, writes=[("pb", ob + c2)], sig=(ti == len(terms) - 1))
                    b = next_pb()
                    K.op("pe", "matmul", (PB[b][:, 0:256], kzs[half], VTr[:, n, :]), dict(start=True, stop=True),
                         reads=[("kz", half), ("vtr", n)], writes=[("pb", b)])
                    K.op("dve", "scalar_tensor_tensor", (Sst[0], Sst[0], rcol[:, 2:3], PB[b][:, 0:256], ALU.mult, ALU.add),
                         reads=[("pb", b), "Sf", "rcol"], writes=["Sf"])
                    K.op("act", "activation", (), dict(out=Sfb[(n + 1) % 2], in_=Sst[0], func=AF.Copy), reads=["Sf"], writes=[("Sfb", (n + 1) % 2)])
                for c2 in range(2):
                    K.op("act", "activation", (), dict(out=sqs[:, c2, :], in_=PB[ob + c2][:, :], func=AF.Square),
                         reads=[("pb", ob + c2)], writes=[("sq", c2)])
                for c2 in range(2):
                    K.op("pe", "matmul", (PM[:, :], ones_bf[:, :], sqs[:, c2, :]), dict(start=(c2 == 0), stop=(c2 == 1)),
                         reads=[("sq", c2), "ones_bf"], writes=["pm"], sig=(c2 == 1))
                K.op("act", "activation", (), dict(out=rs, in_=PM[:, :], func=AF.Sqrt, bias=epsc[:, 1:2]),
                     reads=["pm", "epsc"], writes=["rs"])
                K.op("dve", RECIP, (rs, rs), reads=["rs"], writes=["rs"])
                for c2 in range(2):
                    K.op("dve", "scalar_tensor_tensor", (PO[:, 2 + c2, tsl], PB[ob + c2][:, :], 16.0, rs, ALU.mult, ALU.mult),
                         reads=[("pb", ob + c2), "rs"], writes=[("po", 2 + c2)])
            if l + 1 < L:
                mods_mm(l + 1, 2, STG)
                mods_finish(l + 1)
            K.barrier()
            Wo[0] = load_w(wout_d[l, :, :], 512, dst=MTa, key="Wo")

            def evac2(ch, tt, pap, bk):
                tsl = slice(tt * TT, (tt + 1) * TT)
                K.op("act", "activation", (), dict(out=PO[:, ch, tsl], in_=pap, func=AF.Silu), reads=[bk], writes=[("po", ch)])

            def post(tt):
                tsl = slice(tt * TT, (tt + 1) * TT)
                ys = ysr[tt % 2]
                for c2 in range(2):
                    K.op("dve", "tensor_tensor", (ys[:, c2, :], PO[:, 2 + c2, tsl], PO[:, c2, tsl], ALU.mult),
                         reads=[("po", 2 + c2), ("po", c2)], writes=[("ysr", tt % 2)])
                K.dma("sp", ag3r_in[tt].rearrange("(c p) t -> p c t", p=128), ys, reads=[("ysr", tt % 2)], writes=[("ag3r_in", tt)])
                K.cc([ag3r_in[tt]], [ag3r_out[tt]], GROUPS, reads=[("ag3r_in", tt)], writes=[("ag3r_out", tt)])
            proj_pass(l, 12, 2, evac2, post_tile=post)
            K.barrier()

        ystage = HY[:, 0, :, :].rearrange("p k t -> p (k t)")
        for l in range(L):
            A_ = modA[:, l * 4:l * 4 + 4]
            B_ = modB[:, l * 4:l * 4 + 4]
            G_ = modG[:, l * 4:l * 4 + 4]
            W_A0 = load_w(win_d[l, :, 0:512], 512)
            norm_stats()
            if stop == "n1":
                finish_raw()
                return nc
            for tt in range(NTT):
                tsl = slice(tt * TT, (tt + 1) * TT)
                slot = tt % 2
                for c in range(4):
                    K.op("dve", "scalar_tensor_tensor", (tA[:], X[:, c, tsl], A_[:, c:c + 1], rstd_all[:, tsl], ALU.mult, ALU.mult),
                         reads=[("x", c, tt), ("rstd", tt), ("modA", l)], writes=["tA"])
                    K.op("act", "activation", (), dict(out=HY[:, slot, c, :], in_=tA[:], func=AF.Identity, bias=B_[:, c:c + 1]),
                         reads=["tA", ("modB", l)], writes=[("hy", slot)])
                K.dma("sp", ag2_in[tt].rearrange("(c p) t -> p c t", p=128), HY[:, slot, 0:4, :],
                      reads=[("hy", slot)], writes=[("ag2_in", tt)])
                K.cc([ag2_in[tt]], [ag2_out[tt]], GROUPS, reads=[("ag2_in", tt)], writes=[("ag2_out", tt)])
            if dbg and l == 0:
                for tt in range(NTT):
                    load_tile(ag2_out, tt, tt % 2)
                    K.dma("sp", dbg_d["h"].rearrange("(k p) t -> p k t", p=128)[:, :, tt * TT:(tt + 1) * TT], HY[:, tt % 2, :, :],
                          reads=[("hy", tt % 2)])

            if stop == "n2":
                finish_raw()
                return nc
            if mix:
                attention_head(l, 0)
                attention_head(l, 1)
                retention_head(l)
            if dbg and l == 0:
                for tt in range(NTT):
                    load_tile(None, tt, tt % 2)
                    K.dma("sp", dbg_d["y"].rearrange("(k p) t -> p k t", p=128)[:, :, tt * TT:(tt + 1) * TT], HY[:, tt % 2, :, :],
                          reads=[("hy", tt % 2)])

            if stop == "m":
                finish_raw()
                return nc
            Wv = Wo[0]
            for tt in range(NTT):
                tsl = slice(tt * TT, (tt + 1) * TT)
                slot = tt % 2
                load_tile(None, tt, slot)
                for c in range(4):
                    b = next_pb()
                    for kc in range(16):
                        K.op("pe", "matmul", (PB[b][:, :], Wv[:, kc, c * 128:(c + 1) * 128], HY[:, slot, kc, :]),
                             dict(start=(kc == 0), stop=(kc == 15)), reads=["Wo", ("hy", slot)], writes=[("pb", b)],
                             sig=(kc == 15))
                    K.op("dve", "scalar_tensor_tensor", (X[:, c, tsl], PB[b][:, :], G_[:, c:c + 1], X[:, c, tsl], ALU.mult, ALU.add),
                         reads=[("pb", b), ("x", c, tt), ("modG", l)], writes=[("x", c, tt)])

        if stop == "o":
            finish_raw()
            return nc
        norm_stats()
        outs = []
        for tt in range(NTT):
            tsl = slice(tt * TT, (tt + 1) * TT)
            for c in range(4):
                K.op("dve", "scalar_tensor_tensor", (tA[:], X[:, c, tsl], fgain[:, c:c + 1], rstd_all[:, tsl], ALU.mult, ALU.mult),
                     reads=[("x", c, tt), ("rstd", tt), "fgain"], writes=["tA"])
                outs.append(K.dma("sp", out_d[c * 128:(c + 1) * 128, tsl], tA[:], reads=["tA"]))
        K.wait_only("sp", outs)
        K.emit()
    return nc


def _host_consts():
    p = np.arange(128)[:, None]
    c = np.arange(256)[None, :]
    rel = np.abs(c - 64 - p).astype(np.float64)
    return rel


def prep_inputs(x, c, norm_gain, w_ada, b_ada, w_in, w_out, ret_decay_logit_f, ret_decay_logit_b, final_gain, L=NL):
    f32 = np.float32
    x = np.asarray(x, f32); c = np.asarray(c, f32)
    norm_gain = np.asarray(norm_gain, f32); w_ada = np.asarray(w_ada, f32)[:L]; b_ada = np.asarray(b_ada, f32)
    w_in = np.asarray(w_in, f32)[:L]; w_out = np.asarray(w_out, f32)[:L]
    dlf = np.asarray(ret_decay_logit_f, f32); dlb = np.asarray(ret_decay_logit_b, f32)
    final_gain = np.asarray(final_gain, f32)
    rel = _host_consts()
    ident = np.eye(128, dtype=f32)
    j = np.arange(128)[:, None].astype(np.float64)
    i = np.arange(128)[None, :].astype(np.float64)
    Rf = np.maximum(i - j, 0); Uf = (i >= j).astype(np.float64)
    Rb = np.maximum(j - i, 0); Ub = (j >= i).astype(np.float64)
    I1 = np.broadcast_to(i + 1.0, (128, 128)); I2 = np.broadcast_to(128.0 - i, (128, 128))
    cols = np.concatenate([127.0 - j, j, np.full((128, 1), 128.0), np.zeros((128, 1))], axis=1)
    rconst = np.concatenate([Rf, Uf, Rb, Ub, I1, I2, cols], axis=1).astype(f32)
    in_maps = []
    for core in range(8):
        b, g = core // 4, core % 4
        dsl = slice(512 * g, 512 * g + 512)
        m = {}
        m["xT"] = np.ascontiguousarray(x[b].T[dsl, :])
        m["cT"] = np.ascontiguousarray(c[b].reshape(16, 128).T)
        m["wada"] = np.ascontiguousarray(np.concatenate([w_ada[:, :, dsl], w_ada[:, :, 2048:4096][:, :, dsl],
                                                         w_ada[:, :, 4096:6144][:, :, dsl]], axis=2))
        ba = np.concatenate([b_ada[:, dsl], b_ada[:, 2048:4096][:, dsl], b_ada[:, 4096:6144][:, dsl]], axis=1)
        m["bada"] = np.ascontiguousarray(ba.reshape(NL, 12, 128).transpose(2, 0, 1).reshape(128, NL * 12))
        m["gain"] = np.ascontiguousarray(norm_gain[:, dsl].reshape(NL, 4, 128).transpose(2, 0, 1).reshape(128, NL * 4))
        m["fgain"] = np.ascontiguousarray(final_gain[dsl].reshape(4, 128).T)
        cols_in = []
        for hh in (2 * g, 2 * g + 1):
            for base in (0, 1024, 2048, 3072):
                cols_in.append(np.arange(base + 128 * hh, base + 128 * hh + 128))
        cols_in.append(np.arange(4096 + 128 * g, 4096 + 128 * g + 128))
        cols_in.append(np.arange(4608 + 128 * g, 4608 + 128 * g + 128))
        cols_in.append(np.arange(5120 + 256 * g, 5120 + 256 * g + 256))
        cols_in.append(np.arange(6144 + 256 * g, 6144 + 256 * g + 256))
        cols_in = np.concatenate(cols_in)
        m["win"] = np.ascontiguousarray(w_in[:, :, cols_in])
        rows = []
        for g2 in range(4):
            rows.append(np.arange(128 * (2 * g2), 128 * (2 * g2) + 128))
            rows.append(np.arange(128 * (2 * g2 + 1), 128 * (2 * g2 + 1) + 128))
            rows.append(np.arange(1024 + 256 * g2, 1024 + 256 * g2 + 256))
        rows = np.concatenate(rows)
        m["wout"] = np.ascontiguousarray(w_out[:, :, dsl])
        dl = np.stack([dlf[:, g], dlb[:, g]], axis=1).reshape(1, NL * 2)
        m["dlog"] = np.ascontiguousarray(np.broadcast_to(dl, (128, NL * 2))).astype(f32)
        m["ident"] = ident
        am = []
        for hh in (2 * g, 2 * g + 1):
            slope = 2.0 ** (-(hh + 1.0))
            for d in PATTERNS:
                am.append(np.where(rel <= 64, np.exp(-slope * d * rel), 0.0))
        m["amask"] = np.ascontiguousarray(np.concatenate(am, axis=1)).astype(f32)
        m["rconst"] = rconst
        in_maps.append(m)
    return in_maps


def assemble(results):
    out = np.empty((2, T, D), np.float32)
    for core in range(8):
        b, g = core // 4, core % 4
        out[b, :, 512 * g:512 * g + 512] = results[core]["outT"].T
    return out


_NC_CACHE = {}


def kernel(x, c, norm_gain, w_ada, b_ada, w_in, w_out, ret_decay_logit_f, ret_decay_logit_b, final_gain):
    in_maps = prep_inputs(x, c, norm_gain, w_ada, b_ada, w_in, w_out, ret_decay_logit_f, ret_decay_logit_b, final_gain)
    nc = build()
    res = run_bass_kernel_spmd(nc, in_maps, core_ids=list(range(8)))
    return assemble(res.results)
```

```python
import numpy as np
import ml_dtypes
from contextlib import ExitStack
import concourse.bass as bass
import concourse.mybir as mybir
from concourse.bass_utils import run_bass_kernel_spmd

F32 = mybir.dt.float32
BF16 = mybir.dt.bfloat16
AF = mybir.ActivationFunctionType
ALU = mybir.AluOpType

D = 2048
T = 4096
NL = 4
TT = 512
NTT = T // TT
EPS = 1e-6
PATTERNS = (1, 4, 16)
ENGS = ("pe", "act", "dve", "pool", "sp")


def ssl(start, n, step):
    return slice(start, start + (n - 1) * step + 1, step)
NRING = 8
RECIP = "reciprocal"


class Sched:
    def __init__(self, nc, es):
        self.nc, self.es = nc, es
        self.q = {e: [] for e in ENGS}
        self.sems, self.cnt = {}, {}
        self.waited = {e: {} for e in ENGS}
        self.lastw, self.readers = {}, {}
        self.ring_idx = {"sp": 0, "pool": 0, "act": 0}
        self.pend_r, self.pend_w = set(), set()

    def sem(self, key):
        if key not in self.sems:
            self.sems[key] = self.es.enter_context(self.nc.semaphore("sem_" + str(key)))
            self.cnt[key] = 0
        return self.sems[key]

    def _deps(self, eng, reads, writes):
        evs = []
        for k in list(reads) + list(writes):
            if eng != "pe" and (k in self.pend_r or k in self.pend_w):
                raise RuntimeError(f"dependency on unsignalled PE op for key {k}")
        for k in reads:
            w = self.lastw.get(k)
            if w:
                evs.append(w)
        for k in writes:
            w = self.lastw.get(k)
            if w:
                evs.append(w)
            for sk, v in self.readers.get(k, {}).items():
                evs.append((sk, v))
        return evs

    def _waits(self, eng, evs):
        out = []
        for sk, v in evs:
            if eng == "pe" and sk == "pe":
                continue
            if self.waited[eng].get(sk, 0) >= v:
                continue
            self.waited[eng][sk] = v
            out.append((sk, v))
        return out

    def _register(self, ev, reads, writes):
        for k in reads:
            self.readers.setdefault(k, {})[ev[0]] = ev[1]
        for k in writes:
            self.lastw[k] = ev
            self.readers[k] = {}

    @staticmethod
    def _is_psum(k):
        return k == "pm" or k == "pt" or (isinstance(k, tuple) and k[0] == "pb")

    def op(self, eng, name, args=(), kw=None, reads=(), writes=(), sig=True, extra=()):
        kw = kw or {}
        self.sem(eng)
        writes = list(writes) + [k for k in reads if self._is_psum(k)]
        reads = [k for k in reads if not self._is_psum(k)]
        evs = self._deps(eng, reads, writes) + list(extra)
        waits = self._waits(eng, evs)
        if sig:
            self.cnt[eng] += 1
            ev = (eng, self.cnt[eng])
            if eng == "pe":
                reads = set(reads) | self.pend_r
                writes = set(writes) | self.pend_w
                self.pend_r, self.pend_w = set(), set()
            self._register(ev, reads, writes)
        else:
            assert eng == "pe"
            ev = None
            self.pend_r |= set(reads)
            self.pend_w |= set(writes)
        self.q[eng].append((waits, name, args, kw, ("eng", eng) if sig else None))
        return ev

    def dma(self, qeng, out, in_, reads=(), writes=(), extra=(), **kw):
        ring = self.ring_idx[qeng]
        self.ring_idx[qeng] += 1
        sk = f"d{qeng}{ring % NRING}"
        self.sem(sk)
        evs = self._deps(qeng, reads, writes) + list(extra)
        if self.cnt[sk] > 0:
            evs.append((sk, self.cnt[sk]))
        waits = self._waits(qeng, evs)
        self.cnt[sk] += 16
        ev = (sk, self.cnt[sk])
        self._register(ev, reads, writes)
        kw2 = dict(out=out, in_=in_)
        kw2.update(kw)
        self.q[qeng].append((waits, "dma_start", (), kw2, ("dma", sk)))
        return ev

    def cc(self, ins, outs, groups, reads=(), writes=()):
        self.sem("cc")
        evs = self._deps("pool", reads, writes)
        waits = self._waits("pool", evs)
        self.cnt["cc"] += 1
        ev = ("cc", self.cnt["cc"])
        self._register(ev, reads, writes)
        kw = dict(replica_groups=groups, ins=ins, outs=outs)
        self.q["pool"].append((waits, "collective_compute", ("AllGather", ALU.bypass), kw, ("cc", "cc")))
        return ev

    def barrier(self):
        evs = [(k, v) for k, v in self.cnt.items() if v > 0]
        assert not self.pend_r and not self.pend_w
        for e in ENGS:
            self.wait_only(e, [ev for ev in evs if ev[0] != e])

    def wait_only(self, eng, evs):
        waits = self._waits(eng, evs)
        self.q[eng].append((waits, None, (), {}, None))

    def emit(self):
        nc = self.nc
        block = self.es.enter_context(nc.Block())

        def mk(engname):
            def f(e):
                for waits, name, args, kw, sig in self.q[engname]:
                    for sk, v in waits:
                        e.wait_ge(self.sems[sk], v)
                    if name is None:
                        continue
                    ins = getattr(e, name)(*args, **kw)
                    if sig is None:
                        continue
                    if sig[0] == "eng":
                        ins.then_inc(self.sems[sig[1]], 1)
                    elif sig[0] == "dma":
                        ins.then_inc(self.sems[sig[1]], 16)
                    else:
                        ins.then_inc(self.sems[sig[1]])
            return f

        block.tensor(mk("pe"))
        block.scalar(mk("act"))
        block.vector(mk("dve"))
        block.gpsimd(mk("pool"))
        block.sync(mk("sp"))


def build(n_layers=NL, mix=True, dbg=False, stop=None):
    nc = bass.Bass("TRN2", target_bir_lowering=False)
    L = n_layers

    def din(name, shape, dt=F32):
        return nc.dram_tensor(name, shape, dt, kind="ExternalInput").ap()

    xT_d = din("xT", [512, T])
    cT_d = din("cT", [128, 16])
    wada_d = din("wada", [L, D, 1536])
    bada_d = din("bada", [128, NL * 12])
    gain_d = din("gain", [128, NL * 4])
    fgain_d = din("fgain", [128, 4])
    win_d = din("win", [L, D, 1792])
    wout_d = din("wout", [L, D, 512])
    dlog_d = din("dlog", [128, NL * 2])
    ident_d = din("ident", [128, 128])
    amask_d = din("amask", [128, 6 * 256])
    rconst_d = din("rconst", [128, 6 * 128 + 4])
    out_d = nc.dram_tensor("outT", [512, T], F32, kind="ExternalOutput").ap()
    dbg_d = {}
    if dbg:
        dbg_d["h"] = nc.dram_tensor("dbg_h", [D, T], BF16, kind="ExternalOutput").ap()
        dbg_d["y"] = nc.dram_tensor("dbg_y", [D, T], BF16, kind="ExternalOutput").ap()
        dbg_d["mods"] = nc.dram_tensor("dbg_mods", [128, 12], F32, kind="ExternalOutput").ap()

    ag1_in = nc.dram_tensor("ag1_in", [1, T], F32).ap()
    ag1_out = nc.dram_tensor("ag1_out", [4, T], F32).ap()
    ag2_in = nc.dram_tensor("ag2_in", [NTT, 512, TT], BF16).ap()
    ag2_out = nc.dram_tensor("ag2_out", [NTT, D, TT], BF16).ap()
    ag3a_in = nc.dram_tensor("ag3a_in", [NTT, 256, TT], BF16).ap()
    ag3a_out = nc.dram_tensor("ag3a_out", [NTT, 1024, TT], BF16).ap()
    ag3r_in = nc.dram_tensor("ag3r_in", [NTT, 256, TT], BF16).ap()
    ag3r_out = nc.dram_tensor("ag3r_out", [NTT, 1024, TT], BF16).ap()
    GROUPS = [[0, 1, 2, 3], [4, 5, 6, 7]]

    with ExitStack() as es:
        def sb(name, shape, dt):
            return es.enter_context(nc.sbuf_tensor("sb_" + name, shape, dt))

        def ps(name, shape, dt):
            return es.enter_context(nc.psum_tensor("ps_" + name, shape, dt))

        X = sb("X", [128, 4, T], F32)
        HY = sb("HY", [128, 2, 16, TT], BF16)
        WB = sb("WB", [128, 16 * 512], BF16)
        PO = sb("PO", [128, 4, T], BF16)
        MT = sb("MT", [128, 16384], BF16)
        ident = sb("ident", [128, 128], BF16)
        ones_bf = sb("ones_bf", [128, 128], BF16)
        ones_f = sb("ones_f", [128, 128], F32)
        amask = sb("amask", [128, 6, 256], F32)
        rconst = sb("rconst", [128, 6 * 128 + 4], F32)
        cT = sb("cT", [128, 16], F32)
        cA = sb("cA", [128, 16], BF16)
        bada = sb("bada", [128, NL * 12], F32)
        gain = sb("gain", [128, NL * 4], F32)
        fgain = sb("fgain", [128, 4], F32)
        dlog = sb("dlog", [128, NL * 2], F32)
        lg = sb("lg", [128, NL * 2], F32)
        modA = sb("modA", [128, NL * 4], F32)
        modB = sb("modB", [128, NL * 4], F32)
        modG = sb("modG", [128, NL * 4], F32)
        modrow = sb("modrow", [1, 1536], F32)
        rM = sb("rM", [128, 128], F32)
        rtmp = sb("rtmp", [128, 128], F32)
        rxi = sb("rxi", [128, 2, 128], F32)
        rcol = sb("rcol", [128, 4], F32)
        tA = sb("tA", [128, TT], F32)
        rstd = sb("rstd", [128, TT], F32)
        ssq_st = sb("ssq_st", [1, TT], F32)
        ssq4 = sb("ssq4", [4, TT], F32)

        PB = [ps(f"pb{i}", [128, 512], F32) for i in range(6)]
        PT = ps("pt", [128, 8, 128], BF16)
        PM = ps("pm", [128, 512], F32)

        K = Sched(nc, es)

        K.dma("pool", ident[:], ident_d, writes=["ident"])
        K.dma("sp", amask[:], amask_d.rearrange("p (a b) -> p a b", b=256), writes=["amask"])
        K.dma("sp", rconst[:], rconst_d, writes=["rconst"])
        K.dma("sp", cT[:], cT_d, writes=["cT"])
        K.dma("sp", bada[:], bada_d, writes=["bada"])
        K.dma("sp", gain[:], gain_d, writes=["gain"])
        K.dma("sp", fgain[:], fgain_d, writes=["fgain"])
        K.dma("sp", dlog[:], dlog_d, writes=["dlog"])
        for c in range(4):
            K.dma("sp", X[:, c, :], xT_d[c * 128:(c + 1) * 128, :], writes=[("x", c, tt) for tt in range(NTT)])
        K.op("dve", "memset", (ones_bf[:], 1.0), writes=["ones_bf"])
        K.op("dve", "memset", (ones_f[:], 1.0), writes=["ones_f"])
        K.op("act", "activation", (), dict(out=cA[:], in_=cT[:], func=AF.Silu), reads=["cT"], writes=["cA"])
        K.op("act", "activation", (), dict(out=lg[:], in_=dlog[:], func=AF.Exp, scale=-1.0), reads=["dlog"], writes=["lg"])
        K.op("act", "activation", (), dict(out=lg[:], in_=lg[:], func=AF.Ln, bias=1.0), reads=["lg"], writes=["lg"])
        K.op("dve", "tensor_scalar", (lg[:], lg[:], -1.0, None, ALU.mult), reads=["lg"], writes=["lg"])

        SQD = float(np.sqrt(D))

        def mods_load(l2, grp, stage):
            for q4 in range(4):
                K.dma("pool", stage[:, q4 * 4:(q4 + 1) * 4, :],
                      wada_d[l2, q4 * 512:(q4 + 1) * 512, grp * 512:(grp + 1) * 512].rearrange("(k p) n -> p k n", p=128),
                      writes=[("hy", 1)])

        def mods_mm(l2, grp, stage):
            for kc in range(16):
                K.op("pe", "matmul", (PM[0:1, :], cA[:, kc:kc + 1], stage[:, kc, :]),
                     dict(start=(kc == 0), stop=(kc == 15)), reads=[("hy", 1), "cA"], writes=["pm"], sig=(kc == 15))
            K.op("act", "activation", (), dict(out=modrow[0:1, grp * 512:(grp + 1) * 512], in_=PM[0:1, :], func=AF.Copy),
                 reads=["pm"], writes=["modrow"])

        def mods_finish(l2):
            for j in range(12):
                K.op("pe", "matmul", (PM[:, j:j + 1], modrow[0:1, j * 128:(j + 1) * 128], ones_f[0:1, 0:1]),
                     dict(start=True, stop=True), reads=["modrow", "ones_f"], writes=["pm"], sig=(j == 11))
            sl4 = slice(l2 * 4, l2 * 4 + 4)
            K.op("dve", "tensor_tensor", (modB[:, sl4], PM[:, 0:4], bada[:, l2 * 12:l2 * 12 + 4], ALU.add),
                 reads=["pm", "bada"], writes=[("modB", l2)])
            K.op("dve", "tensor_tensor", (modA[:, sl4], PM[:, 4:8], bada[:, l2 * 12 + 4:l2 * 12 + 8], ALU.add),
                 reads=["pm", "bada"], writes=[("modA", l2)])
            K.op("dve", "scalar_tensor_tensor", (modA[:, sl4], modA[:, sl4], 1.0, gain[:, sl4], ALU.add, ALU.mult),
                 reads=[("modA", l2), "gain"], writes=[("modA", l2)])
            K.op("dve", "tensor_scalar", (modA[:, sl4], modA[:, sl4], SQD, None, ALU.mult), reads=[("modA", l2)], writes=[("modA", l2)])
            K.op("dve", "tensor_tensor", (modG[:, sl4], PM[:, 8:12], bada[:, l2 * 12 + 8:l2 * 12 + 12], ALU.add),
                 reads=["pm", "bada"], writes=[("modG", l2)])

        STG = HY[:, 1, :, :]
        for grp in range(3):
            mods_load(0, grp, STG)
            mods_mm(0, grp, STG)
        mods_finish(0)
        K.op("dve", "tensor_scalar", (fgain[:], fgain[:], SQD, None, ALU.mult), reads=["fgain"], writes=["fgain"])
        if dbg:
            K.op("dve", "tensor_copy", (tA[:, 0:4], modB[:, 0:4]), reads=[("modB", 0)], writes=["tA"])
            K.op("dve", "tensor_copy", (tA[:, 4:8], modA[:, 0:4]), reads=[("modA", 0)], writes=["tA"])
            K.op("dve", "tensor_copy", (tA[:, 8:12], modG[:, 0:4]), reads=[("modG", 0)], writes=["tA"])
            K.dma("sp", dbg_d["mods"], tA[:, 0:12], reads=["tA"])

        def finish_raw():
            outs = []
            for c in range(4):
                outs.append(K.dma("sp", out_d[c * 128:(c + 1) * 128, :], X[:, c, :], reads=[("x", c, tt) for tt in range(NTT)]))
            K.wait_only("sp", outs)
            K.emit()

        if stop == "pro":
            finish_raw()
            return nc

        def norm_stats():
            for tt in range(NTT):
                tsl = slice(tt * TT, (tt + 1) * TT)
                hs = HY[:, tt % 2, 0:4, :]
                for c in range(4):
                    K.op("act", "activation", (), dict(out=hs[:, c, :], in_=X[:, c, tsl], func=AF.Square),
                         reads=[("x", c, tt)], writes=[("hy", tt % 2)])
                for c in range(4):
                    K.op("pe", "matmul", (PM[0:1, :], ones_bf[:, 0:1], hs[:, c, :]), dict(start=(c == 0), stop=(c == 3)),
                         reads=[("hy", tt % 2), "ones_bf"], writes=["pm"], sig=(c == 3))
                K.op("act", "activation", (), dict(out=ssq_st[0:1, :], in_=PM[0:1, :], func=AF.Copy),
                     reads=["pm"], writes=["ssq_st"])
                K.dma("sp", ag1_in[0:1, tsl], ssq_st[0:1, :], reads=["ssq_st"], writes=["ag1_in"])
            K.cc([ag1_in], [ag1_out], GROUPS, reads=["ag1_in"], writes=["ag1_out"])
            for tt in range(NTT):
                rstd_tile(tt)

        rstd_all = MT.ap()[:, 0:8192].bitcast(F32)

        def rstd_tile(tt):
            tsl = slice(tt * TT, (tt + 1) * TT)
            rstd = rstd_all[:, tsl]
            K.dma("sp", ssq4[0:4, :], ag1_out[0:4, tsl], reads=["ag1_out"], writes=["ssq4"])
            K.op("pe", "matmul", (PM[:, :], ones_f[0:4, :], ssq4[0:4, :]), dict(start=True, stop=True),
                 reads=["ssq4", "ones_f"], writes=["pm"])
            K.op("act", "activation", (), dict(out=rstd, in_=PM[:, :], func=AF.Sqrt, bias=epsc[:, 0:1]),
                 reads=["pm", "epsc"], writes=[("rstd", tt)])
            K.op("dve", RECIP, (rstd, rstd), reads=[("rstd", tt)], writes=[("rstd", tt)])

        epsc = sb("epsc", [128, 2], F32)
        K.op("dve", "memset", (epsc[:, 0:1], D * EPS), writes=["epsc"])
        K.op("dve", "memset", (epsc[:, 1:2], 256 * EPS), writes=["epsc"])

        def load_tile(src, tt, slot):
            if src is ag2_out:
                parts = [(ag2_out, "ag2_out", 0, 16)]
            else:
                parts = [(ag3a_out, "ag3a_out", 0, 8), (ag3r_out, "ag3r_out", 8, 8)]
            for sap, skey, k0, nk in parts:
                v = sap[tt].rearrange("(k p) t -> p k t", p=128)
                for q4 in range(nk // 4):
                    K.dma("sp", HY[:, slot, k0 + q4 * 4:k0 + (q4 + 1) * 4, :], v[:, q4 * 4:(q4 + 1) * 4, :],
                          reads=[(skey, tt)], writes=[("hy", slot)])

        def load_w(src2d, ncols, dst=None, key="W", extra=()):
            base = WB.ap() if dst is None else dst
            Wv = base[:, 0:16 * ncols].rearrange("p (k n) -> p k n", k=16)
            for q4 in range(4):
                K.dma("pool", Wv[:, q4 * 4:(q4 + 1) * 4, :],
                      src2d[q4 * 512:(q4 + 1) * 512, :].rearrange("(k p) n -> p k n", p=128), writes=[key], extra=extra)
            return Wv

        pbi = [0]

        def next_pb(n=2):
            i = pbi[0] % n
            pbi[0] += 1
            return i

        def proj_pass(l, col0, nch, evac, Wv=None, post_tile=None):
            if Wv is None:
                Wv = load_w(win_d[l, :, col0 * 128:(col0 + nch) * 128], nch * 128)
            for tt in range(NTT):
                slot = tt % 2
                load_tile(ag2_out, tt, slot)
                for ch in range(nch):
                    b = next_pb()
                    for kc in range(16):
                        K.op("pe", "matmul", (PB[b][:, :], Wv[:, kc, ch * 128:(ch + 1) * 128], HY[:, slot, kc, :]),
                             dict(start=(kc == 0), stop=(kc == 15)), reads=["W", ("hy", slot)], writes=[("pb", b)],
                             sig=(kc == 15))
                    evac(ch, tt, PB[b][:, :], ("pb", b))
                if post_tile is not None:
                    post_tile(tt)


        WBa = WB.ap()
        MTa = MT.ap()
        acc_o = MTa[:, 0:8192].bitcast(F32)
        acc_d = MTa[:, 8192:16384].bitcast(F32)
        VT = WBa[:, 0:4096].rearrange("p (i e) -> p i e", e=128)
        Es = [WBa[:, 4096 + 512 * i:4096 + 512 * (i + 1)].bitcast(F32) for i in range(2)]
        Ps = [WBa[:, 5120 + 256 * i:5120 + 256 * (i + 1)] for i in range(2)]
        SCALE = 128.0 ** -0.5
        PT2 = PM.ap().bitcast(BF16).rearrange("p (i e) -> p i e", e=128)
        PTS = [(PT, "pt"), (PT2, "pm")]

        ya_stage = MTa[:, 8192:16384].rearrange("p (n t) -> p n t", t=1024)

        def attention_head(l, hh, Wv_in):
            def evac(ch, tt, pap, bk):
                tsl = slice(tt * TT, (tt + 1) * TT)
                if ch == 0:
                    K.op("act", "activation", (), dict(out=PO[:, 0, tsl], in_=pap, func=AF.Copy, scale=SCALE),
                         reads=[bk], writes=[("po", 0)])
                elif ch == 3:
                    K.op("act", "activation", (), dict(out=PO[:, 3, tsl], in_=pap, func=AF.Silu),
                         reads=[bk], writes=[("po", 3)])
                else:
                    K.op("dve", "tensor_copy", (PO[:, ch, tsl], pap), reads=[bk], writes=[("po", ch)])
            proj_pass(l, hh * 4, 4, evac, Wv=Wv_in)
            K.barrier()
            if l + 1 < L:
                mods_load(l + 1, hh, STG)
            for pi, d in enumerate(PATTERNS):
                Wm = amask[:, hh * 3 + pi, :]
                Ls = T // d
                nkt = Ls // 128
                for idx in range(32):
                    r, m = idx // nkt, idx % nkt
                    tok0 = r + d * 128 * m
                    half = (idx // 4) % 2
                    PTb, ptk = PTS[half]
                    K.op("pe", "transpose", (PTb[:, idx % 4, :], PO[:, 2, ssl(tok0, 128, d)], ident[:]),
                         reads=[("po", 2), "ident"], writes=[ptk], sig=(idx % 4 == 3))
                    if idx % 4 == 3:
                        if half == 0:
                            K.op("act", "activation", (), dict(out=VT[:, idx - 3:idx + 1, :], in_=PTb[:, 0:4, :], func=AF.Copy),
                                 reads=[ptk], writes=[("vt", idx // 4)])
                        else:
                            K.op("dve", "tensor_copy", (VT[:, idx - 3:idx + 1, :], PTb[:, 0:4, :]),
                                 reads=[ptk], writes=[("vt", idx // 4)])
                tiles = [(r, m) for r in range(d) for m in range(nkt)]

                def geom(r, m):
                    c_lo = 64 if m == 0 else 0
                    c_hi = 192 if m == nkt - 1 else 256
                    return c_lo, c_hi

                def emit_S(i):
                    r, m = tiles[i]
                    c_lo, c_hi = geom(r, m)
                    nq = c_hi - c_lo
                    tq0 = r + d * (128 * m - 64 + c_lo)
                    kt0 = r + d * 128 * m
                    sbk = i % 2
                    K.op("pe", "matmul", (PB[sbk][:, 0:nq], PO[:, 1, ssl(kt0, 128, d)], PO[:, 0, ssl(tq0, nq, d)]),
                         dict(start=True, stop=True), reads=[("po", 0), ("po", 1)], writes=[("pb", sbk)])

                def emit_EP(i):
                    r, m = tiles[i]
                    c_lo, c_hi = geom(r, m)
                    nq = c_hi - c_lo
                    sbk = i % 2
                    K.op("act", "activation", (), dict(out=Es[sbk][:, 0:nq], in_=PB[sbk][:, 0:nq], func=AF.Exp),
                         reads=[("pb", sbk)], writes=[("E", sbk)])
                    K.op("dve", "tensor_tensor", (Ps[sbk][:, c_lo:c_hi], Es[sbk][:, 0:nq], Wm[:, c_lo:c_hi], ALU.mult),
                         reads=[("E", sbk), "amask"], writes=[("P", sbk)])

                def emit_PV(i):
                    r, m = tiles[i]
                    c_lo, c_hi = geom(r, m)
                    sbk = i % 2
                    vt = VT[:, r * nkt + m, :]
                    vtk = ("vt", (r * nkt + m) // 4)
                    bka = (m // 4) % 2
                    ca = (m % 4) * 128
                    for which, lhs, base in (("o", vt, 2), ("d", ones_bf[:, :], 4)):
                        K.op("pe", "matmul", (PB[base + bka][:, ca + c_lo:ca + 128], lhs, Ps[sbk][:, c_lo:128]),
                             dict(start=(m == 0), stop=True), reads=[("P", sbk), vtk, "ones_bf"],
                             writes=[("pb", base + bka)], sig=(which == "d"))
                    bkb = ((m + 1) // 4) % 2
                    cb = ((m + 1) % 4) * 128
                    for which, lhs, base in (("o", vt, 2), ("d", ones_bf[:, :], 4)):
                        K.op("pe", "matmul", (PB[base + bkb][:, cb:cb + c_hi - 128], lhs, Ps[sbk][:, 128:c_hi]),
                             dict(start=True, stop=(m == nkt - 1)), reads=[("P", sbk), vtk, "ones_bf"],
                             writes=[("pb", base + bkb)], sig=(which == "d"))
                    groups = []
                    if m % 4 == 3:
                        groups.append(m // 4)
                    if m == nkt - 1:
                        groups.append(nkt // 4)
                    for k in groups:
                        jmax = min(4 * k + 3, nkt)
                        lo = 64 if k == 0 else 0
                        hi = (jmax % 4) * 128 + (64 if jmax == nkt else 128)
                        sub0 = 128 * 4 * k - 64 + lo
                        n = hi - lo
                        t0 = r + d * sub0
                        bk = k % 2
                        for acc, base, key in ((acc_o, 2, "acco"), (acc_d, 4, "accd")):
                            dst = acc[:, ssl(t0, n, d)]
                            if pi == 0:
                                K.op("dve", "tensor_copy", (dst, PB[base + bk][:, lo:hi]), reads=[("pb", base + bk)], writes=[key])
                            else:
                                K.op("dve", "tensor_tensor", (dst, dst, PB[base + bk][:, lo:hi], ALU.add),
                                     reads=[("pb", base + bk), key], writes=[key])

                emit_S(0)
                for i in range(len(tiles)):
                    emit_EP(i)
                    if i + 1 < len(tiles):
                        emit_S(i + 1)
                    emit_PV(i)
            if l + 1 < L:
                mods_mm(l + 1, hh, STG)
            ncol0 = (hh + 1) * 4
            allev = [(e, K.cnt[e]) for e in ("pe", "act", "dve", "pool") if K.cnt.get(e, 0) > 0]
            Wv_next = load_w(win_d[l, :, ncol0 * 128:(ncol0 + 4) * 128], 512, extra=allev)
            for tt in range(NTT):
                tsl = slice(tt * TT, (tt + 1) * TT)
                K.op("dve", "reciprocal", (acc_d[:, tsl], acc_d[:, tsl]), reads=["accd"], writes=[("accd", tt)])
                K.op("pool", "tensor_tensor", (acc_o[:, tsl], acc_o[:, tsl], acc_d[:, tsl], ALU.mult), reads=["acco", ("accd", tt)], writes=[("acco", tt)])
                K.op("pool", "tensor_tensor", (ya_stage[:, tt, 0:TT], acc_o[:, tsl], PO[:, 3, tsl], ALU.mult),
                     reads=[("acco", tt), ("accd", tt), ("po", 3)], writes=[("yast", tt)])
            K.dma("sp", ag3a_in[:, hh * 128:(hh + 1) * 128, :].rearrange("n p t -> p n t"),
                  ya_stage[:, :, 0:TT], reads=[("yast", tt) for tt in range(NTT)], writes=[("ag3a_in", tt) for tt in range(NTT)])
            return Wv_next

        Sb_all = MTa[:, 0:8192].rearrange("p (n e) -> p n e", e=256)
        VTr = MTa[:, 8192:16384].rearrange("p (n e) -> p n e", e=256)
        kzs = [WBa[:, 128 * i:128 * (i + 1)] for i in range(2)]
        Prs = [WBa[:, 256 + 128 * i:256 + 128 * (i + 1)] for i in range(2)]
        qxi = [WBa[:, 512 + 512 * i:512 + 512 * (i + 1)] for i in range(2)]
        Sst = [WBa[:, 1536 + 512 * i:1536 + 512 * (i + 1)].bitcast(F32) for i in range(2)]
        Sfb = [WBa[:, 2560 + 256 * i:2560 + 256 * (i + 1)] for i in range(2)]
        sqs = WBa[:, 3072:4096].rearrange("p (c t) -> p c t", c=2)
        rs = WBa[:, 4096:5120].bitcast(F32)
        RC = 768
        ysr = [MTa[:, 8192 + 1024 * i:8192 + 1024 * (i + 1)].rearrange("p (c t) -> p c t", c=2) for i in range(2)]
        Wo = [None]

        def retention_head(l, Wv_b1):
            lgf = lg[:, 2 * l:2 * l + 1]
            lgb = lg[:, 2 * l + 1:2 * l + 2]
            K.op("act", "activation", (), dict(out=rM[:], in_=rconst[:, 0:128], func=AF.Exp, scale=lgf), reads=["rconst", "lg"], writes=["rM"])
            K.op("dve", "tensor_tensor", (rM[:], rM[:], rconst[:, 128:256], ALU.mult), reads=["rM", "rconst"], writes=["rM"])
            K.op("act", "activation", (), dict(out=rtmp[:], in_=rconst[:, 256:384], func=AF.Exp, scale=lgb), reads=["rconst", "lg"], writes=["rtmp"])
            K.op("dve", "tensor_tensor", (rtmp[:], rtmp[:], rconst[:, 384:512], ALU.mult), reads=["rtmp", "rconst"], writes=["rtmp"])
            K.op("dve", "tensor_tensor", (rM[:], rM[:], rtmp[:], ALU.add), reads=["rM", "rtmp"], writes=["rM"])
            K.op("act", "activation", (), dict(out=rxi[:, 0, :], in_=rconst[:, 512:640], func=AF.Exp, scale=lgf), reads=["rconst", "lg"], writes=["rxi"])
            K.op("act", "activation", (), dict(out=rxi[:, 1, :], in_=rconst[:, 640:768], func=AF.Exp, scale=lgb), reads=["rconst", "lg"], writes=["rxi"])
            K.op("act", "activation", (), dict(out=rcol[:, 0:1], in_=rconst[:, RC:RC + 1], func=AF.Exp, scale=lgf), reads=["rconst", "lg"], writes=["rcol"])
            K.op("act", "activation", (), dict(out=rcol[:, 1:2], in_=rconst[:, RC + 1:RC + 2], func=AF.Exp, scale=lgb), reads=["rconst", "lg"], writes=["rcol"])
            K.op("act", "activation", (), dict(out=rcol[:, 2:3], in_=rconst[:, RC + 2:RC + 3], func=AF.Exp, scale=lgf), reads=["rconst", "lg"], writes=["rcol"])
            K.op("act", "activation", (), dict(out=rcol[:, 3:4], in_=rconst[:, RC + 2:RC + 3], func=AF.Exp, scale=lgb), reads=["rconst", "lg"], writes=["rcol"])

            def evac(ch, tt, pap, bk):
                tsl = slice(tt * TT, (tt + 1) * TT)
                if ch == 1:
                    K.op("act", "activation", (), dict(out=PO[:, 1, tsl], in_=pap, func=AF.Copy, scale=SCALE),
                         reads=[bk], writes=[("po", 1)])
                elif ch == 0:
                    K.op("act", "activation", (), dict(out=PO[:, 0, tsl], in_=pap, func=AF.Copy), reads=[bk], writes=[("po", 0)])
                else:
                    K.op("dve", "tensor_copy", (PO[:, ch, tsl], pap), reads=[bk], writes=[("po", ch)])
            for tt in range(NTT):
                K.cc([ag3a_in[tt]], [ag3a_out[tt]], GROUPS, reads=[("ag3a_in", tt)], writes=[("ag3a_out", tt)])
            proj_pass(l, 8, 4, evac, Wv=Wv_b1)
            K.barrier()
            if l + 1 < L:
                mods_load(l + 1, 2, STG)
            K.op("dve", "memset", (Sst[0], 0.0), writes=["Sf"])
            K.op("dve", "memset", (Sst[1], 0.0), writes=["Sb"])
            for n in range(31, -1, -1):
                csl = slice(n * 128, (n + 1) * 128)
                half = n % 2
                PTb, ptk = PTS[half]
                for c2 in range(2):
                    K.op("pe", "transpose", (PTb[:, c2, :], PO[:, 2 + c2, csl], ident[:]),
                         reads=[("po", 2 + c2), "ident"], writes=[ptk], sig=False)
                K.op("pe", "transpose", (PTb[:, 2, :], PO[:, 1, csl], ident[:]),
                     reads=[("po", 1), "ident"], writes=[ptk], sig=True)
                K.op("act", "activation", (), dict(out=VTr[:, n, :].rearrange("p (c e) -> p c e", c=2), in_=PTb[:, 0:2, :], func=AF.Copy),
                     reads=[ptk], writes=[("vtr", n)])
                K.op("dve", "tensor_scalar", (kzs[half], PTb[:, 2, :], rcol[:, 1:2], None, ALU.mult),
                     reads=[ptk, "rcol"], writes=[("kz", half)])
                K.op("act", "activation", (), dict(out=Sb_all[:, n, :], in_=Sst[1], func=AF.Copy), reads=["Sb"], writes=[("sball", n)])
                b = next_pb()
                K.op("pe", "matmul", (PB[b][:, 0:256], kzs[half], VTr[:, n, :]), dict(start=True, stop=True),
                     reads=[("kz", half), ("vtr", n)], writes=[("pb", b)])
                K.op("dve", "scalar_tensor_tensor", (Sst[1], Sst[1], rcol[:, 3:4], PB[b][:, 0:256], ALU.mult, ALU.add),
                     reads=[("pb", b), "Sb", "rcol"], writes=["Sb"])
            for gq in range(NTT):
                tsl = slice(gq * TT, (gq + 1) * TT)
                for ci in range(4):
                    csl = slice(gq * TT + ci * 128, gq * TT + (ci + 1) * 128)
                    K.op("pool", "tensor_tensor", (qxi[0][:, ci * 128:(ci + 1) * 128], PO[:, 0, csl], rxi[:, 0, :], ALU.mult),
                         reads=[("po", 0), "rxi"], writes=["qxi0"])
                    K.op("pool", "tensor_tensor", (qxi[1][:, ci * 128:(ci + 1) * 128], PO[:, 0, csl], rxi[:, 1, :], ALU.mult),
                         reads=[("po", 0), "rxi"], writes=["qxi1"])
                ob = 2 + 2 * (gq % 2)
                for ci in range(4):
                    n = gq * 4 + ci
                    csl = slice(n * 128, (n + 1) * 128)
                    half = n % 2
                    K.op("pe", "transpose", (PT[:, 0, :], PO[:, 1, csl], ident[:]),
                         reads=[("po", 1), "ident"], writes=["pt"], sig=True)
                    K.op("dve", "tensor_scalar", (kzs[half], PT[:, 0, :], rcol[:, 0:1], None, ALU.mult),
                         reads=["pt", "rcol"], writes=[("kz", half)])
                    b = next_pb()
                    K.op("pe", "matmul", (PB[b][:, 0:128], PO[:, 1, csl], PO[:, 0, csl]), dict(start=True, stop=True),
                         reads=[("po", 0), ("po", 1)], writes=[("pb", b)])
                    K.op("dve", "tensor_tensor", (Prs[half], PB[b][:, 0:128], rM[:], ALU.mult), reads=[("pb", b), "rM"], writes=[("Pr", half)])
                    for c2 in range(2):
                        dst = PB[ob + c2][:, ci * 128:(ci + 1) * 128]
                        terms = [(VTr[:, n, c2 * 128:(c2 + 1) * 128], Prs[half], [("vtr", n), ("Pr", half)])]
                        if n > 0:
                            terms.append((Sfb[n % 2][:, c2 * 128:(c2 + 1) * 128], qxi[0][:, ci * 128:(ci + 1) * 128], [("Sfb", n % 2), "qxi0"]))
                        if n < 31:
                            terms.append((Sb_all[:, n, c2 * 128:(c2 + 1) * 128], qxi[1][:, ci * 128:(ci + 1) * 128], [("sball", n), "qxi1"]))
                        for ti, (lhs, rhs, rk) in enumerate(terms):
                            K.op("pe", "matmul", (dst, lhs, rhs), dict(start=(ti == 0), stop=(ti == len(terms) - 1)),
                                 reads=rk, writes=[("pb", ob + c2)], sig=(ti == len(terms) - 1))
                    b = next_pb()
                    K.op("pe", "matmul", (PB[b][:, 0:256], kzs[half], VTr[:, n, :]), dict(start=True, stop=True),
                         reads=[("kz", half), ("vtr", n)], writes=[("pb", b)])
                    K.op("dve", "scalar_tensor_tensor", (Sst[0], Sst[0], rcol[:, 2:3], PB[b][:, 0:256], ALU.mult, ALU.add),
                         reads=[("pb", b), "Sf", "rcol"], writes=["Sf"])
                    K.op("act", "activation", (), dict(out=Sfb[(n + 1) % 2], in_=Sst[0], func=AF.Copy), reads=["Sf"], writes=[("Sfb", (n + 1) % 2)])
                for c2 in range(2):
                    K.op("act", "activation", (), dict(out=sqs[:, c2, :], in_=PB[ob + c2][:, :], func=AF.Square),
                         reads=[("pb", ob + c2)], writes=[("sq", c2)])
                for c2 in range(2):
                    K.op("pe", "matmul", (PM[:, :], ones_bf[:, :], sqs[:, c2, :]), dict(start=(c2 == 0), stop=(c2 == 1)),
                         reads=[("sq", c2), "ones_bf"], writes=["pm"], sig=(c2 == 1))
                K.op("act", "activation", (), dict(out=rs, in_=PM[:, :], func=AF.Sqrt, bias=epsc[:, 1:2]),
                     reads=["pm", "epsc"], writes=["rs"])
                K.op("dve", RECIP, (rs, rs), reads=["rs"], writes=["rs"])
                for c2 in range(2):
                    K.op("dve", "scalar_tensor_tensor", (PO[:, 2 + c2, tsl], PB[ob + c2][:, :], 16.0, rs, ALU.mult, ALU.mult),
                         reads=[("pb", ob + c2), "rs"], writes=[("po", 2 + c2)])
            if l + 1 < L:
                mods_mm(l + 1, 2, STG)
                mods_finish(l + 1)
            K.barrier()
            Wo[0] = load_w(wout_d[l, :, :], 512, dst=MTa, key="Wo")

            def evac2(ch, tt, pap, bk):
                tsl = slice(tt * TT, (tt + 1) * TT)
                K.op("act", "activation", (), dict(out=PO[:, ch, tsl], in_=pap, func=AF.Silu), reads=[bk], writes=[("po", ch)])

            def post(tt):
                tsl = slice(tt * TT, (tt + 1) * TT)
                ys = ysr[tt % 2]
                for c2 in range(2):
                    K.op("dve", "tensor_tensor", (ys[:, c2, :], PO[:, 2 + c2, tsl], PO[:, c2, tsl], ALU.mult),
                         reads=[("po", 2 + c2), ("po", c2)], writes=[("ysr", tt % 2)])
                K.dma("pool", ag3r_in[tt].rearrange("(c p) t -> p c t", p=128), ys, reads=[("ysr", tt % 2)], writes=[("ag3r_in", tt)])
                K.cc([ag3r_in[tt]], [ag3r_out[tt]], GROUPS, reads=[("ag3r_in", tt)], writes=[("ag3r_out", tt)])
            proj_pass(l, 12, 2, evac2, post_tile=post)
            K.barrier()

        ystage = HY[:, 0, :, :].rearrange("p k t -> p (k t)")
        for l in range(L):
            A_ = modA[:, l * 4:l * 4 + 4]
            B_ = modB[:, l * 4:l * 4 + 4]
            G_ = modG[:, l * 4:l * 4 + 4]
            W_A0 = load_w(win_d[l, :, 0:512], 512)
            norm_stats()
            if stop == "n1":
                finish_raw()
                return nc
            for tt in range(NTT):
                tsl = slice(tt * TT, (tt + 1) * TT)
                slot = tt % 2
                for c in range(4):
                    K.op("dve", "scalar_tensor_tensor", (tA[:], X[:, c, tsl], A_[:, c:c + 1], rstd_all[:, tsl], ALU.mult, ALU.mult),
                         reads=[("x", c, tt), ("rstd", tt), ("modA", l)], writes=["tA"])
                    K.op("act", "activation", (), dict(out=HY[:, slot, c, :], in_=tA[:], func=AF.Identity, bias=B_[:, c:c + 1]),
                         reads=["tA", ("modB", l)], writes=[("hy", slot)])
                K.dma("sp", ag2_in[tt].rearrange("(c p) t -> p c t", p=128), HY[:, slot, 0:4, :],
                      reads=[("hy", slot)], writes=[("ag2_in", tt)])
                K.cc([ag2_in[tt]], [ag2_out[tt]], GROUPS, reads=[("ag2_in", tt)], writes=[("ag2_out", tt)])
            if dbg and l == 0:
                for tt in range(NTT):
                    load_tile(ag2_out, tt, tt % 2)
                    K.dma("sp", dbg_d["h"].rearrange("(k p) t -> p k t", p=128)[:, :, tt * TT:(tt + 1) * TT], HY[:, tt % 2, :, :],
                          reads=[("hy", tt % 2)])

            if stop == "n2":
                finish_raw()
                return nc
            if mix:
                Wn = attention_head(l, 0, W_A0)
                Wn = attention_head(l, 1, Wn)
                retention_head(l, Wn)
            if dbg and l == 0:
                for tt in range(NTT):
                    load_tile(None, tt, tt % 2)
                    K.dma("sp", dbg_d["y"].rearrange("(k p) t -> p k t", p=128)[:, :, tt * TT:(tt + 1) * TT], HY[:, tt % 2, :, :],
                          reads=[("hy", tt % 2)])

            if stop == "m":
                finish_raw()
                return nc
            Wv = Wo[0]
            for tt in range(NTT):
                tsl = slice(tt * TT, (tt + 1) * TT)
                slot = tt % 2
                load_tile(None, tt, slot)
                for c in range(4):
                    b = next_pb()
                    for kc in range(16):
                        K.op("pe", "matmul", (PB[b][:, :], Wv[:, kc, c * 128:(c + 1) * 128], HY[:, slot, kc, :]),
                             dict(start=(kc == 0), stop=(kc == 15)), reads=["Wo", ("hy", slot)], writes=[("pb", b)],
                             sig=(kc == 15))
                    K.op("dve", "scalar_tensor_tensor", (X[:, c, tsl], PB[b][:, :], G_[:, c:c + 1], X[:, c, tsl], ALU.mult, ALU.add),
                         reads=[("pb", b), ("x", c, tt), ("modG", l)], writes=[("x", c, tt)])

        if stop == "o":
            finish_raw()
            return nc
        norm_stats()
        outs = []
        for tt in range(NTT):
            tsl = slice(tt * TT, (tt + 1) * TT)
            for c in range(4):
                K.op("dve", "scalar_tensor_tensor", (tA[:], X[:, c, tsl], fgain[:, c:c + 1], rstd_all[:, tsl], ALU.mult, ALU.mult),
                     reads=[("x", c, tt), ("rstd", tt), "fgain"], writes=["tA"])
                outs.append(K.dma("sp", out_d[c * 128:(c + 1) * 128, tsl], tA[:], reads=["tA"]))
        K.wait_only("sp", outs)
        K.emit()
    return nc


def _host_consts():
    p = np.arange(128)[:, None]
    c = np.arange(256)[None, :]
    rel = np.abs(c - 64 - p).astype(np.float64)
    return rel


def prep_inputs(x, c, norm_gain, w_ada, b_ada, w_in, w_out, ret_decay_logit_f, ret_decay_logit_b, final_gain, L=NL):
    f32 = np.float32
    x = np.asarray(x, f32); c = np.asarray(c, f32)
    norm_gain = np.asarray(norm_gain, f32); w_ada = np.asarray(w_ada, f32)[:L]; b_ada = np.asarray(b_ada, f32)
    w_in = np.asarray(w_in, f32)[:L]; w_out = np.asarray(w_out, f32)[:L]
    dlf = np.asarray(ret_decay_logit_f, f32); dlb = np.asarray(ret_decay_logit_b, f32)
    final_gain = np.asarray(final_gain, f32)
    rel = _host_consts()
    ident = np.eye(128, dtype=f32)
    j = np.arange(128)[:, None].astype(np.float64)
    i = np.arange(128)[None, :].astype(np.float64)
    Rf = np.maximum(i - j, 0); Uf = (i >= j).astype(np.float64)
    Rb = np.maximum(j - i, 0); Ub = (j >= i).astype(np.float64)
    I1 = np.broadcast_to(i + 1.0, (128, 128)); I2 = np.broadcast_to(128.0 - i, (128, 128))
    cols = np.concatenate([127.0 - j, j, np.full((128, 1), 128.0), np.zeros((128, 1))], axis=1)
    rconst = np.concatenate([Rf, Uf, Rb, Ub, I1, I2, cols], axis=1).astype(f32)
    in_maps = []
    for core in range(8):
        b, g = core // 4, core % 4
        dsl = slice(512 * g, 512 * g + 512)
        m = {}
        m["xT"] = np.ascontiguousarray(x[b].T[dsl, :])
        m["cT"] = np.ascontiguousarray(c[b].reshape(16, 128).T)
        m["wada"] = np.ascontiguousarray(np.concatenate([w_ada[:, :, dsl], w_ada[:, :, 2048:4096][:, :, dsl],
                                                         w_ada[:, :, 4096:6144][:, :, dsl]], axis=2))
        ba = np.concatenate([b_ada[:, dsl], b_ada[:, 2048:4096][:, dsl], b_ada[:, 4096:6144][:, dsl]], axis=1)
        m["bada"] = np.ascontiguousarray(ba.reshape(NL, 12, 128).transpose(2, 0, 1).reshape(128, NL * 12))
        m["gain"] = np.ascontiguousarray(norm_gain[:, dsl].reshape(NL, 4, 128).transpose(2, 0, 1).reshape(128, NL * 4))
        m["fgain"] = np.ascontiguousarray(final_gain[dsl].reshape(4, 128).T)
        cols_in = []
        for hh in (2 * g, 2 * g + 1):
            for base in (0, 1024, 2048, 3072):
                cols_in.append(np.arange(base + 128 * hh, base + 128 * hh + 128))
        cols_in.append(np.arange(4096 + 128 * g, 4096 + 128 * g + 128))
        cols_in.append(np.arange(4608 + 128 * g, 4608 + 128 * g + 128))
        cols_in.append(np.arange(5120 + 256 * g, 5120 + 256 * g + 256))
        cols_in.append(np.arange(6144 + 256 * g, 6144 + 256 * g + 256))
        cols_in = np.concatenate(cols_in)
        m["win"] = np.ascontiguousarray(w_in[:, :, cols_in])
        rows = []
        for g2 in range(4):
            rows.append(np.arange(128 * (2 * g2), 128 * (2 * g2) + 128))
            rows.append(np.arange(128 * (2 * g2 + 1), 128 * (2 * g2 + 1) + 128))
            rows.append(np.arange(1024 + 256 * g2, 1024 + 256 * g2 + 256))
        rows = np.concatenate(rows)
        m["wout"] = np.ascontiguousarray(w_out[:, :, dsl])
        dl = np.stack([dlf[:, g], dlb[:, g]], axis=1).reshape(1, NL * 2)
        m["dlog"] = np.ascontiguousarray(np.broadcast_to(dl, (128, NL * 2))).astype(f32)
        m["ident"] = ident
        am = []
        for hh in (2 * g, 2 * g + 1):
            slope = 2.0 ** (-(hh + 1.0))
            for d in PATTERNS:
                am.append(np.where(rel <= 64, np.exp(-slope * d * rel), 0.0))
        m["amask"] = np.ascontiguousarray(np.concatenate(am, axis=1)).astype(f32)
        m["rconst"] = rconst
        in_maps.append(m)
    return in_maps


def assemble(results):
    out = np.empty((2, T, D), np.float32)
    for core in range(8):
        b, g = core // 4, core % 4
        out[b, :, 512 * g:512 * g + 512] = results[core]["outT"].T
    return out


_NC_CACHE = {}


def kernel(x, c, norm_gain, w_ada, b_ada, w_in, w_out, ret_decay_logit_f, ret_decay_logit_b, final_gain):
    in_maps = prep_inputs(x, c, norm_gain, w_ada, b_ada, w_in, w_out, ret_decay_logit_f, ret_decay_logit_b, final_gain)
    nc = build()
    res = run_bass_kernel_spmd(nc, in_maps, core_ids=list(range(8)))
    return assemble(res.results)
```

```python
import numpy as np
import ml_dtypes
from contextlib import ExitStack
import concourse.bass as bass
import concourse.mybir as mybir
from concourse.bass_utils import run_bass_kernel_spmd

F32 = mybir.dt.float32
BF16 = mybir.dt.bfloat16
AF = mybir.ActivationFunctionType
ALU = mybir.AluOpType

D = 2048
T = 4096
NL = 4
TT = 512
NTT = T // TT
EPS = 1e-6
PATTERNS = (1, 4, 16)
ENGS = ("pe", "act", "dve", "pool", "sp")


def ssl(start, n, step):
    return slice(start, start + (n - 1) * step + 1, step)
NRING = 8
RECIP = "reciprocal"


class Sched:
    def __init__(self, nc, es):
        self.nc, self.es = nc, es
        self.q = {e: [] for e in ENGS}
        self.sems, self.cnt = {}, {}
        self.waited = {e: {} for e in ENGS}
        self.lastw, self.readers = {}, {}
        self.ring_idx = {"sp": 0, "pool": 0, "act": 0}
        self.pend_r, self.pend_w = set(), set()

    def sem(self, key):
        if key not in self.sems:
            self.sems[key] = self.es.enter_context(self.nc.semaphore("sem_" + str(key)))
            self.cnt[key] = 0
        return self.sems[key]

    def _deps(self, eng, reads, writes):
        evs = []
        for k in list(reads) + list(writes):
            if eng != "pe" and (k in self.pend_r or k in self.pend_w):
                raise RuntimeError(f"dependency on unsignalled PE op for key {k}")
        for k in reads:
            w = self.lastw.get(k)
            if w:
                evs.append(w)
        for k in writes:
            w = self.lastw.get(k)
            if w:
                evs.append(w)
            for sk, v in self.readers.get(k, {}).items():
                evs.append((sk, v))
        return evs

    def _waits(self, eng, evs):
        out = []
        for sk, v in evs:
            if eng == "pe" and sk == "pe":
                continue
            if self.waited[eng].get(sk, 0) >= v:
                continue
            self.waited[eng][sk] = v
            out.append((sk, v))
        return out

    def _register(self, ev, reads, writes):
        for k in reads:
            self.readers.setdefault(k, {})[ev[0]] = ev[1]
        for k in writes:
            self.lastw[k] = ev
            self.readers[k] = {}

    @staticmethod
    def _is_psum(k):
        return k == "pm" or k == "pt" or (isinstance(k, tuple) and k[0] == "pb")

    def op(self, eng, name, args=(), kw=None, reads=(), writes=(), sig=True, extra=()):
        kw = kw or {}
        self.sem(eng)
        writes = list(writes) + [k for k in reads if self._is_psum(k)]
        reads = [k for k in reads if not self._is_psum(k)]
        evs = self._deps(eng, reads, writes) + list(extra)
        waits = self._waits(eng, evs)
        if sig:
            self.cnt[eng] += 1
            ev = (eng, self.cnt[eng])
            if eng == "pe":
                reads = set(reads) | self.pend_r
                writes = set(writes) | self.pend_w
                self.pend_r, self.pend_w = set(), set()
            self._register(ev, reads, writes)
        else:
            assert eng == "pe"
            ev = None
            self.pend_r |= set(reads)
            self.pend_w |= set(writes)
        self.q[eng].append((waits, name, args, kw, ("eng", eng) if sig else None))
        return ev

    def dma(self, qeng, out, in_, reads=(), writes=(), extra=(), **kw):
        ring = self.ring_idx[qeng]
        self.ring_idx[qeng] += 1
        sk = f"d{qeng}{ring % NRING}"
        self.sem(sk)
        evs = self._deps(qeng, reads, writes) + list(extra)
        if self.cnt[sk] > 0:
            evs.append((sk, self.cnt[sk]))
        waits = self._waits(qeng, evs)
        self.cnt[sk] += 16
        ev = (sk, self.cnt[sk])
        self._register(ev, reads, writes)
        kw2 = dict(out=out, in_=in_)
        kw2.update(kw)
        self.q[qeng].append((waits, "dma_start", (), kw2, ("dma", sk)))
        return ev

    def cc(self, ins, outs, groups, reads=(), writes=()):
        self.sem("cc")
        evs = self._deps("pool", reads, writes)
        waits = self._waits("pool", evs)
        self.cnt["cc"] += 1
        ev = ("cc", self.cnt["cc"])
        self._register(ev, reads, writes)
        kw = dict(replica_groups=groups, ins=ins, outs=outs)
        self.q["pool"].append((waits, "collective_compute", ("AllGather", ALU.bypass), kw, ("cc", "cc")))
        return ev

    def barrier(self):
        evs = [(k, v) for k, v in self.cnt.items() if v > 0]
        assert not self.pend_r and not self.pend_w
        for e in ENGS:
            self.wait_only(e, [ev for ev in evs if ev[0] != e])

    def wait_only(self, eng, evs):
        waits = self._waits(eng, evs)
        self.q[eng].append((waits, None, (), {}, None))

    def emit(self):
        nc = self.nc
        block = self.es.enter_context(nc.Block())

        def mk(engname):
            def f(e):
                for waits, name, args, kw, sig in self.q[engname]:
                    for sk, v in waits:
                        e.wait_ge(self.sems[sk], v)
                    if name is None:
                        continue
                    ins = getattr(e, name)(*args, **kw)
                    if sig is None:
                        continue
                    if sig[0] == "eng":
                        ins.then_inc(self.sems[sig[1]], 1)
                    elif sig[0] == "dma":
                        ins.then_inc(self.sems[sig[1]], 16)
                    else:
                        ins.then_inc(self.sems[sig[1]])
            return f

        block.tensor(mk("pe"))
        block.scalar(mk("act"))
        block.vector(mk("dve"))
        block.gpsimd(mk("pool"))
        block.sync(mk("sp"))


def build(n_layers=NL, mix=True, dbg=False, stop=None):
    nc = bass.Bass("TRN2", target_bir_lowering=False)
    L = n_layers

    def din(name, shape, dt=F32):
        return nc.dram_tensor(name, shape, dt, kind="ExternalInput").ap()

    xT_d = din("xT", [512, T])
    cT_d = din("cT", [128, 16])
    wada_d = din("wada", [L, D, 1536])
    bada_d = din("bada", [128, NL * 12])
    gain_d = din("gain", [128, NL * 4])
    fgain_d = din("fgain", [128, 4])
    win_d = din("win", [L, D, 1792])
    wout_d = din("wout", [L, D, 512])
    dlog_d = din("dlog", [128, NL * 2])
    ident_d = din("ident", [128, 128])
    amask_d = din("amask", [128, 6 * 256])
    rconst_d = din("rconst", [128, 6 * 128 + 4])
    out_d = nc.dram_tensor("outT", [512, T], F32, kind="ExternalOutput").ap()
    dbg_d = {}
    if dbg:
        dbg_d["h"] = nc.dram_tensor("dbg_h", [D, T], BF16, kind="ExternalOutput").ap()
        dbg_d["y"] = nc.dram_tensor("dbg_y", [D, T], BF16, kind="ExternalOutput").ap()
        dbg_d["mods"] = nc.dram_tensor("dbg_mods", [128, 12], F32, kind="ExternalOutput").ap()

    ag1_in = nc.dram_tensor("ag1_in", [1, T], F32).ap()
    ag1_out = nc.dram_tensor("ag1_out", [4, T], F32).ap()
    ag2_in = nc.dram_tensor("ag2_in", [NTT, 512, TT], BF16).ap()
    ag2_out = nc.dram_tensor("ag2_out", [NTT, D, TT], BF16).ap()
    ag3a_in = nc.dram_tensor("ag3a_in", [NTT, 256, TT], BF16).ap()
    ag3a_out = nc.dram_tensor("ag3a_out", [NTT, 1024, TT], BF16).ap()
    ag3r_in = nc.dram_tensor("ag3r_in", [NTT, 256, TT], BF16).ap()
    ag3r_out = nc.dram_tensor("ag3r_out", [NTT, 1024, TT], BF16).ap()
    GROUPS = [[0, 1, 2, 3], [4, 5, 6, 7]]

    with ExitStack() as es:
        def sb(name, shape, dt):
            return es.enter_context(nc.sbuf_tensor("sb_" + name, shape, dt))

        def ps(name, shape, dt):
            return es.enter_context(nc.psum_tensor("ps_" + name, shape, dt))

        X = sb("X", [128, 4, T], F32)
        HY = sb("HY", [128, 2, 16, TT], BF16)
        WB = sb("WB", [128, 16 * 512], BF16)
        PO = sb("PO", [128, 4, T], BF16)
        MT = sb("MT", [128, 16384], BF16)
        ident = sb("ident", [128, 128], BF16)
        ones_bf = sb("ones_bf", [128, 128], BF16)
        ones_f = sb("ones_f", [128, 128], F32)
        amask = sb("amask", [128, 6, 256], F32)
        rconst = sb("rconst", [128, 6 * 128 + 4], F32)
        cT = sb("cT", [128, 16], F32)
        cA = sb("cA", [128, 16], BF16)
        bada = sb("bada", [128, NL * 12], F32)
        gain = sb("gain", [128, NL * 4], F32)
        fgain = sb("fgain", [128, 4], F32)
        dlog = sb("dlog", [128, NL * 2], F32)
        lg = sb("lg", [128, NL * 2], F32)
        modA = sb("modA", [128, NL * 4], F32)
        modB = sb("modB", [128, NL * 4], F32)
        modG = sb("modG", [128, NL * 4], F32)
        modrow = sb("modrow", [1, 1536], F32)
        rM = sb("rM", [128, 128], F32)
        rtmp = sb("rtmp", [128, 128], F32)
        rxi = sb("rxi", [128, 2, 128], F32)
        rcol = sb("rcol", [128, 4], F32)
        tA = sb("tA", [128, TT], F32)
        rstd = sb("rstd", [128, TT], F32)
        ssq_st = sb("ssq_st", [1, TT], F32)
        ssq4 = sb("ssq4", [4, TT], F32)

        PB = [ps(f"pb{i}", [128, 512], F32) for i in range(6)]
        PT = ps("pt", [128, 8, 128], BF16)
        PM = ps("pm", [128, 512], F32)

        K = Sched(nc, es)

        K.dma("pool", ident[:], ident_d, writes=["ident"])
        K.dma("sp", amask[:], amask_d.rearrange("p (a b) -> p a b", b=256), writes=["amask"])
        K.dma("sp", rconst[:], rconst_d, writes=["rconst"])
        K.dma("sp", cT[:], cT_d, writes=["cT"])
        K.dma("sp", bada[:], bada_d, writes=["bada"])
        K.dma("sp", gain[:], gain_d, writes=["gain"])
        K.dma("sp", fgain[:], fgain_d, writes=["fgain"])
        K.dma("sp", dlog[:], dlog_d, writes=["dlog"])
        for c in range(4):
            K.dma("sp", X[:, c, :], xT_d[c * 128:(c + 1) * 128, :], writes=[("x", c, tt) for tt in range(NTT)])
        K.op("dve", "memset", (ones_bf[:], 1.0), writes=["ones_bf"])
        K.op("dve", "memset", (ones_f[:], 1.0), writes=["ones_f"])
        K.op("act", "activation", (), dict(out=cA[:], in_=cT[:], func=AF.Silu), reads=["cT"], writes=["cA"])
        K.op("act", "activation", (), dict(out=lg[:], in_=dlog[:], func=AF.Exp, scale=-1.0), reads=["dlog"], writes=["lg"])
        K.op("act", "activation", (), dict(out=lg[:], in_=lg[:], func=AF.Ln, bias=1.0), reads=["lg"], writes=["lg"])
        K.op("dve", "tensor_scalar", (lg[:], lg[:], -1.0, None, ALU.mult), reads=["lg"], writes=["lg"])

        SQD = float(np.sqrt(D))

        def mods_load(l2, grp, stage):
            for q4 in range(4):
                K.dma("pool", stage[:, q4 * 4:(q4 + 1) * 4, :],
                      wada_d[l2, q4 * 512:(q4 + 1) * 512, grp * 512:(grp + 1) * 512].rearrange("(k p) n -> p k n", p=128),
                      writes=[("hy", 1)])

        def mods_mm(l2, grp, stage):
            for kc in range(16):
                K.op("pe", "matmul", (PM[0:1, :], cA[:, kc:kc + 1], stage[:, kc, :]),
                     dict(start=(kc == 0), stop=(kc == 15)), reads=[("hy", 1), "cA"], writes=["pm"], sig=(kc == 15))
            K.op("act", "activation", (), dict(out=modrow[0:1, grp * 512:(grp + 1) * 512], in_=PM[0:1, :], func=AF.Copy),
                 reads=["pm"], writes=["modrow"])

        def mods_finish(l2):
            for j in range(12):
                K.op("pe", "matmul", (PM[:, j:j + 1], modrow[0:1, j * 128:(j + 1) * 128], ones_f[0:1, 0:1]),
                     dict(start=True, stop=True), reads=["modrow", "ones_f"], writes=["pm"], sig=(j == 11))
            sl4 = slice(l2 * 4, l2 * 4 + 4)
            K.op("dve", "tensor_tensor", (modB[:, sl4], PM[:, 0:4], bada[:, l2 * 12:l2 * 12 + 4], ALU.add),
                 reads=["pm", "bada"], writes=[("modB", l2)])
            K.op("dve", "tensor_tensor", (modA[:, sl4], PM[:, 4:8], bada[:, l2 * 12 + 4:l2 * 12 + 8], ALU.add),
                 reads=["pm", "bada"], writes=[("modA", l2)])
            K.op("dve", "scalar_tensor_tensor", (modA[:, sl4], modA[:, sl4], 1.0, gain[:, sl4], ALU.add, ALU.mult),
                 reads=[("modA", l2), "gain"], writes=[("modA", l2)])
            K.op("dve", "tensor_scalar", (modA[:, sl4], modA[:, sl4], SQD, None, ALU.mult), reads=[("modA", l2)], writes=[("modA", l2)])
            K.op("dve", "tensor_tensor", (modG[:, sl4], PM[:, 8:12], bada[:, l2 * 12 + 8:l2 * 12 + 12], ALU.add),
                 reads=["pm", "bada"], writes=[("modG", l2)])

        STG = HY[:, 1, :, :]
        for grp in range(3):
            mods_load(0, grp, STG)
            mods_mm(0, grp, STG)
        mods_finish(0)
        K.op("dve", "tensor_scalar", (fgain[:], fgain[:], SQD, None, ALU.mult), reads=["fgain"], writes=["fgain"])
        if dbg:
            K.op("dve", "tensor_copy", (tA[:, 0:4], modB[:, 0:4]), reads=[("modB", 0)], writes=["tA"])
            K.op("dve", "tensor_copy", (tA[:, 4:8], modA[:, 0:4]), reads=[("modA", 0)], writes=["tA"])
            K.op("dve", "tensor_copy", (tA[:, 8:12], modG[:, 0:4]), reads=[("modG", 0)], writes=["tA"])
            K.dma("sp", dbg_d["mods"], tA[:, 0:12], reads=["tA"])

        def finish_raw():
            outs = []
            for c in range(4):
                outs.append(K.dma("sp", out_d[c * 128:(c + 1) * 128, :], X[:, c, :], reads=[("x", c, tt) for tt in range(NTT)]))
            K.wait_only("sp", outs)
            K.emit()

        if stop == "pro":
            finish_raw()
            return nc

        def norm_stats():
            for tt in range(NTT):
                tsl = slice(tt * TT, (tt + 1) * TT)
                hs = HY[:, tt % 2, 0:4, :]
                for c in range(4):
                    K.op("act", "activation", (), dict(out=hs[:, c, :], in_=X[:, c, tsl], func=AF.Square),
                         reads=[("x", c, tt)], writes=[("hy", tt % 2)])
                for c in range(4):
                    K.op("pe", "matmul", (PM[0:1, :], ones_bf[:, 0:1], hs[:, c, :]), dict(start=(c == 0), stop=(c == 3)),
                         reads=[("hy", tt % 2), "ones_bf"], writes=["pm"], sig=(c == 3))
                K.op("act", "activation", (), dict(out=ssq_st[0:1, :], in_=PM[0:1, :], func=AF.Copy),
                     reads=["pm"], writes=["ssq_st"])
                K.dma("sp", ag1_in[0:1, tsl], ssq_st[0:1, :], reads=["ssq_st"], writes=["ag1_in"])
            K.cc([ag1_in], [ag1_out], GROUPS, reads=["ag1_in"], writes=["ag1_out"])
            for tt in range(NTT):
                rstd_tile(tt)

        rstd_all = MT.ap()[:, 0:8192].bitcast(F32)

        def rstd_tile(tt):
            tsl = slice(tt * TT, (tt + 1) * TT)
            rstd = rstd_all[:, tsl]
            K.dma("sp", ssq4[0:4, :], ag1_out[0:4, tsl], reads=["ag1_out"], writes=["ssq4"])
            K.op("pe", "matmul", (PM[:, :], ones_f[0:4, :], ssq4[0:4, :]), dict(start=True, stop=True),
                 reads=["ssq4", "ones_f"], writes=["pm"])
            K.op("act", "activation", (), dict(out=rstd, in_=PM[:, :], func=AF.Sqrt, bias=epsc[:, 0:1]),
                 reads=["pm", "epsc"], writes=[("rstd", tt)])
            K.op("dve", RECIP, (rstd, rstd), reads=[("rstd", tt)], writes=[("rstd", tt)])

        epsc = sb("epsc", [128, 2], F32)
        K.op("dve", "memset", (epsc[:, 0:1], D * EPS), writes=["epsc"])
        K.op("dve", "memset", (epsc[:, 1:2], 256 * EPS), writes=["epsc"])

        def load_tile(src, tt, slot):
            if src is ag2_out:
                parts = [(ag2_out, "ag2_out", 0, 16)]
            else:
                parts = [(ag3a_out, "ag3a_out", 0, 8), (ag3r_out, "ag3r_out", 8, 8)]
            for sap, skey, k0, nk in parts:
                v = sap[tt].rearrange("(k p) t -> p k t", p=128)
                for q4 in range(nk // 4):
                    K.dma("sp", HY[:, slot, k0 + q4 * 4:k0 + (q4 + 1) * 4, :], v[:, q4 * 4:(q4 + 1) * 4, :],
                          reads=[(skey, tt)], writes=[("hy", slot)])

        def load_w(src2d, ncols, dst=None, key="W", extra=()):
            base = WB.ap() if dst is None else dst
            Wv = base[:, 0:16 * ncols].rearrange("p (k n) -> p k n", k=16)
            for q4 in range(4):
                K.dma("pool", Wv[:, q4 * 4:(q4 + 1) * 4, :],
                      src2d[q4 * 512:(q4 + 1) * 512, :].rearrange("(k p) n -> p k n", p=128), writes=[key], extra=extra)
            return Wv

        pbi = [0]

        def next_pb(n=2):
            i = pbi[0] % n
            pbi[0] += 1
            return i

        def proj_pass(l, col0, nch, evac, Wv=None, post_tile=None):
            if Wv is None:
                Wv = load_w(win_d[l, :, col0 * 128:(col0 + nch) * 128], nch * 128)
            for tt in range(NTT):
                slot = tt % 2
                load_tile(ag2_out, tt, slot)
                for ch in range(nch):
                    b = next_pb()
                    for kc in range(16):
                        K.op("pe", "matmul", (PB[b][:, :], Wv[:, kc, ch * 128:(ch + 1) * 128], HY[:, slot, kc, :]),
                             dict(start=(kc == 0), stop=(kc == 15)), reads=["W", ("hy", slot)], writes=[("pb", b)],
                             sig=(kc == 15))
                    evac(ch, tt, PB[b][:, :], ("pb", b))
                if post_tile is not None:
                    post_tile(tt)


        WBa = HY[:, 0, :, :].rearrange("p k t -> p (k t)")
        MTa = MT.ap()
        acc_o = MTa[:, 0:8192].bitcast(F32)
        acc_d = MTa[:, 8192:16384].bitcast(F32)
        VT = WBa[:, 0:4096].rearrange("p (i e) -> p i e", e=128)
        Es = [WBa[:, 4096 + 512 * i:4096 + 512 * (i + 1)].bitcast(F32) for i in range(2)]
        Ps = [WBa[:, 5120 + 256 * i:5120 + 256 * (i + 1)] for i in range(2)]
        SCALE = 128.0 ** -0.5
        PT2 = PM.ap().bitcast(BF16).rearrange("p (i e) -> p i e", e=128)
        PTS = [(PT, "pt"), (PT2, "pm")]

        ya_stage = MTa[:, 8192:16384].rearrange("p (n t) -> p n t", t=1024)

        def attention_head(l, hh, Wv_in):
            def evac(ch, tt, pap, bk):
                tsl = slice(tt * TT, (tt + 1) * TT)
                if ch == 0:
                    K.op("act", "activation", (), dict(out=PO[:, 0, tsl], in_=pap, func=AF.Copy, scale=SCALE),
                         reads=[bk], writes=[("po", 0)])
                elif ch == 3:
                    K.op("act", "activation", (), dict(out=PO[:, 3, tsl], in_=pap, func=AF.Silu),
                         reads=[bk], writes=[("po", 3)])
                else:
                    K.op("dve", "tensor_copy", (PO[:, ch, tsl], pap), reads=[bk], writes=[("po", ch)])
            proj_pass(l, hh * 4, 4, evac, Wv=Wv_in)
            K.barrier()
            ncol0 = (hh + 1) * 4
            Wv_next = load_w(win_d[l, :, ncol0 * 128:(ncol0 + 4) * 128], 512)
            if l + 1 < L:
                mods_load(l + 1, hh, STG)
            for pi, d in enumerate(PATTERNS):
                Wm = amask[:, hh * 3 + pi, :]
                Ls = T // d
                nkt = Ls // 128
                for idx in range(32):
                    r, m = idx // nkt, idx % nkt
                    tok0 = r + d * 128 * m
                    half = (idx // 4) % 2
                    PTb, ptk = PTS[half]
                    K.op("pe", "transpose", (PTb[:, idx % 4, :], PO[:, 2, ssl(tok0, 128, d)], ident[:]),
                         reads=[("po", 2), "ident"], writes=[ptk], sig=(idx % 4 == 3))
                    if idx % 4 == 3:
                        if half == 0:
                            K.op("act", "activation", (), dict(out=VT[:, idx - 3:idx + 1, :], in_=PTb[:, 0:4, :], func=AF.Copy),
                                 reads=[ptk], writes=[("vt", idx // 4)])
                        else:
                            K.op("dve", "tensor_copy", (VT[:, idx - 3:idx + 1, :], PTb[:, 0:4, :]),
                                 reads=[ptk], writes=[("vt", idx // 4)])
                tiles = [(r, m) for r in range(d) for m in range(nkt)]

                def geom(r, m):
                    c_lo = 64 if m == 0 else 0
                    c_hi = 192 if m == nkt - 1 else 256
                    return c_lo, c_hi

                def emit_S(i):
                    r, m = tiles[i]
                    c_lo, c_hi = geom(r, m)
                    nq = c_hi - c_lo
                    tq0 = r + d * (128 * m - 64 + c_lo)
                    kt0 = r + d * 128 * m
                    sbk = i % 2
                    K.op("pe", "matmul", (PB[sbk][:, 0:nq], PO[:, 1, ssl(kt0, 128, d)], PO[:, 0, ssl(tq0, nq, d)]),
                         dict(start=True, stop=True), reads=[("po", 0), ("po", 1)], writes=[("pb", sbk)])

                def emit_EP(i):
                    r, m = tiles[i]
                    c_lo, c_hi = geom(r, m)
                    nq = c_hi - c_lo
                    sbk = i % 2
                    K.op("act", "activation", (), dict(out=Es[sbk][:, 0:nq], in_=PB[sbk][:, 0:nq], func=AF.Exp),
                         reads=[("pb", sbk)], writes=[("E", sbk)])
                    K.op("dve", "tensor_tensor", (Ps[sbk][:, c_lo:c_hi], Es[sbk][:, 0:nq], Wm[:, c_lo:c_hi], ALU.mult),
                         reads=[("E", sbk), "amask"], writes=[("P", sbk)])

                def emit_PV(i):
                    r, m = tiles[i]
                    c_lo, c_hi = geom(r, m)
                    sbk = i % 2
                    vt = VT[:, r * nkt + m, :]
                    vtk = ("vt", (r * nkt + m) // 4)
                    bka = (m // 4) % 2
                    ca = (m % 4) * 128
                    for which, lhs, base in (("o", vt, 2), ("d", ones_bf[:, :], 4)):
                        K.op("pe", "matmul", (PB[base + bka][:, ca + c_lo:ca + 128], lhs, Ps[sbk][:, c_lo:128]),
                             dict(start=(m == 0), stop=True), reads=[("P", sbk), vtk, "ones_bf"],
                             writes=[("pb", base + bka)], sig=(which == "d"))
                    bkb = ((m + 1) // 4) % 2
                    cb = ((m + 1) % 4) * 128
                    for which, lhs, base in (("o", vt, 2), ("d", ones_bf[:, :], 4)):
                        K.op("pe", "matmul", (PB[base + bkb][:, cb:cb + c_hi - 128], lhs, Ps[sbk][:, 128:c_hi]),
                             dict(start=True, stop=(m == nkt - 1)), reads=[("P", sbk), vtk, "ones_bf"],
                             writes=[("pb", base + bkb)], sig=(which == "d"))
                    groups = []
                    if m % 4 == 3:
                        groups.append(m // 4)
                    if m == nkt - 1:
                        groups.append(nkt // 4)
                    for k in groups:
                        jmax = min(4 * k + 3, nkt)
                        lo = 64 if k == 0 else 0
                        hi = (jmax % 4) * 128 + (64 if jmax == nkt else 128)
                        sub0 = 128 * 4 * k - 64 + lo
                        n = hi - lo
                        t0 = r + d * sub0
                        bk = k % 2
                        for acc, base, key in ((acc_o, 2, "acco"), (acc_d, 4, "accd")):
                            dst = acc[:, ssl(t0, n, d)]
                            if pi == 0:
                                K.op("dve", "tensor_copy", (dst, PB[base + bk][:, lo:hi]), reads=[("pb", base + bk)], writes=[key])
                            else:
                                K.op("dve", "tensor_tensor", (dst, dst, PB[base + bk][:, lo:hi], ALU.add),
                                     reads=[("pb", base + bk), key], writes=[key])

                emit_S(0)
                for i in range(len(tiles)):
                    emit_EP(i)
                    if i + 1 < len(tiles):
                        emit_S(i + 1)
                    emit_PV(i)
            if l + 1 < L:
                mods_mm(l + 1, hh, STG)
            allev = [(e, K.cnt[e]) for e in ("pe", "act", "dve", "pool") if K.cnt.get(e, 0) > 0]
            K.wait_only("sp", allev)
            for tt in range(NTT):
                tsl = slice(tt * TT, (tt + 1) * TT)
                K.op("dve", "reciprocal", (acc_d[:, tsl], acc_d[:, tsl]), reads=["accd"], writes=[("accd", tt)])
                K.op("pool", "tensor_tensor", (acc_o[:, tsl], acc_o[:, tsl], acc_d[:, tsl], ALU.mult), reads=["acco", ("accd", tt)], writes=[("acco", tt)])
                K.op("pool", "tensor_tensor", (ya_stage[:, tt, 0:TT], acc_o[:, tsl], PO[:, 3, tsl], ALU.mult),
                     reads=[("acco", tt), ("accd", tt), ("po", 3)], writes=[("yast", tt)])
            K.dma("sp", ag3a_in[:, hh * 128:(hh + 1) * 128, :].rearrange("n p t -> p n t"),
                  ya_stage[:, :, 0:TT], reads=[("yast", tt) for tt in range(NTT)], writes=[("ag3a_in", tt) for tt in range(NTT)])
            return Wv_next

        Sb_all = MTa[:, 0:8192].rearrange("p (n e) -> p n e", e=256)
        VTr = MTa[:, 8192:16384].rearrange("p (n e) -> p n e", e=256)
        kzs = [WBa[:, 128 * i:128 * (i + 1)] for i in range(2)]
        Prs = [WBa[:, 256 + 128 * i:256 + 128 * (i + 1)] for i in range(2)]
        qxi = [WBa[:, 512 + 512 * i:512 + 512 * (i + 1)] for i in range(2)]
        Sst = [WBa[:, 1536 + 512 * i:1536 + 512 * (i + 1)].bitcast(F32) for i in range(2)]
        Sfb = [WBa[:, 2560 + 256 * i:2560 + 256 * (i + 1)] for i in range(2)]
        sqs = WBa[:, 3072:4096].rearrange("p (c t) -> p c t", c=2)
        rs = WBa[:, 4096:5120].bitcast(F32)
        RC = 768
        ysr = [MTa[:, 8192 + 1024 * i:8192 + 1024 * (i + 1)].rearrange("p (c t) -> p c t", c=2) for i in range(2)]
        Wo = [None]

        def retention_head(l, Wv_b1):
            lgf = lg[:, 2 * l:2 * l + 1]
            lgb = lg[:, 2 * l + 1:2 * l + 2]
            K.op("act", "activation", (), dict(out=rM[:], in_=rconst[:, 0:128], func=AF.Exp, scale=lgf), reads=["rconst", "lg"], writes=["rM"])
            K.op("dve", "tensor_tensor", (rM[:], rM[:], rconst[:, 128:256], ALU.mult), reads=["rM", "rconst"], writes=["rM"])
            K.op("act", "activation", (), dict(out=rtmp[:], in_=rconst[:, 256:384], func=AF.Exp, scale=lgb), reads=["rconst", "lg"], writes=["rtmp"])
            K.op("dve", "tensor_tensor", (rtmp[:], rtmp[:], rconst[:, 384:512], ALU.mult), reads=["rtmp", "rconst"], writes=["rtmp"])
            K.op("dve", "tensor_tensor", (rM[:], rM[:], rtmp[:], ALU.add), reads=["rM", "rtmp"], writes=["rM"])
            K.op("act", "activation", (), dict(out=rxi[:, 0, :], in_=rconst[:, 512:640], func=AF.Exp, scale=lgf), reads=["rconst", "lg"], writes=["rxi"])
            K.op("act", "activation", (), dict(out=rxi[:, 1, :], in_=rconst[:, 640:768], func=AF.Exp, scale=lgb), reads=["rconst", "lg"], writes=["rxi"])
            K.op("act", "activation", (), dict(out=rcol[:, 0:1], in_=rconst[:, RC:RC + 1], func=AF.Exp, scale=lgf), reads=["rconst", "lg"], writes=["rcol"])
            K.op("act", "activation", (), dict(out=rcol[:, 1:2], in_=rconst[:, RC + 1:RC + 2], func=AF.Exp, scale=lgb), reads=["rconst", "lg"], writes=["rcol"])
            K.op("act", "activation", (), dict(out=rcol[:, 2:3], in_=rconst[:, RC + 2:RC + 3], func=AF.Exp, scale=lgf), reads=["rconst", "lg"], writes=["rcol"])
            K.op("act", "activation", (), dict(out=rcol[:, 3:4], in_=rconst[:, RC + 2:RC + 3], func=AF.Exp, scale=lgb), reads=["rconst", "lg"], writes=["rcol"])

            def evac(ch, tt, pap, bk):
                tsl = slice(tt * TT, (tt + 1) * TT)
                if ch == 1:
                    K.op("act", "activation", (), dict(out=PO[:, 1, tsl], in_=pap, func=AF.Copy, scale=SCALE),
                         reads=[bk], writes=[("po", 1)])
                elif ch == 0:
                    K.op("act", "activation", (), dict(out=PO[:, 0, tsl], in_=pap, func=AF.Copy), reads=[bk], writes=[("po", 0)])
                else:
                    K.op("dve", "tensor_copy", (PO[:, ch, tsl], pap), reads=[bk], writes=[("po", ch)])
            for tt in range(NTT):
                K.cc([ag3a_in[tt]], [ag3a_out[tt]], GROUPS, reads=[("ag3a_in", tt)], writes=[("ag3a_out", tt)])
            proj_pass(l, 8, 4, evac, Wv=Wv_b1)
            K.barrier()
            Wv_b2 = load_w(win_d[l, :, 12 * 128:14 * 128], 256)
            if l + 1 < L:
                mods_load(l + 1, 2, STG)
            K.op("dve", "memset", (Sst[0], 0.0), writes=["Sf"])
            K.op("dve", "memset", (Sst[1], 0.0), writes=["Sb"])
            for n in range(31, -1, -1):
                csl = slice(n * 128, (n + 1) * 128)
                half = n % 2
                PTb, ptk = PTS[half]
                for c2 in range(2):
                    K.op("pe", "transpose", (PTb[:, c2, :], PO[:, 2 + c2, csl], ident[:]),
                         reads=[("po", 2 + c2), "ident"], writes=[ptk], sig=False)
                K.op("pe", "transpose", (PTb[:, 2, :], PO[:, 1, csl], ident[:]),
                     reads=[("po", 1), "ident"], writes=[ptk], sig=True)
                K.op("act", "activation", (), dict(out=VTr[:, n, :].rearrange("p (c e) -> p c e", c=2), in_=PTb[:, 0:2, :], func=AF.Copy),
                     reads=[ptk], writes=[("vtr", n)])
                K.op("dve", "tensor_scalar", (kzs[half], PTb[:, 2, :], rcol[:, 1:2], None, ALU.mult),
                     reads=[ptk, "rcol"], writes=[("kz", half)])
                K.op("act", "activation", (), dict(out=Sb_all[:, n, :], in_=Sst[1], func=AF.Copy), reads=["Sb"], writes=[("sball", n)])
                b = next_pb()
                K.op("pe", "matmul", (PB[b][:, 0:256], kzs[half], VTr[:, n, :]), dict(start=True, stop=True),
                     reads=[("kz", half), ("vtr", n)], writes=[("pb", b)])
                K.op("dve", "scalar_tensor_tensor", (Sst[1], Sst[1], rcol[:, 3:4], PB[b][:, 0:256], ALU.mult, ALU.add),
                     reads=[("pb", b), "Sb", "rcol"], writes=["Sb"])
            for gq in range(NTT):
                tsl = slice(gq * TT, (gq + 1) * TT)
                for ci in range(4):
                    csl = slice(gq * TT + ci * 128, gq * TT + (ci + 1) * 128)
                    K.op("pool", "tensor_tensor", (qxi[0][:, ci * 128:(ci + 1) * 128], PO[:, 0, csl], rxi[:, 0, :], ALU.mult),
                         reads=[("po", 0), "rxi"], writes=["qxi0"])
                    K.op("pool", "tensor_tensor", (qxi[1][:, ci * 128:(ci + 1) * 128], PO[:, 0, csl], rxi[:, 1, :], ALU.mult),
                         reads=[("po", 0), "rxi"], writes=["qxi1"])
                ob = 2 + 2 * (gq % 2)
                for ci in range(4):
                    n = gq * 4 + ci
                    csl = slice(n * 128, (n + 1) * 128)
                    half = n % 2
                    K.op("pe", "transpose", (PT[:, 0, :], PO[:, 1, csl], ident[:]),
                         reads=[("po", 1), "ident"], writes=["pt"], sig=True)
                    K.op("dve", "tensor_scalar", (kzs[half], PT[:, 0, :], rcol[:, 0:1], None, ALU.mult),
                         reads=["pt", "rcol"], writes=[("kz", half)])
                    b = next_pb()
                    K.op("pe", "matmul", (PB[b][:, 0:128], PO[:, 1, csl], PO[:, 0, csl]), dict(start=True, stop=True),
                         reads=[("po", 0), ("po", 1)], writes=[("pb", b)])
                    K.op("dve", "tensor_tensor", (Prs[half], PB[b][:, 0:128], rM[:], ALU.mult), reads=[("pb", b), "rM"], writes=[("Pr", half)])
                    for c2 in range(2):
                        dst = PB[ob + c2][:, ci * 128:(ci + 1) * 128]
                        terms = [(VTr[:, n, c2 * 128:(c2 + 1) * 128], Prs[half], [("vtr", n), ("Pr", half)])]
                        if n > 0:
                            terms.append((Sfb[n % 2][:, c2 * 128:(c2 + 1) * 128], qxi[0][:, ci * 128:(ci + 1) * 128], [("Sfb", n % 2), "qxi0"]))
                        if n < 31:
                            terms.append((Sb_all[:, n, c2 * 128:(c2 + 1) * 128], qxi[1][:, ci * 128:(ci + 1) * 128], [("sball", n), "qxi1"]))
                        for ti, (lhs, rhs, rk) in enumerate(terms):
                            K.op("pe", "matmul", (dst, lhs, rhs), dict(start=(ti == 0), stop=(ti == len(terms) - 1)),
                                 reads=rk, writes=[("pb", ob + c2)], sig=(ti == len(terms) - 1))
                    b = next_pb()
                    K.op("pe", "matmul", (PB[b][:, 0:256], kzs[half], VTr[:, n, :]), dict(start=True, stop=True),
                         reads=[("kz", half), ("vtr", n)], writes=[("pb", b)])
                    K.op("dve", "scalar_tensor_tensor", (Sst[0], Sst[0], rcol[:, 2:3], PB[b][:, 0:256], ALU.mult, ALU.add),
                         reads=[("pb", b), "Sf", "rcol"], writes=["Sf"])
                    K.op("act", "activation", (), dict(out=Sfb[(n + 1) % 2], in_=Sst[0], func=AF.Copy), reads=["Sf"], writes=[("Sfb", (n + 1) % 2)])
                for c2 in range(2):
                    K.op("act", "activation", (), dict(out=sqs[:, c2, :], in_=PB[ob + c2][:, :], func=AF.Square),
                         reads=[("pb", ob + c2)], writes=[("sq", c2)])
                for c2 in range(2):
                    K.op("pe", "matmul", (PM[:, :], ones_bf[:, :], sqs[:, c2, :]), dict(start=(c2 == 0), stop=(c2 == 1)),
                         reads=[("sq", c2), "ones_bf"], writes=["pm"], sig=(c2 == 1))
                K.op("act", "activation", (), dict(out=rs, in_=PM[:, :], func=AF.Sqrt, bias=epsc[:, 1:2]),
                     reads=["pm", "epsc"], writes=["rs"])
                K.op("dve", RECIP, (rs, rs), reads=["rs"], writes=["rs"])
                for c2 in range(2):
                    K.op("dve", "scalar_tensor_tensor", (PO[:, 2 + c2, tsl], PB[ob + c2][:, :], 16.0, rs, ALU.mult, ALU.mult),
                         reads=[("pb", ob + c2), "rs"], writes=[("po", 2 + c2)])
            if l + 1 < L:
                mods_mm(l + 1, 2, STG)
                mods_finish(l + 1)
            K.barrier()
            Wo[0] = load_w(wout_d[l, :, :], 512, dst=MTa, key="Wo")

            def evac2(ch, tt, pap, bk):
                tsl = slice(tt * TT, (tt + 1) * TT)
                K.op("act", "activation", (), dict(out=PO[:, ch, tsl], in_=pap, func=AF.Silu), reads=[bk], writes=[("po", ch)])

            def post(tt):
                tsl = slice(tt * TT, (tt + 1) * TT)
                ys = ysr[tt % 2]
                for c2 in range(2):
                    K.op("dve", "tensor_tensor", (ys[:, c2, :], PO[:, 2 + c2, tsl], PO[:, c2, tsl], ALU.mult),
                         reads=[("po", 2 + c2), ("po", c2)], writes=[("ysr", tt % 2)])
                K.dma("pool", ag3r_in[tt].rearrange("(c p) t -> p c t", p=128), ys, reads=[("ysr", tt % 2)], writes=[("ag3r_in", tt)])
                K.cc([ag3r_in[tt]], [ag3r_out[tt]], GROUPS, reads=[("ag3r_in", tt)], writes=[("ag3r_out", tt)])
            proj_pass(l, 12, 2, evac2, Wv=Wv_b2, post_tile=post)
            K.barrier()

        ystage = HY[:, 0, :, :].rearrange("p k t -> p (k t)")
        for l in range(L):
            A_ = modA[:, l * 4:l * 4 + 4]
            B_ = modB[:, l * 4:l * 4 + 4]
            G_ = modG[:, l * 4:l * 4 + 4]
            W_A0 = load_w(win_d[l, :, 0:512], 512)
            norm_stats()
            if stop == "n1":
                finish_raw()
                return nc
            for tt in range(NTT):
                tsl = slice(tt * TT, (tt + 1) * TT)
                slot = tt % 2
                for c in range(4):
                    K.op("dve", "scalar_tensor_tensor", (tA[:], X[:, c, tsl], A_[:, c:c + 1], rstd_all[:, tsl], ALU.mult, ALU.mult),
                         reads=[("x", c, tt), ("rstd", tt), ("modA", l)], writes=["tA"])
                    K.op("act", "activation", (), dict(out=HY[:, slot, c, :], in_=tA[:], func=AF.Identity, bias=B_[:, c:c + 1]),
                         reads=["tA", ("modB", l)], writes=[("hy", slot)])
                K.dma("sp", ag2_in[tt].rearrange("(c p) t -> p c t", p=128), HY[:, slot, 0:4, :],
                      reads=[("hy", slot)], writes=[("ag2_in", tt)])
                K.cc([ag2_in[tt]], [ag2_out[tt]], GROUPS, reads=[("ag2_in", tt)], writes=[("ag2_out", tt)])
            if dbg and l == 0:
                for tt in range(NTT):
                    load_tile(ag2_out, tt, tt % 2)
                    K.dma("sp", dbg_d["h"].rearrange("(k p) t -> p k t", p=128)[:, :, tt * TT:(tt + 1) * TT], HY[:, tt % 2, :, :],
                          reads=[("hy", tt % 2)])

            if stop == "n2":
                finish_raw()
                return nc
            if mix:
                Wn = attention_head(l, 0, W_A0)
                Wn = attention_head(l, 1, Wn)
                retention_head(l, Wn)
            if dbg and l == 0:
                for tt in range(NTT):
                    load_tile(None, tt, tt % 2)
                    K.dma("sp", dbg_d["y"].rearrange("(k p) t -> p k t", p=128)[:, :, tt * TT:(tt + 1) * TT], HY[:, tt % 2, :, :],
                          reads=[("hy", tt % 2)])

            if stop == "m":
                finish_raw()
                return nc
            Wv = Wo[0]
            for tt in range(NTT):
                tsl = slice(tt * TT, (tt + 1) * TT)
                slot = tt % 2
                load_tile(None, tt, slot)
                for c in range(4):
                    b = next_pb()
                    for kc in range(16):
                        K.op("pe", "matmul", (PB[b][:, :], Wv[:, kc, c * 128:(c + 1) * 128], HY[:, slot, kc, :]),
                             dict(start=(kc == 0), stop=(kc == 15)), reads=["Wo", ("hy", slot)], writes=[("pb", b)],
                             sig=(kc == 15))
                    K.op("dve", "scalar_tensor_tensor", (X[:, c, tsl], PB[b][:, :], G_[:, c:c + 1], X[:, c, tsl], ALU.mult, ALU.add),
                         reads=[("pb", b), ("x", c, tt), ("modG", l)], writes=[("x", c, tt)])

        if stop == "o":
            finish_raw()
            return nc
        norm_stats()
        outs = []
        for tt in range(NTT):
            tsl = slice(tt * TT, (tt + 1) * TT)
            for c in range(4):
                K.op("dve", "scalar_tensor_tensor", (tA[:], X[:, c, tsl], fgain[:, c:c + 1], rstd_all[:, tsl], ALU.mult, ALU.mult),
                     reads=[("x", c, tt), ("rstd", tt), "fgain"], writes=["tA"])
                outs.append(K.dma("sp", out_d[c * 128:(c + 1) * 128, tsl], tA[:], reads=["tA"]))
        K.wait_only("sp", outs)
        K.emit()
    return nc


def _host_consts():
    p = np.arange(128)[:, None]
    c = np.arange(256)[None, :]
    rel = np.abs(c - 64 - p).astype(np.float64)
    return rel


def prep_inputs(x, c, norm_gain, w_ada, b_ada, w_in, w_out, ret_decay_logit_f, ret_decay_logit_b, final_gain, L=NL):
    f32 = np.float32
    x = np.asarray(x, f32); c = np.asarray(c, f32)
    norm_gain = np.asarray(norm_gain, f32); w_ada = np.asarray(w_ada, f32)[:L]; b_ada = np.asarray(b_ada, f32)
    w_in = np.asarray(w_in, f32)[:L]; w_out = np.asarray(w_out, f32)[:L]
    dlf = np.asarray(ret_decay_logit_f, f32); dlb = np.asarray(ret_decay_logit_b, f32)
    final_gain = np.asarray(final_gain, f32)
    rel = _host_consts()
    ident = np.eye(128, dtype=f32)
    j = np.arange(128)[:, None].astype(np.float64)
    i = np.arange(128)[None, :].astype(np.float64)
    Rf = np.maximum(i - j, 0); Uf = (i >= j).astype(np.float64)
    Rb = np.maximum(j - i, 0); Ub = (j >= i).astype(np.float64)
    I1 = np.broadcast_to(i + 1.0, (128, 128)); I2 = np.broadcast_to(128.0 - i, (128, 128))
    cols = np.concatenate([127.0 - j, j, np.full((128, 1), 128.0), np.zeros((128, 1))], axis=1)
    rconst = np.concatenate([Rf, Uf, Rb, Ub, I1, I2, cols], axis=1).astype(f32)
    in_maps = []
    for core in range(8):
        b, g = core // 4, core % 4
        dsl = slice(512 * g, 512 * g + 512)
        m = {}
        m["xT"] = np.ascontiguousarray(x[b].T[dsl, :])
        m["cT"] = np.ascontiguousarray(c[b].reshape(16, 128).T)
        m["wada"] = np.ascontiguousarray(np.concatenate([w_ada[:, :, dsl], w_ada[:, :, 2048:4096][:, :, dsl],
                                                         w_ada[:, :, 4096:6144][:, :, dsl]], axis=2))
        ba = np.concatenate([b_ada[:, dsl], b_ada[:, 2048:4096][:, dsl], b_ada[:, 4096:6144][:, dsl]], axis=1)
        m["bada"] = np.ascontiguousarray(ba.reshape(NL, 12, 128).transpose(2, 0, 1).reshape(128, NL * 12))
        m["gain"] = np.ascontiguousarray(norm_gain[:, dsl].reshape(NL, 4, 128).transpose(2, 0, 1).reshape(128, NL * 4))
        m["fgain"] = np.ascontiguousarray(final_gain[dsl].reshape(4, 128).T)
        cols_in = []
        for hh in (2 * g, 2 * g + 1):
            for base in (0, 1024, 2048, 3072):
                cols_in.append(np.arange(base + 128 * hh, base + 128 * hh + 128))
        cols_in.append(np.arange(4096 + 128 * g, 4096 + 128 * g + 128))
        cols_in.append(np.arange(4608 + 128 * g, 4608 + 128 * g + 128))
        cols_in.append(np.arange(5120 + 256 * g, 5120 + 256 * g + 256))
        cols_in.append(np.arange(6144 + 256 * g, 6144 + 256 * g + 256))
        cols_in = np.concatenate(cols_in)
        m["win"] = np.ascontiguousarray(w_in[:, :, cols_in])
        rows = []
        for g2 in range(4):
            rows.append(np.arange(128 * (2 * g2), 128 * (2 * g2) + 128))
            rows.append(np.arange(128 * (2 * g2 + 1), 128 * (2 * g2 + 1) + 128))
            rows.append(np.arange(1024 + 256 * g2, 1024 + 256 * g2 + 256))
        rows = np.concatenate(rows)
        m["wout"] = np.ascontiguousarray(w_out[:, :, dsl])
        dl = np.stack([dlf[:, g], dlb[:, g]], axis=1).reshape(1, NL * 2)
        m["dlog"] = np.ascontiguousarray(np.broadcast_to(dl, (128, NL * 2))).astype(f32)
        m["ident"] = ident
        am = []
        for hh in (2 * g, 2 * g + 1):
            slope = 2.0 ** (-(hh + 1.0))
            for d in PATTERNS:
                am.append(np.where(rel <= 64, np.exp(-slope * d * rel), 0.0))
        m["amask"] = np.ascontiguousarray(np.concatenate(am, axis=1)).astype(f32)
        m["rconst"] = rconst
        in_maps.append(m)
    return in_maps


def assemble(results):
    out = np.empty((2, T, D), np.float32)
    for core in range(8):
        b, g = core // 4, core % 4
        out[b, :, 512 * g:512 * g + 512] = results[core]["outT"].T
    return out


_NC_CACHE = {}


def kernel(x, c, norm_gain, w_ada, b_ada, w_in, w_out, ret_decay_logit_f, ret_decay_logit_b, final_gain):
    in_maps = prep_inputs(x, c, norm_gain, w_ada, b_ada, w_in, w_out, ret_decay_logit_f, ret_decay_logit_b, final_gain)
    nc = build()
    res = run_bass_kernel_spmd(nc, in_maps, core_ids=list(range(8)))
    return assemble(res.results)
```

```python
import numpy as np
import ml_dtypes
from contextlib import ExitStack
import concourse.bass as bass
import concourse.mybir as mybir
from concourse.bass_utils import run_bass_kernel_spmd

F32 = mybir.dt.float32
BF16 = mybir.dt.bfloat16
AF = mybir.ActivationFunctionType
ALU = mybir.AluOpType

D = 2048
T = 4096
NL = 4
TT = 512
NTT = T // TT
EPS = 1e-6
PATTERNS = (1, 4, 16)
ENGS = ("pe", "act", "dve", "pool", "sp")


def ssl(start, n, step):
    return slice(start, start + (n - 1) * step + 1, step)
NRING = 8
RECIP = "reciprocal"


class Sched:
    def __init__(self, nc, es):
        self.nc, self.es = nc, es
        self.q = {e: [] for e in ENGS}
        self.sems, self.cnt = {}, {}
        self.waited = {e: {} for e in ENGS}
        self.lastw, self.readers = {}, {}
        self.ring_idx = {"sp": 0, "pool": 0, "act": 0}
        self.pend_r, self.pend_w = set(), set()

    def sem(self, key):
        if key not in self.sems:
            self.sems[key] = self.es.enter_context(self.nc.semaphore("sem_" + str(key)))
            self.cnt[key] = 0
        return self.sems[key]

    def _deps(self, eng, reads, writes):
        evs = []
        for k in list(reads) + list(writes):
            if eng != "pe" and (k in self.pend_r or k in self.pend_w):
                raise RuntimeError(f"dependency on unsignalled PE op for key {k}")
        for k in reads:
            w = self.lastw.get(k)
            if w:
                evs.append(w)
        for k in writes:
            w = self.lastw.get(k)
            if w:
                evs.append(w)
            for sk, v in self.readers.get(k, {}).items():
                evs.append((sk, v))
        return evs

    def _waits(self, eng, evs):
        out = []
        for sk, v in evs:
            if eng == "pe" and sk == "pe":
                continue
            if self.waited[eng].get(sk, 0) >= v:
                continue
            self.waited[eng][sk] = v
            out.append((sk, v))
        return out

    def _register(self, ev, reads, writes):
        for k in reads:
            self.readers.setdefault(k, {})[ev[0]] = ev[1]
        for k in writes:
            self.lastw[k] = ev
            self.readers[k] = {}

    @staticmethod
    def _is_psum(k):
        return k == "pm" or k == "pt" or (isinstance(k, tuple) and k[0] == "pb")

    def op(self, eng, name, args=(), kw=None, reads=(), writes=(), sig=True, extra=()):
        kw = kw or {}
        self.sem(eng)
        writes = list(writes) + [k for k in reads if self._is_psum(k)]
        reads = [k for k in reads if not self._is_psum(k)]
        evs = self._deps(eng, reads, writes) + list(extra)
        waits = self._waits(eng, evs)
        if sig:
            self.cnt[eng] += 1
            ev = (eng, self.cnt[eng])
            if eng == "pe":
                reads = set(reads) | self.pend_r
                writes = set(writes) | self.pend_w
                self.pend_r, self.pend_w = set(), set()
            self._register(ev, reads, writes)
        else:
            assert eng == "pe"
            ev = None
            self.pend_r |= set(reads)
            self.pend_w |= set(writes)
        self.q[eng].append((waits, name, args, kw, ("eng", eng) if sig else None))
        return ev

    def dma(self, qeng, out, in_, reads=(), writes=(), extra=(), **kw):
        ring = self.ring_idx[qeng]
        self.ring_idx[qeng] += 1
        sk = f"d{qeng}{ring % NRING}"
        self.sem(sk)
        evs = self._deps(qeng, reads, writes) + list(extra)
        if self.cnt[sk] > 0:
            evs.append((sk, self.cnt[sk]))
        waits = self._waits(qeng, evs)
        self.cnt[sk] += 16
        ev = (sk, self.cnt[sk])
        self._register(ev, reads, writes)
        kw2 = dict(out=out, in_=in_)
        kw2.update(kw)
        self.q[qeng].append((waits, "dma_start", (), kw2, ("dma", sk)))
        return ev

    def cc(self, ins, outs, groups, reads=(), writes=()):
        self.sem("cc")
        evs = self._deps("pool", reads, writes)
        waits = self._waits("pool", evs)
        self.cnt["cc"] += 1
        ev = ("cc", self.cnt["cc"])
        self._register(ev, reads, writes)
        kw = dict(replica_groups=groups, ins=ins, outs=outs)
        self.q["pool"].append((waits, "collective_compute", ("AllGather", ALU.bypass), kw, ("cc", "cc")))
        return ev

    def barrier(self):
        evs = [(k, v) for k, v in self.cnt.items() if v > 0]
        assert not self.pend_r and not self.pend_w
        for e in ENGS:
            self.wait_only(e, [ev for ev in evs if ev[0] != e])

    def wait_only(self, eng, evs):
        waits = self._waits(eng, evs)
        self.q[eng].append((waits, None, (), {}, None))

    def emit(self):
        nc = self.nc
        block = self.es.enter_context(nc.Block())

        def mk(engname):
            def f(e):
                for waits, name, args, kw, sig in self.q[engname]:
                    for sk, v in waits:
                        e.wait_ge(self.sems[sk], v)
                    if name is None:
                        continue
                    ins = getattr(e, name)(*args, **kw)
                    if sig is None:
                        continue
                    if sig[0] == "eng":
                        ins.then_inc(self.sems[sig[1]], 1)
                    elif sig[0] == "dma":
                        ins.then_inc(self.sems[sig[1]], 16)
                    else:
                        ins.then_inc(self.sems[sig[1]])
            return f

        block.tensor(mk("pe"))
        block.scalar(mk("act"))
        block.vector(mk("dve"))
        block.gpsimd(mk("pool"))
        block.sync(mk("sp"))


def build(n_layers=NL, mix=True, dbg=False, stop=None):
    nc = bass.Bass("TRN2", target_bir_lowering=False)
    L = n_layers

    def din(name, shape, dt=F32):
        return nc.dram_tensor(name, shape, dt, kind="ExternalInput").ap()

    xT_d = din("xT", [512, T])
    cT_d = din("cT", [128, 16])
    wada_d = din("wada", [L, D, 1536])
    bada_d = din("bada", [128, NL * 12])
    gain_d = din("gain", [128, NL * 4])
    fgain_d = din("fgain", [128, 4])
    win_d = din("win", [L, D, 1792])
    wout_d = din("wout", [L, D, 512])
    dlog_d = din("dlog", [128, NL * 2])
    ident_d = din("ident", [128, 128])
    amask_d = din("amask", [128, 6 * 256])
    rconst_d = din("rconst", [128, 6 * 128 + 4])
    out_d = nc.dram_tensor("outT", [512, T], F32, kind="ExternalOutput").ap()
    dbg_d = {}
    if dbg:
        dbg_d["h"] = nc.dram_tensor("dbg_h", [D, T], BF16, kind="ExternalOutput").ap()
        dbg_d["y"] = nc.dram_tensor("dbg_y", [D, T], BF16, kind="ExternalOutput").ap()
        dbg_d["mods"] = nc.dram_tensor("dbg_mods", [128, 12], F32, kind="ExternalOutput").ap()

    ag1_in = nc.dram_tensor("ag1_in", [1, T], F32).ap()
    ag1_out = nc.dram_tensor("ag1_out", [4, T], F32).ap()
    ag2_in = nc.dram_tensor("ag2_in", [NTT, 512, TT], BF16).ap()
    ag2_out = nc.dram_tensor("ag2_out", [NTT, D, TT], BF16).ap()
    ag3a_in = nc.dram_tensor("ag3a_in", [NTT, 256, TT], BF16).ap()
    ag3a_out = nc.dram_tensor("ag3a_out", [NTT, 1024, TT], BF16).ap()
    ag3r_in = nc.dram_tensor("ag3r_in", [NTT, 256, TT], BF16).ap()
    ag3r_out = nc.dram_tensor("ag3r_out", [NTT, 1024, TT], BF16).ap()
    GROUPS = [[0, 1, 2, 3], [4, 5, 6, 7]]

    with ExitStack() as es:
        def sb(name, shape, dt):
            return es.enter_context(nc.sbuf_tensor("sb_" + name, shape, dt))

        def ps(name, shape, dt):
            return es.enter_context(nc.psum_tensor("ps_" + name, shape, dt))

        X = sb("X", [128, 4, T], F32)
        HY = sb("HY", [128, 2, 16, TT], BF16)
        WB = sb("WB", [128, 16 * 512], BF16)
        PO = sb("PO", [128, 4, T], BF16)
        MT = sb("MT", [128, 16384], BF16)
        ident = sb("ident", [128, 128], BF16)
        ones_bf = sb("ones_bf", [128, 128], BF16)
        ones_f = sb("ones_f", [128, 128], F32)
        amask = sb("amask", [128, 6, 256], F32)
        rconst = sb("rconst", [128, 6 * 128 + 4], F32)
        cT = sb("cT", [128, 16], F32)
        cA = sb("cA", [128, 16], BF16)
        bada = sb("bada", [128, NL * 12], F32)
        gain = sb("gain", [128, NL * 4], F32)
        fgain = sb("fgain", [128, 4], F32)
        dlog = sb("dlog", [128, NL * 2], F32)
        lg = sb("lg", [128, NL * 2], F32)
        modA = sb("modA", [128, NL * 4], F32)
        modB = sb("modB", [128, NL * 4], F32)
        modG = sb("modG", [128, NL * 4], F32)
        modrow = sb("modrow", [1, 1536], F32)
        rM = sb("rM", [128, 128], F32)
        rtmp = sb("rtmp", [128, 128], F32)
        rxi = sb("rxi", [128, 2, 128], F32)
        rcol = sb("rcol", [128, 4], F32)
        tA = sb("tA", [128, TT], F32)
        rstd = sb("rstd", [128, TT], F32)
        ssq_st = sb("ssq_st", [1, TT], F32)
        ssq4 = sb("ssq4", [4, TT], F32)

        PB = [ps(f"pb{i}", [128, 512], F32) for i in range(6)]
        PT = ps("pt", [128, 8, 128], BF16)
        PM = ps("pm", [128, 512], F32)

        K = Sched(nc, es)

        K.dma("pool", ident[:], ident_d, writes=["ident"])
        K.dma("sp", amask[:], amask_d.rearrange("p (a b) -> p a b", b=256), writes=["amask"])
        K.dma("sp", rconst[:], rconst_d, writes=["rconst"])
        K.dma("sp", cT[:], cT_d, writes=["cT"])
        K.dma("sp", bada[:], bada_d, writes=["bada"])
        K.dma("sp", gain[:], gain_d, writes=["gain"])
        K.dma("sp", fgain[:], fgain_d, writes=["fgain"])
        K.dma("sp", dlog[:], dlog_d, writes=["dlog"])
        for c in range(4):
            K.dma("sp", X[:, c, :], xT_d[c * 128:(c + 1) * 128, :], writes=[("x", c, tt) for tt in range(NTT)])
        K.op("dve", "memset", (ones_bf[:], 1.0), writes=["ones_bf"])
        K.op("dve", "memset", (ones_f[:], 1.0), writes=["ones_f"])
        K.op("act", "activation", (), dict(out=cA[:], in_=cT[:], func=AF.Silu), reads=["cT"], writes=["cA"])
        K.op("act", "activation", (), dict(out=lg[:], in_=dlog[:], func=AF.Exp, scale=-1.0), reads=["dlog"], writes=["lg"])
        K.op("act", "activation", (), dict(out=lg[:], in_=lg[:], func=AF.Ln, bias=1.0), reads=["lg"], writes=["lg"])
        K.op("dve", "tensor_scalar", (lg[:], lg[:], -1.0, None, ALU.mult), reads=["lg"], writes=["lg"])

        SQD = float(np.sqrt(D))

        def mods_load(l2, grp, stage):
            for q4 in range(4):
                K.dma("pool", stage[:, q4 * 4:(q4 + 1) * 4, :],
                      wada_d[l2, q4 * 512:(q4 + 1) * 512, grp * 512:(grp + 1) * 512].rearrange("(k p) n -> p k n", p=128),
                      writes=[("hy", 1)])

        def mods_mm(l2, grp, stage):
            for kc in range(16):
                K.op("pe", "matmul", (PM[0:1, :], cA[:, kc:kc + 1], stage[:, kc, :]),
                     dict(start=(kc == 0), stop=(kc == 15)), reads=[("hy", 1), "cA"], writes=["pm"], sig=(kc == 15))
            K.op("act", "activation", (), dict(out=modrow[0:1, grp * 512:(grp + 1) * 512], in_=PM[0:1, :], func=AF.Copy),
                 reads=["pm"], writes=["modrow"])

        def mods_finish(l2):
            for j in range(12):
                K.op("pe", "matmul", (PM[:, j:j + 1], modrow[0:1, j * 128:(j + 1) * 128], ones_f[0:1, 0:1]),
                     dict(start=True, stop=True), reads=["modrow", "ones_f"], writes=["pm"], sig=(j == 11))
            sl4 = slice(l2 * 4, l2 * 4 + 4)
            K.op("dve", "tensor_tensor", (modB[:, sl4], PM[:, 0:4], bada[:, l2 * 12:l2 * 12 + 4], ALU.add),
                 reads=["pm", "bada"], writes=[("modB", l2)])
            K.op("dve", "tensor_tensor", (modA[:, sl4], PM[:, 4:8], bada[:, l2 * 12 + 4:l2 * 12 + 8], ALU.add),
                 reads=["pm", "bada"], writes=[("modA", l2)])
            K.op("dve", "scalar_tensor_tensor", (modA[:, sl4], modA[:, sl4], 1.0, gain[:, sl4], ALU.add, ALU.mult),
                 reads=[("modA", l2), "gain"], writes=[("modA", l2)])
            K.op("dve", "tensor_scalar", (modA[:, sl4], modA[:, sl4], SQD, None, ALU.mult), reads=[("modA", l2)], writes=[("modA", l2)])
            K.op("dve", "tensor_tensor", (modG[:, sl4], PM[:, 8:12], bada[:, l2 * 12 + 8:l2 * 12 + 12], ALU.add),
                 reads=["pm", "bada"], writes=[("modG", l2)])

        STG = HY[:, 1, :, :]
        for grp in range(3):
            mods_load(0, grp, STG)
            mods_mm(0, grp, STG)
        mods_finish(0)
        K.op("dve", "tensor_scalar", (fgain[:], fgain[:], SQD, None, ALU.mult), reads=["fgain"], writes=["fgain"])
        if dbg:
            K.op("dve", "tensor_copy", (tA[:, 0:4], modB[:, 0:4]), reads=[("modB", 0)], writes=["tA"])
            K.op("dve", "tensor_copy", (tA[:, 4:8], modA[:, 0:4]), reads=[("modA", 0)], writes=["tA"])
            K.op("dve", "tensor_copy", (tA[:, 8:12], modG[:, 0:4]), reads=[("modG", 0)], writes=["tA"])
            K.dma("sp", dbg_d["mods"], tA[:, 0:12], reads=["tA"])

        def finish_raw():
            outs = []
            for c in range(4):
                outs.append(K.dma("sp", out_d[c * 128:(c + 1) * 128, :], X[:, c, :], reads=[("x", c, tt) for tt in range(NTT)]))
            K.wait_only("sp", outs)
            K.emit()

        if stop == "pro":
            finish_raw()
            return nc

        def norm_stats():
            for tt in range(NTT):
                tsl = slice(tt * TT, (tt + 1) * TT)
                hs = HY[:, tt % 2, 0:4, :]
                for c in range(4):
                    K.op("act", "activation", (), dict(out=hs[:, c, :], in_=X[:, c, tsl], func=AF.Square),
                         reads=[("x", c, tt)], writes=[("hy", tt % 2)])
                for c in range(4):
                    K.op("pe", "matmul", (PM[0:1, :], ones_bf[:, 0:1], hs[:, c, :]), dict(start=(c == 0), stop=(c == 3)),
                         reads=[("hy", tt % 2), "ones_bf"], writes=["pm"], sig=(c == 3))
                K.op("act", "activation", (), dict(out=ssq_st[0:1, :], in_=PM[0:1, :], func=AF.Copy),
                     reads=["pm"], writes=["ssq_st"])
                K.dma("sp", ag1_in[0:1, tsl], ssq_st[0:1, :], reads=["ssq_st"], writes=["ag1_in"])
            K.cc([ag1_in], [ag1_out], GROUPS, reads=["ag1_in"], writes=["ag1_out"])
            for tt in range(NTT):
                rstd_tile(tt)

        rstd_all = MT.ap()[:, 0:8192].bitcast(F32)

        def rstd_tile(tt):
            tsl = slice(tt * TT, (tt + 1) * TT)
            rstd = rstd_all[:, tsl]
            K.dma("sp", ssq4[0:4, :], ag1_out[0:4, tsl], reads=["ag1_out"], writes=["ssq4"])
            K.op("pe", "matmul", (PM[:, :], ones_f[0:4, :], ssq4[0:4, :]), dict(start=True, stop=True),
                 reads=["ssq4", "ones_f"], writes=["pm"])
            K.op("act", "activation", (), dict(out=rstd, in_=PM[:, :], func=AF.Sqrt, bias=epsc[:, 0:1]),
                 reads=["pm", "epsc"], writes=[("rstd", tt)])
            K.op("dve", RECIP, (rstd, rstd), reads=[("rstd", tt)], writes=[("rstd", tt)])

        epsc = sb("epsc", [128, 2], F32)
        K.op("dve", "memset", (epsc[:, 0:1], D * EPS), writes=["epsc"])
        K.op("dve", "memset", (epsc[:, 1:2], 256 * EPS), writes=["epsc"])

        def load_tile(src, tt, slot):
            if src is ag2_out:
                parts = [(ag2_out, "ag2_out", 0, 16)]
            else:
                parts = [(ag3a_out, "ag3a_out", 0, 8), (ag3r_out, "ag3r_out", 8, 8)]
            for sap, skey, k0, nk in parts:
                v = sap[tt].rearrange("(k p) t -> p k t", p=128)
                for q4 in range(nk // 4):
                    K.dma("sp", HY[:, slot, k0 + q4 * 4:k0 + (q4 + 1) * 4, :], v[:, q4 * 4:(q4 + 1) * 4, :],
                          reads=[(skey, tt)], writes=[("hy", slot)])

        def load_w(src2d, ncols, dst=None, key="W", extra=()):
            base = WB.ap() if dst is None else dst
            Wv = base[:, 0:16 * ncols].rearrange("p (k n) -> p k n", k=16)
            for q4 in range(4):
                K.dma("pool", Wv[:, q4 * 4:(q4 + 1) * 4, :],
                      src2d[q4 * 512:(q4 + 1) * 512, :].rearrange("(k p) n -> p k n", p=128), writes=[key], extra=extra)
            return Wv

        pbi = [0]

        def next_pb(n=2):
            i = pbi[0] % n
            pbi[0] += 1
            return i

        def proj_pass(l, col0, nch, evac, Wv=None, post_tile=None):
            if Wv is None:
                Wv = load_w(win_d[l, :, col0 * 128:(col0 + nch) * 128], nch * 128)
            for tt in range(NTT):
                slot = tt % 2
                load_tile(ag2_out, tt, slot)
                for ch in range(nch):
                    b = next_pb()
                    for kc in range(16):
                        K.op("pe", "matmul", (PB[b][:, :], Wv[:, kc, ch * 128:(ch + 1) * 128], HY[:, slot, kc, :]),
                             dict(start=(kc == 0), stop=(kc == 15)), reads=["W", ("hy", slot)], writes=[("pb", b)],
                             sig=(kc == 15))
                    evac(ch, tt, PB[b][:, :], ("pb", b))
                if post_tile is not None:
                    post_tile(tt)


        WBa = HY[:, 0, :, :].rearrange("p k t -> p (k t)")
        MTa = MT.ap()
        acc_o = MTa[:, 0:8192].bitcast(F32)
        acc_d = MTa[:, 8192:16384].bitcast(F32)
        VT = WBa[:, 0:4096].rearrange("p (i e) -> p i e", e=128)
        NSL = 4
        Es = [WBa[:, 4096 + 512 * i:4096 + 512 * (i + 1)].bitcast(F32) for i in range(NSL)]
        Ps = [WBa[:, 6144 + 256 * i:6144 + 256 * (i + 1)] for i in range(NSL)]
        SCALE = 128.0 ** -0.5
        PT2 = PM.ap().bitcast(BF16).rearrange("p (i e) -> p i e", e=128)
        PTS = [(PT, "pt"), (PT2, "pm")]
        PT32 = PT.ap().rearrange("p i e -> p (i e)").bitcast(F32)
        SBK = [(PB[0], ("pb", 0)), (PB[1], ("pb", 1)), (PT32, "pt"), (PM, "pm")]

        ya_stage = MTa[:, 8192:16384].rearrange("p (n t) -> p n t", t=1024)

        def attention_head(l, hh, Wv_in):
            def evac(ch, tt, pap, bk):
                tsl = slice(tt * TT, (tt + 1) * TT)
                if ch == 0:
                    K.op("act", "activation", (), dict(out=PO[:, 0, tsl], in_=pap, func=AF.Copy, scale=SCALE),
                         reads=[bk], writes=[("po", 0)])
                elif ch == 3:
                    K.op("act", "activation", (), dict(out=PO[:, 3, tsl], in_=pap, func=AF.Silu),
                         reads=[bk], writes=[("po", 3)])
                else:
                    K.op("dve", "tensor_copy", (PO[:, ch, tsl], pap), reads=[bk], writes=[("po", ch)])
            proj_pass(l, hh * 4, 4, evac, Wv=Wv_in)
            K.barrier()
            ncol0 = (hh + 1) * 4
            Wv_next = load_w(win_d[l, :, ncol0 * 128:(ncol0 + 4) * 128], 512)
            if l + 1 < L:
                mods_load(l + 1, hh, STG)
            for pi, d in enumerate(PATTERNS):
                Wm = amask[:, hh * 3 + pi, :]
                Ls = T // d
                nkt = Ls // 128
                for idx in range(32):
                    r, m = idx // nkt, idx % nkt
                    tok0 = r + d * 128 * m
                    half = (idx // 4) % 2
                    PTb, ptk = PTS[half]
                    K.op("pe", "transpose", (PTb[:, idx % 4, :], PO[:, 2, ssl(tok0, 128, d)], ident[:]),
                         reads=[("po", 2), "ident"], writes=[ptk], sig=(idx % 4 == 3))
                    if idx % 4 == 3:
                        if half == 0:
                            K.op("act", "activation", (), dict(out=VT[:, idx - 3:idx + 1, :], in_=PTb[:, 0:4, :], func=AF.Copy),
                                 reads=[ptk], writes=[("vt", idx // 4)])
                        else:
                            K.op("dve", "tensor_copy", (VT[:, idx - 3:idx + 1, :], PTb[:, 0:4, :]),
                                 reads=[ptk], writes=[("vt", idx // 4)])
                tiles = [(r, m) for r in range(d) for m in range(nkt)]

                def geom(r, m):
                    c_lo = 64 if m == 0 else 0
                    c_hi = 192 if m == nkt - 1 else 256
                    return c_lo, c_hi

                def emit_S(i):
                    r, m = tiles[i]
                    c_lo, c_hi = geom(r, m)
                    nq = c_hi - c_lo
                    tq0 = r + d * (128 * m - 64 + c_lo)
                    kt0 = r + d * 128 * m
                    sbk = i % NSL
                    Sb_, Sk_ = SBK[sbk]
                    K.op("pe", "matmul", (Sb_[:, 0:nq], PO[:, 1, ssl(kt0, 128, d)], PO[:, 0, ssl(tq0, nq, d)]),
                         dict(start=True, stop=True), reads=[("po", 0), ("po", 1)], writes=[Sk_])

                def emit_EP(i):
                    r, m = tiles[i]
                    c_lo, c_hi = geom(r, m)
                    nq = c_hi - c_lo
                    sbk = i % NSL
                    Sb_, Sk_ = SBK[sbk]
                    K.op("act", "activation", (), dict(out=Es[sbk][:, 0:nq], in_=Sb_[:, 0:nq], func=AF.Exp),
                         reads=[Sk_], writes=[("E", sbk)])
                    K.op("dve", "tensor_tensor", (Ps[sbk][:, c_lo:c_hi], Es[sbk][:, 0:nq], Wm[:, c_lo:c_hi], ALU.mult),
                         reads=[("E", sbk), "amask"], writes=[("P", sbk)])

                def emit_PV(i):
                    r, m = tiles[i]
                    c_lo, c_hi = geom(r, m)
                    sbk = i % NSL
                    vt = VT[:, r * nkt + m, :]
                    vtk = ("vt", (r * nkt + m) // 4)
                    bka = (m // 4) % 2
                    ca = (m % 4) * 128
                    for which, lhs, base in (("o", vt, 2), ("d", ones_bf[:, :], 4)):
                        K.op("pe", "matmul", (PB[base + bka][:, ca + c_lo:ca + 128], lhs, Ps[sbk][:, c_lo:128]),
                             dict(start=(m == 0), stop=True), reads=[("P", sbk), vtk, "ones_bf"],
                             writes=[("pb", base + bka)], sig=(which == "d"))
                    bkb = ((m + 1) // 4) % 2
                    cb = ((m + 1) % 4) * 128
                    for which, lhs, base in (("o", vt, 2), ("d", ones_bf[:, :], 4)):
                        K.op("pe", "matmul", (PB[base + bkb][:, cb:cb + c_hi - 128], lhs, Ps[sbk][:, 128:c_hi]),
                             dict(start=True, stop=(m == nkt - 1)), reads=[("P", sbk), vtk, "ones_bf"],
                             writes=[("pb", base + bkb)], sig=(which == "d"))
                    groups = []
                    if m % 4 == 3:
                        groups.append(m // 4)
                    if m == nkt - 1:
                        groups.append(nkt // 4)
                    for k in groups:
                        jmax = min(4 * k + 3, nkt)
                        lo = 64 if k == 0 else 0
                        hi = (jmax % 4) * 128 + (64 if jmax == nkt else 128)
                        sub0 = 128 * 4 * k - 64 + lo
                        n = hi - lo
                        t0 = r + d * sub0
                        bk = k % 2
                        for acc, base, key in ((acc_o, 2, "acco"), (acc_d, 4, "accd")):
                            dst = acc[:, ssl(t0, n, d)]
                            if pi == 0:
                                K.op("dve", "tensor_copy", (dst, PB[base + bk][:, lo:hi]), reads=[("pb", base + bk)], writes=[key])
                            else:
                                K.op("dve", "tensor_tensor", (dst, dst, PB[base + bk][:, lo:hi], ALU.add),
                                     reads=[("pb", base + bk), key], writes=[key])

                LA = NSL - 1
                for j in range(min(LA, len(tiles))):
                    emit_S(j)
                for i in range(len(tiles)):
                    emit_EP(i)
                    if i + LA < len(tiles):
                        emit_S(i + LA)
                    emit_PV(i)
            if l + 1 < L:
                mods_mm(l + 1, hh, STG)
            allev = [(e, K.cnt[e]) for e in ("pe", "act", "dve", "pool") if K.cnt.get(e, 0) > 0]
            K.wait_only("sp", allev)
            for tt in range(NTT):
                tsl = slice(tt * TT, (tt + 1) * TT)
                K.op("dve", "reciprocal", (acc_d[:, tsl], acc_d[:, tsl]), reads=["accd"], writes=[("accd", tt)])
                K.op("pool", "tensor_tensor", (acc_o[:, tsl], acc_o[:, tsl], acc_d[:, tsl], ALU.mult), reads=["acco", ("accd", tt)], writes=[("acco", tt)])
                K.op("pool", "tensor_tensor", (ya_stage[:, tt, 0:TT], acc_o[:, tsl], PO[:, 3, tsl], ALU.mult),
                     reads=[("acco", tt), ("accd", tt), ("po", 3)], writes=[("yast", tt)])
            K.dma("sp", ag3a_in[:, hh * 128:(hh + 1) * 128, :].rearrange("n p t -> p n t"),
                  ya_stage[:, :, 0:TT], reads=[("yast", tt) for tt in range(NTT)], writes=[("ag3a_in", tt) for tt in range(NTT)])
            return Wv_next

        Sb_all = MTa[:, 0:8192].rearrange("p (n e) -> p n e", e=256)
        VTr = MTa[:, 8192:16384].rearrange("p (n e) -> p n e", e=256)
        kzs = [WBa[:, 128 * i:128 * (i + 1)] for i in range(2)]
        Prs = [WBa[:, 256 + 128 * i:256 + 128 * (i + 1)] for i in range(2)]
        qxi = [WBa[:, 512 + 512 * i:512 + 512 * (i + 1)] for i in range(2)]
        Sst = [WBa[:, 1536 + 512 * i:1536 + 512 * (i + 1)].bitcast(F32) for i in range(2)]
        Sfb = [WBa[:, 2560 + 256 * i:2560 + 256 * (i + 1)] for i in range(2)]
        sqs = WBa[:, 3072:4096].rearrange("p (c t) -> p c t", c=2)
        rs = WBa[:, 4096:5120].bitcast(F32)
        RC = 768
        ysr = [MTa[:, 8192 + 1024 * i:8192 + 1024 * (i + 1)].rearrange("p (c t) -> p c t", c=2) for i in range(2)]
        Wo = [None]

        def retention_head(l, Wv_b1):
            lgf = lg[:, 2 * l:2 * l + 1]
            lgb = lg[:, 2 * l + 1:2 * l + 2]
            K.op("act", "activation", (), dict(out=rM[:], in_=rconst[:, 0:128], func=AF.Exp, scale=lgf), reads=["rconst", "lg"], writes=["rM"])
            K.op("dve", "tensor_tensor", (rM[:], rM[:], rconst[:, 128:256], ALU.mult), reads=["rM", "rconst"], writes=["rM"])
            K.op("act", "activation", (), dict(out=rtmp[:], in_=rconst[:, 256:384], func=AF.Exp, scale=lgb), reads=["rconst", "lg"], writes=["rtmp"])
            K.op("dve", "tensor_tensor", (rtmp[:], rtmp[:], rconst[:, 384:512], ALU.mult), reads=["rtmp", "rconst"], writes=["rtmp"])
            K.op("dve", "tensor_tensor", (rM[:], rM[:], rtmp[:], ALU.add), reads=["rM", "rtmp"], writes=["rM"])
            K.op("act", "activation", (), dict(out=rxi[:, 0, :], in_=rconst[:, 512:640], func=AF.Exp, scale=lgf), reads=["rconst", "lg"], writes=["rxi"])
            K.op("act", "activation", (), dict(out=rxi[:, 1, :], in_=rconst[:, 640:768], func=AF.Exp, scale=lgb), reads=["rconst", "lg"], writes=["rxi"])
            K.op("act", "activation", (), dict(out=rcol[:, 0:1], in_=rconst[:, RC:RC + 1], func=AF.Exp, scale=lgf), reads=["rconst", "lg"], writes=["rcol"])
            K.op("act", "activation", (), dict(out=rcol[:, 1:2], in_=rconst[:, RC + 1:RC + 2], func=AF.Exp, scale=lgb), reads=["rconst", "lg"], writes=["rcol"])
            K.op("act", "activation", (), dict(out=rcol[:, 2:3], in_=rconst[:, RC + 2:RC + 3], func=AF.Exp, scale=lgf), reads=["rconst", "lg"], writes=["rcol"])
            K.op("act", "activation", (), dict(out=rcol[:, 3:4], in_=rconst[:, RC + 2:RC + 3], func=AF.Exp, scale=lgb), reads=["rconst", "lg"], writes=["rcol"])

            def evac(ch, tt, pap, bk):
                tsl = slice(tt * TT, (tt + 1) * TT)
                if ch == 1:
                    K.op("act", "activation", (), dict(out=PO[:, 1, tsl], in_=pap, func=AF.Copy, scale=SCALE),
                         reads=[bk], writes=[("po", 1)])
                elif ch == 0:
                    K.op("act", "activation", (), dict(out=PO[:, 0, tsl], in_=pap, func=AF.Copy), reads=[bk], writes=[("po", 0)])
                else:
                    K.op("dve", "tensor_copy", (PO[:, ch, tsl], pap), reads=[bk], writes=[("po", ch)])
            for tt in range(NTT):
                K.cc([ag3a_in[tt]], [ag3a_out[tt]], GROUPS, reads=[("ag3a_in", tt)], writes=[("ag3a_out", tt)])
            proj_pass(l, 8, 4, evac, Wv=Wv_b1)
            K.barrier()
            Wv_b2 = load_w(win_d[l, :, 12 * 128:14 * 128], 256)
            if l + 1 < L:
                mods_load(l + 1, 2, STG)
            K.op("dve", "memset", (Sst[0], 0.0), writes=["Sf"])
            K.op("dve", "memset", (Sst[1], 0.0), writes=["Sb"])
            for n in range(31, -1, -1):
                csl = slice(n * 128, (n + 1) * 128)
                half = n % 2
                PTb, ptk = PTS[half]
                for c2 in range(2):
                    K.op("pe", "transpose", (PTb[:, c2, :], PO[:, 2 + c2, csl], ident[:]),
                         reads=[("po", 2 + c2), "ident"], writes=[ptk], sig=False)
                K.op("pe", "transpose", (PTb[:, 2, :], PO[:, 1, csl], ident[:]),
                     reads=[("po", 1), "ident"], writes=[ptk], sig=True)
                K.op("act", "activation", (), dict(out=VTr[:, n, :].rearrange("p (c e) -> p c e", c=2), in_=PTb[:, 0:2, :], func=AF.Copy),
                     reads=[ptk], writes=[("vtr", n)])
                K.op("dve", "tensor_scalar", (kzs[half], PTb[:, 2, :], rcol[:, 1:2], None, ALU.mult),
                     reads=[ptk, "rcol"], writes=[("kz", half)])
                K.op("act", "activation", (), dict(out=Sb_all[:, n, :], in_=Sst[1], func=AF.Copy), reads=["Sb"], writes=[("sball", n)])
                b = next_pb()
                K.op("pe", "matmul", (PB[b][:, 0:256], kzs[half], VTr[:, n, :]), dict(start=True, stop=True),
                     reads=[("kz", half), ("vtr", n)], writes=[("pb", b)])
                K.op("dve", "scalar_tensor_tensor", (Sst[1], Sst[1], rcol[:, 3:4], PB[b][:, 0:256], ALU.mult, ALU.add),
                     reads=[("pb", b), "Sb", "rcol"], writes=["Sb"])
            for gq in range(NTT):
                tsl = slice(gq * TT, (gq + 1) * TT)
                for ci in range(4):
                    csl = slice(gq * TT + ci * 128, gq * TT + (ci + 1) * 128)
                    K.op("pool", "tensor_tensor", (qxi[0][:, ci * 128:(ci + 1) * 128], PO[:, 0, csl], rxi[:, 0, :], ALU.mult),
                         reads=[("po", 0), "rxi"], writes=["qxi0"])
                    K.op("pool", "tensor_tensor", (qxi[1][:, ci * 128:(ci + 1) * 128], PO[:, 0, csl], rxi[:, 1, :], ALU.mult),
                         reads=[("po", 0), "rxi"], writes=["qxi1"])
                ob = 2 + 2 * (gq % 2)
                for ci in range(4):
                    n = gq * 4 + ci
                    csl = slice(n * 128, (n + 1) * 128)
                    half = n % 2
                    K.op("pe", "transpose", (PT[:, 0, :], PO[:, 1, csl], ident[:]),
                         reads=[("po", 1), "ident"], writes=["pt"], sig=True)
                    K.op("dve", "tensor_scalar", (kzs[half], PT[:, 0, :], rcol[:, 0:1], None, ALU.mult),
                         reads=["pt", "rcol"], writes=[("kz", half)])
                    b = next_pb()
                    K.op("pe", "matmul", (PB[b][:, 0:128], PO[:, 1, csl], PO[:, 0, csl]), dict(start=True, stop=True),
                         reads=[("po", 0), ("po", 1)], writes=[("pb", b)])
                    K.op("dve", "tensor_tensor", (Prs[half], PB[b][:, 0:128], rM[:], ALU.mult), reads=[("pb", b), "rM"], writes=[("Pr", half)])
                    for c2 in range(2):
                        dst = PB[ob + c2][:, ci * 128:(ci + 1) * 128]
                        terms = [(VTr[:, n, c2 * 128:(c2 + 1) * 128], Prs[half], [("vtr", n), ("Pr", half)])]
                        if n > 0:
                            terms.append((Sfb[n % 2][:, c2 * 128:(c2 + 1) * 128], qxi[0][:, ci * 128:(ci + 1) * 128], [("Sfb", n % 2), "qxi0"]))
                        if n < 31:
                            terms.append((Sb_all[:, n, c2 * 128:(c2 + 1) * 128], qxi[1][:, ci * 128:(ci + 1) * 128], [("sball", n), "qxi1"]))
                        for ti, (lhs, rhs, rk) in enumerate(terms):
                            K.op("pe", "matmul", (dst, lhs, rhs), dict(start=(ti == 0), stop=(ti == len(terms) - 1)),
                                 reads=rk, writes=[("pb", ob + c2)], sig=(ti == len(terms) - 1))
                    b = next_pb()
                    K.op("pe", "matmul", (PB[b][:, 0:256], kzs[half], VTr[:, n, :]), dict(start=True, stop=True),
                         reads=[("kz", half), ("vtr", n)], writes=[("pb", b)])
                    K.op("dve", "scalar_tensor_tensor", (Sst[0], Sst[0], rcol[:, 2:3], PB[b][:, 0:256], ALU.mult, ALU.add),
                         reads=[("pb", b), "Sf", "rcol"], writes=["Sf"])
                    K.op("act", "activation", (), dict(out=Sfb[(n + 1) % 2], in_=Sst[0], func=AF.Copy), reads=["Sf"], writes=[("Sfb", (n + 1) % 2)])
                for c2 in range(2):
                    K.op("act", "activation", (), dict(out=sqs[:, c2, :], in_=PB[ob + c2][:, :], func=AF.Square),
                         reads=[("pb", ob + c2)], writes=[("sq", c2)])
                for c2 in range(2):
                    K.op("pe", "matmul", (PM[:, :], ones_bf[:, :], sqs[:, c2, :]), dict(start=(c2 == 0), stop=(c2 == 1)),
                         reads=[("sq", c2), "ones_bf"], writes=["pm"], sig=(c2 == 1))
                K.op("act", "activation", (), dict(out=rs, in_=PM[:, :], func=AF.Sqrt, bias=epsc[:, 1:2]),
                     reads=["pm", "epsc"], writes=["rs"])
                K.op("dve", RECIP, (rs, rs), reads=["rs"], writes=["rs"])
                for c2 in range(2):
                    K.op("dve", "scalar_tensor_tensor", (PO[:, 2 + c2, tsl], PB[ob + c2][:, :], 16.0, rs, ALU.mult, ALU.mult),
                         reads=[("pb", ob + c2), "rs"], writes=[("po", 2 + c2)])
            if l + 1 < L:
                mods_mm(l + 1, 2, STG)
                mods_finish(l + 1)
            K.barrier()
            Wo[0] = load_w(wout_d[l, :, :], 512, dst=MTa, key="Wo")

            def evac2(ch, tt, pap, bk):
                tsl = slice(tt * TT, (tt + 1) * TT)
                K.op("act", "activation", (), dict(out=PO[:, ch, tsl], in_=pap, func=AF.Silu), reads=[bk], writes=[("po", ch)])

            def post(tt):
                tsl = slice(tt * TT, (tt + 1) * TT)
                ys = ysr[tt % 2]
                for c2 in range(2):
                    K.op("dve", "tensor_tensor", (ys[:, c2, :], PO[:, 2 + c2, tsl], PO[:, c2, tsl], ALU.mult),
                         reads=[("po", 2 + c2), ("po", c2)], writes=[("ysr", tt % 2)])
                K.dma("pool", ag3r_in[tt].rearrange("(c p) t -> p c t", p=128), ys, reads=[("ysr", tt % 2)], writes=[("ag3r_in", tt)])
                K.cc([ag3r_in[tt]], [ag3r_out[tt]], GROUPS, reads=[("ag3r_in", tt)], writes=[("ag3r_out", tt)])
            proj_pass(l, 12, 2, evac2, Wv=Wv_b2, post_tile=post)
            K.barrier()

        ystage = HY[:, 0, :, :].rearrange("p k t -> p (k t)")
        for l in range(L):
            A_ = modA[:, l * 4:l * 4 + 4]
            B_ = modB[:, l * 4:l * 4 + 4]
            G_ = modG[:, l * 4:l * 4 + 4]
            W_A0 = load_w(win_d[l, :, 0:512], 512)
            norm_stats()
            if stop == "n1":
                finish_raw()
                return nc
            for tt in range(NTT):
                tsl = slice(tt * TT, (tt + 1) * TT)
                slot = tt % 2
                for c in range(4):
                    K.op("dve", "scalar_tensor_tensor", (tA[:], X[:, c, tsl], A_[:, c:c + 1], rstd_all[:, tsl], ALU.mult, ALU.mult),
                         reads=[("x", c, tt), ("rstd", tt), ("modA", l)], writes=["tA"])
                    K.op("act", "activation", (), dict(out=HY[:, slot, c, :], in_=tA[:], func=AF.Identity, bias=B_[:, c:c + 1]),
                         reads=["tA", ("modB", l)], writes=[("hy", slot)])
                K.dma("sp", ag2_in[tt].rearrange("(c p) t -> p c t", p=128), HY[:, slot, 0:4, :],
                      reads=[("hy", slot)], writes=[("ag2_in", tt)])
                K.cc([ag2_in[tt]], [ag2_out[tt]], GROUPS, reads=[("ag2_in", tt)], writes=[("ag2_out", tt)])
            if dbg and l == 0:
                for tt in range(NTT):
                    load_tile(ag2_out, tt, tt % 2)
                    K.dma("sp", dbg_d["h"].rearrange("(k p) t -> p k t", p=128)[:, :, tt * TT:(tt + 1) * TT], HY[:, tt % 2, :, :],
                          reads=[("hy", tt % 2)])

            if stop == "n2":
                finish_raw()
                return nc
            if mix:
                Wn = attention_head(l, 0, W_A0)
                Wn = attention_head(l, 1, Wn)
                retention_head(l, Wn)
            if dbg and l == 0:
                for tt in range(NTT):
                    load_tile(None, tt, tt % 2)
                    K.dma("sp", dbg_d["y"].rearrange("(k p) t -> p k t", p=128)[:, :, tt * TT:(tt + 1) * TT], HY[:, tt % 2, :, :],
                          reads=[("hy", tt % 2)])

            if stop == "m":
                finish_raw()
                return nc
            Wv = Wo[0]
            for tt in range(NTT):
                tsl = slice(tt * TT, (tt + 1) * TT)
                slot = tt % 2
                load_tile(None, tt, slot)
                for c in range(4):
                    b = next_pb()
                    for kc in range(16):
                        K.op("pe", "matmul", (PB[b][:, :], Wv[:, kc, c * 128:(c + 1) * 128], HY[:, slot, kc, :]),
                             dict(start=(kc == 0), stop=(kc == 15)), reads=["Wo", ("hy", slot)], writes=[("pb", b)],
                             sig=(kc == 15))
                    K.op("dve", "scalar_tensor_tensor", (X[:, c, tsl], PB[b][:, :], G_[:, c:c + 1], X[:, c, tsl], ALU.mult, ALU.add),
                         reads=[("pb", b), ("x", c, tt), ("modG", l)], writes=[("x", c, tt)])

        if stop == "o":
            finish_raw()
            return nc
        norm_stats()
        outs = []
        for tt in range(NTT):
            tsl = slice(tt * TT, (tt + 1) * TT)
            for c in range(4):
                K.op("dve", "scalar_tensor_tensor", (tA[:], X[:, c, tsl], fgain[:, c:c + 1], rstd_all[:, tsl], ALU.mult, ALU.mult),
                     reads=[("x", c, tt), ("rstd", tt), "fgain"], writes=["tA"])
                outs.append(K.dma("sp", out_d[c * 128:(c + 1) * 128, tsl], tA[:], reads=["tA"]))
        K.wait_only("sp", outs)
        K.emit()
    return nc


def _host_consts():
    p = np.arange(128)[:, None]
    c = np.arange(256)[None, :]
    rel = np.abs(c - 64 - p).astype(np.float64)
    return rel


def prep_inputs(x, c, norm_gain, w_ada, b_ada, w_in, w_out, ret_decay_logit_f, ret_decay_logit_b, final_gain, L=NL):
    f32 = np.float32
    x = np.asarray(x, f32); c = np.asarray(c, f32)
    norm_gain = np.asarray(norm_gain, f32); w_ada = np.asarray(w_ada, f32)[:L]; b_ada = np.asarray(b_ada, f32)
    w_in = np.asarray(w_in, f32)[:L]; w_out = np.asarray(w_out, f32)[:L]
    dlf = np.asarray(ret_decay_logit_f, f32); dlb = np.asarray(ret_decay_logit_b, f32)
    final_gain = np.asarray(final_gain, f32)
    rel = _host_consts()
    ident = np.eye(128, dtype=f32)
    j = np.arange(128)[:, None].astype(np.float64)
    i = np.arange(128)[None, :].astype(np.float64)
    Rf = np.maximum(i - j, 0); Uf = (i >= j).astype(np.float64)
    Rb = np.maximum(j - i, 0); Ub = (j >= i).astype(np.float64)
    I1 = np.broadcast_to(i + 1.0, (128, 128)); I2 = np.broadcast_to(128.0 - i, (128, 128))
    cols = np.concatenate([127.0 - j, j, np.full((128, 1), 128.0), np.zeros((128, 1))], axis=1)
    rconst = np.concatenate([Rf, Uf, Rb, Ub, I1, I2, cols], axis=1).astype(f32)
    in_maps = []
    for core in range(8):
        b, g = core // 4, core % 4
        dsl = slice(512 * g, 512 * g + 512)
        m = {}
        m["xT"] = np.ascontiguousarray(x[b].T[dsl, :])
        m["cT"] = np.ascontiguousarray(c[b].reshape(16, 128).T)
        m["wada"] = np.ascontiguousarray(np.concatenate([w_ada[:, :, dsl], w_ada[:, :, 2048:4096][:, :, dsl],
                                                         w_ada[:, :, 4096:6144][:, :, dsl]], axis=2))
        ba = np.concatenate([b_ada[:, dsl], b_ada[:, 2048:4096][:, dsl], b_ada[:, 4096:6144][:, dsl]], axis=1)
        m["bada"] = np.ascontiguousarray(ba.reshape(NL, 12, 128).transpose(2, 0, 1).reshape(128, NL * 12))
        m["gain"] = np.ascontiguousarray(norm_gain[:, dsl].reshape(NL, 4, 128).transpose(2, 0, 1).reshape(128, NL * 4))
        m["fgain"] = np.ascontiguousarray(final_gain[dsl].reshape(4, 128).T)
        cols_in = []
        for hh in (2 * g, 2 * g + 1):
            for base in (0, 1024, 2048, 3072):
                cols_in.append(np.arange(base + 128 * hh, base + 128 * hh + 128))
        cols_in.append(np.arange(4096 + 128 * g, 4096 + 128 * g + 128))
        cols_in.append(np.arange(4608 + 128 * g, 4608 + 128 * g + 128))
        cols_in.append(np.arange(5120 + 256 * g, 5120 + 256 * g + 256))
        cols_in.append(np.arange(6144 + 256 * g, 6144 + 256 * g + 256))
        cols_in = np.concatenate(cols_in)
        m["win"] = np.ascontiguousarray(w_in[:, :, cols_in])
        rows = []
        for g2 in range(4):
            rows.append(np.arange(128 * (2 * g2), 128 * (2 * g2) + 128))
            rows.append(np.arange(128 * (2 * g2 + 1), 128 * (2 * g2 + 1) + 128))
            rows.append(np.arange(1024 + 256 * g2, 1024 + 256 * g2 + 256))
        rows = np.concatenate(rows)
        m["wout"] = np.ascontiguousarray(w_out[:, :, dsl])
        dl = np.stack([dlf[:, g], dlb[:, g]], axis=1).reshape(1, NL * 2)
        m["dlog"] = np.ascontiguousarray(np.broadcast_to(dl, (128, NL * 2))).astype(f32)
        m["ident"] = ident
        am = []
        for hh in (2 * g, 2 * g + 1):
            slope = 2.0 ** (-(hh + 1.0))
            for d in PATTERNS:
                am.append(np.where(rel <= 64, np.exp(-slope * d * rel), 0.0))
        m["amask"] = np.ascontiguousarray(np.concatenate(am, axis=1)).astype(f32)
        m["rconst"] = rconst
        in_maps.append(m)
    return in_maps


def assemble(results):
    out = np.empty((2, T, D), np.float32)
    for core in range(8):
        b, g = core // 4, core % 4
        out[b, :, 512 * g:512 * g + 512] = results[core]["outT"].T
    return out


_NC_CACHE = {}


def kernel(x, c, norm_gain, w_ada, b_ada, w_in, w_out, ret_decay_logit_f, ret_decay_logit_b, final_gain):
    in_maps = prep_inputs(x, c, norm_gain, w_ada, b_ada, w_in, w_out, ret_decay_logit_f, ret_decay_logit_b, final_gain)
    nc = build()
    res = run_bass_kernel_spmd(nc, in_maps, core_ids=list(range(8)))
    return assemble(res.results)
```

```python
import numpy as np
import ml_dtypes
from contextlib import ExitStack
import concourse.bass as bass
import concourse.mybir as mybir
from concourse.bass_utils import run_bass_kernel_spmd

F32 = mybir.dt.float32
BF16 = mybir.dt.bfloat16
AF = mybir.ActivationFunctionType
ALU = mybir.AluOpType

D = 2048
T = 4096
NL = 4
TT = 512
NTT = T // TT
EPS = 1e-6
PATTERNS = (1, 4, 16)
ENGS = ("pe", "act", "dve", "pool", "sp")


def ssl(start, n, step):
    return slice(start, start + (n - 1) * step + 1, step)
NRING = 8
RECIP = "reciprocal"


class Sched:
    def __init__(self, nc, es):
        self.nc, self.es = nc, es
        self.q = {e: [] for e in ENGS}
        self.sems, self.cnt = {}, {}
        self.waited = {e: {} for e in ENGS}
        self.lastw, self.readers = {}, {}
        self.ring_idx = {"sp": 0, "pool": 0, "act": 0}
        self.pend_r, self.pend_w = set(), set()

    def sem(self, key):
        if key not in self.sems:
            self.sems[key] = self.es.enter_context(self.nc.semaphore("sem_" + str(key)))
            self.cnt[key] = 0
        return self.sems[key]

    def _deps(self, eng, reads, writes):
        evs = []
        for k in list(reads) + list(writes):
            if eng != "pe" and (k in self.pend_r or k in self.pend_w):
                raise RuntimeError(f"dependency on unsignalled PE op for key {k}")
        for k in reads:
            w = self.lastw.get(k)
            if w:
                evs.append(w)
        for k in writes:
            w = self.lastw.get(k)
            if w:
                evs.append(w)
            for sk, v in self.readers.get(k, {}).items():
                evs.append((sk, v))
        return evs

    def _waits(self, eng, evs):
        out = []
        for sk, v in evs:
            if eng == "pe" and sk == "pe":
                continue
            if self.waited[eng].get(sk, 0) >= v:
                continue
            self.waited[eng][sk] = v
            out.append((sk, v))
        return out

    def _register(self, ev, reads, writes):
        for k in reads:
            self.readers.setdefault(k, {})[ev[0]] = ev[1]
        for k in writes:
            self.lastw[k] = ev
            self.readers[k] = {}

    @staticmethod
    def _is_psum(k):
        return k == "pm" or k == "pt" or (isinstance(k, tuple) and k[0] == "pb")

    def op(self, eng, name, args=(), kw=None, reads=(), writes=(), sig=True, extra=()):
        kw = kw or {}
        self.sem(eng)
        writes = list(writes) + [k for k in reads if self._is_psum(k)]
        reads = [k for k in reads if not self._is_psum(k)]
        evs = self._deps(eng, reads, writes) + list(extra)
        waits = self._waits(eng, evs)
        if sig:
            self.cnt[eng] += 1
            ev = (eng, self.cnt[eng])
            if eng == "pe":
                reads = set(reads) | self.pend_r
                writes = set(writes) | self.pend_w
                self.pend_r, self.pend_w = set(), set()
            self._register(ev, reads, writes)
        else:
            assert eng == "pe"
            ev = None
            self.pend_r |= set(reads)
            self.pend_w |= set(writes)
        self.q[eng].append((waits, name, args, kw, ("eng", eng) if sig else None))
        return ev

    def dma(self, qeng, out, in_, reads=(), writes=(), extra=(), **kw):
        ring = self.ring_idx[qeng]
        self.ring_idx[qeng] += 1
        sk = f"d{qeng}{ring % NRING}"
        self.sem(sk)
        evs = self._deps(qeng, reads, writes) + list(extra)
        if self.cnt[sk] > 0:
            evs.append((sk, self.cnt[sk]))
        waits = self._waits(qeng, evs)
        self.cnt[sk] += 16
        ev = (sk, self.cnt[sk])
        self._register(ev, reads, writes)
        kw2 = dict(out=out, in_=in_)
        kw2.update(kw)
        self.q[qeng].append((waits, "dma_start", (), kw2, ("dma", sk)))
        return ev

    def cc(self, ins, outs, groups, reads=(), writes=()):
        self.sem("cc")
        evs = self._deps("pool", reads, writes)
        waits = self._waits("pool", evs)
        self.cnt["cc"] += 1
        ev = ("cc", self.cnt["cc"])
        self._register(ev, reads, writes)
        kw = dict(replica_groups=groups, ins=ins, outs=outs)
        self.q["pool"].append((waits, "collective_compute", ("AllGather", ALU.bypass), kw, ("cc", "cc")))
        return ev

    def barrier(self):
        evs = [(k, v) for k, v in self.cnt.items() if v > 0]
        assert not self.pend_r and not self.pend_w
        for e in ENGS:
            self.wait_only(e, [ev for ev in evs if ev[0] != e])

    def wait_only(self, eng, evs):
        waits = self._waits(eng, evs)
        self.q[eng].append((waits, None, (), {}, None))

    def emit(self):
        nc = self.nc
        block = self.es.enter_context(nc.Block())

        def mk(engname):
            def f(e):
                for waits, name, args, kw, sig in self.q[engname]:
                    for sk, v in waits:
                        e.wait_ge(self.sems[sk], v)
                    if name is None:
                        continue
                    ins = getattr(e, name)(*args, **kw)
                    if sig is None:
                        continue
                    if sig[0] == "eng":
                        ins.then_inc(self.sems[sig[1]], 1)
                    elif sig[0] == "dma":
                        ins.then_inc(self.sems[sig[1]], 16)
                    else:
                        ins.then_inc(self.sems[sig[1]])
            return f

        block.tensor(mk("pe"))
        block.scalar(mk("act"))
        block.vector(mk("dve"))
        block.gpsimd(mk("pool"))
        block.sync(mk("sp"))


def build(n_layers=NL, mix=True, dbg=False, stop=None):
    nc = bass.Bass("TRN2", target_bir_lowering=False)
    L = n_layers

    def din(name, shape, dt=F32):
        return nc.dram_tensor(name, shape, dt, kind="ExternalInput").ap()

    xT_d = din("xT", [512, T])
    cT_d = din("cT", [128, 16])
    wada_d = din("wada", [L, D, 1536])
    bada_d = din("bada", [128, NL * 12])
    gain_d = din("gain", [128, NL * 4])
    fgain_d = din("fgain", [128, 4])
    win_d = din("win", [L, D, 1792])
    wout_d = din("wout", [L, D, 512])
    dlog_d = din("dlog", [128, NL * 2])
    ident_d = din("ident", [128, 128])
    amask_d = din("amask", [128, 6 * 256])
    rconst_d = din("rconst", [128, 6 * 128 + 4])
    out_d = nc.dram_tensor("outT", [512, T], F32, kind="ExternalOutput").ap()
    dbg_d = {}
    if dbg:
        dbg_d["h"] = nc.dram_tensor("dbg_h", [D, T], BF16, kind="ExternalOutput").ap()
        dbg_d["y"] = nc.dram_tensor("dbg_y", [D, T], BF16, kind="ExternalOutput").ap()
        dbg_d["mods"] = nc.dram_tensor("dbg_mods", [128, 12], F32, kind="ExternalOutput").ap()

    ag1_in = nc.dram_tensor("ag1_in", [1, T], F32).ap()
    ag1_out = nc.dram_tensor("ag1_out", [4, T], F32).ap()
    ag2_in = nc.dram_tensor("ag2_in", [NTT, 512, TT], BF16).ap()
    ag2_out = nc.dram_tensor("ag2_out", [NTT, D, TT], BF16).ap()
    ag3a_in = nc.dram_tensor("ag3a_in", [NTT, 256, TT], BF16).ap()
    ag3a_out = nc.dram_tensor("ag3a_out", [NTT, 1024, TT], BF16).ap()
    ag3r_in = nc.dram_tensor("ag3r_in", [NTT, 256, TT], BF16).ap()
    ag3r_out = nc.dram_tensor("ag3r_out", [NTT, 1024, TT], BF16).ap()
    GROUPS = [[0, 1, 2, 3], [4, 5, 6, 7]]

    with ExitStack() as es:
        def sb(name, shape, dt):
            return es.enter_context(nc.sbuf_tensor("sb_" + name, shape, dt))

        def ps(name, shape, dt):
            return es.enter_context(nc.psum_tensor("ps_" + name, shape, dt))

        X = sb("X", [128, 4, T], F32)
        HY = sb("HY", [128, 2, 16, TT], BF16)
        WB = sb("WB", [128, 16 * 512], BF16)
        PO = sb("PO", [128, 4, T], BF16)
        MT = sb("MT", [128, 16384], BF16)
        ident = sb("ident", [128, 128], BF16)
        ones_bf = sb("ones_bf", [128, 128], BF16)
        ones_f = sb("ones_f", [128, 128], F32)
        amask = sb("amask", [128, 6, 256], F32)
        rconst = sb("rconst", [128, 6 * 128 + 4], F32)
        cT = sb("cT", [128, 16], F32)
        cA = sb("cA", [128, 16], BF16)
        bada = sb("bada", [128, NL * 12], F32)
        gain = sb("gain", [128, NL * 4], F32)
        fgain = sb("fgain", [128, 4], F32)
        dlog = sb("dlog", [128, NL * 2], F32)
        lg = sb("lg", [128, NL * 2], F32)
        modA = sb("modA", [128, NL * 4], F32)
        modB = sb("modB", [128, NL * 4], F32)
        modG = sb("modG", [128, NL * 4], F32)
        modrow = sb("modrow", [1, 1536], F32)
        rM = sb("rM", [128, 128], F32)
        rtmp = sb("rtmp", [128, 128], F32)
        rxi = sb("rxi", [128, 2, 128], F32)
        rcol = sb("rcol", [128, 4], F32)
        tA = sb("tA", [128, TT], F32)
        rstd = sb("rstd", [128, TT], F32)
        ssq_st = sb("ssq_st", [1, TT], F32)
        ssq4 = sb("ssq4", [4, TT], F32)

        PB = [ps(f"pb{i}", [128, 512], F32) for i in range(6)]
        PT = ps("pt", [128, 8, 128], BF16)
        PM = ps("pm", [128, 512], F32)

        K = Sched(nc, es)

        K.dma("pool", ident[:], ident_d, writes=["ident"])
        K.dma("sp", amask[:], amask_d.rearrange("p (a b) -> p a b", b=256), writes=["amask"])
        K.dma("sp", rconst[:], rconst_d, writes=["rconst"])
        K.dma("sp", cT[:], cT_d, writes=["cT"])
        K.dma("sp", bada[:], bada_d, writes=["bada"])
        K.dma("sp", gain[:], gain_d, writes=["gain"])
        K.dma("sp", fgain[:], fgain_d, writes=["fgain"])
        K.dma("sp", dlog[:], dlog_d, writes=["dlog"])
        for c in range(4):
            K.dma("sp", X[:, c, :], xT_d[c * 128:(c + 1) * 128, :], writes=[("x", c, tt) for tt in range(NTT)])
        K.op("dve", "memset", (ones_bf[:], 1.0), writes=["ones_bf"])
        K.op("dve", "memset", (ones_f[:], 1.0), writes=["ones_f"])
        K.op("act", "activation", (), dict(out=cA[:], in_=cT[:], func=AF.Silu), reads=["cT"], writes=["cA"])
        K.op("act", "activation", (), dict(out=lg[:], in_=dlog[:], func=AF.Exp, scale=-1.0), reads=["dlog"], writes=["lg"])
        K.op("act", "activation", (), dict(out=lg[:], in_=lg[:], func=AF.Ln, bias=1.0), reads=["lg"], writes=["lg"])
        K.op("dve", "tensor_scalar", (lg[:], lg[:], -1.0, None, ALU.mult), reads=["lg"], writes=["lg"])

        SQD = float(np.sqrt(D))

        def mods_load(l2, grp, stage):
            for q4 in range(4):
                K.dma("pool", stage[:, q4 * 4:(q4 + 1) * 4, :],
                      wada_d[l2, q4 * 512:(q4 + 1) * 512, grp * 512:(grp + 1) * 512].rearrange("(k p) n -> p k n", p=128),
                      writes=[("hy", 1)])

        def mods_mm(l2, grp, stage):
            for kc in range(16):
                K.op("pe", "matmul", (PM[0:1, :], cA[:, kc:kc + 1], stage[:, kc, :]),
                     dict(start=(kc == 0), stop=(kc == 15)), reads=[("hy", 1), "cA"], writes=["pm"], sig=(kc == 15))
            K.op("act", "activation", (), dict(out=modrow[0:1, grp * 512:(grp + 1) * 512], in_=PM[0:1, :], func=AF.Copy),
                 reads=["pm"], writes=["modrow"])

        def mods_finish(l2):
            for j in range(12):
                K.op("pe", "matmul", (PM[:, j:j + 1], modrow[0:1, j * 128:(j + 1) * 128], ones_f[0:1, 0:1]),
                     dict(start=True, stop=True), reads=["modrow", "ones_f"], writes=["pm"], sig=(j == 11))
            sl4 = slice(l2 * 4, l2 * 4 + 4)
            K.op("dve", "tensor_tensor", (modB[:, sl4], PM[:, 0:4], bada[:, l2 * 12:l2 * 12 + 4], ALU.add),
                 reads=["pm", "bada"], writes=[("modB", l2)])
            K.op("dve", "tensor_tensor", (modA[:, sl4], PM[:, 4:8], bada[:, l2 * 12 + 4:l2 * 12 + 8], ALU.add),
                 reads=["pm", "bada"], writes=[("modA", l2)])
            K.op("dve", "scalar_tensor_tensor", (modA[:, sl4], modA[:, sl4], 1.0, gain[:, sl4], ALU.add, ALU.mult),
                 reads=[("modA", l2), "gain"], writes=[("modA", l2)])
            K.op("dve", "tensor_scalar", (modA[:, sl4], modA[:, sl4], SQD, None, ALU.mult), reads=[("modA", l2)], writes=[("modA", l2)])
            K.op("dve", "tensor_tensor", (modG[:, sl4], PM[:, 8:12], bada[:, l2 * 12 + 8:l2 * 12 + 12], ALU.add),
                 reads=["pm", "bada"], writes=[("modG", l2)])

        STG = HY[:, 1, :, :]
        for grp in range(3):
            mods_load(0, grp, STG)
            mods_mm(0, grp, STG)
        mods_finish(0)
        K.op("dve", "tensor_scalar", (fgain[:], fgain[:], SQD, None, ALU.mult), reads=["fgain"], writes=["fgain"])
        if dbg:
            K.op("dve", "tensor_copy", (tA[:, 0:4], modB[:, 0:4]), reads=[("modB", 0)], writes=["tA"])
            K.op("dve", "tensor_copy", (tA[:, 4:8], modA[:, 0:4]), reads=[("modA", 0)], writes=["tA"])
            K.op("dve", "tensor_copy", (tA[:, 8:12], modG[:, 0:4]), reads=[("modG", 0)], writes=["tA"])
            K.dma("sp", dbg_d["mods"], tA[:, 0:12], reads=["tA"])

        def finish_raw():
            outs = []
            for c in range(4):
                outs.append(K.dma("sp", out_d[c * 128:(c + 1) * 128, :], X[:, c, :], reads=[("x", c, tt) for tt in range(NTT)]))
            K.wait_only("sp", outs)
            K.emit()

        if stop == "pro":
            finish_raw()
            return nc

        def norm_stats():
            for tt in range(NTT):
                tsl = slice(tt * TT, (tt + 1) * TT)
                hs = HY[:, tt % 2, 0:4, :]
                for c in range(4):
                    K.op("act", "activation", (), dict(out=hs[:, c, :], in_=X[:, c, tsl], func=AF.Square),
                         reads=[("x", c, tt)], writes=[("hy", tt % 2)])
                for c in range(4):
                    K.op("pe", "matmul", (PM[0:1, :], ones_bf[:, 0:1], hs[:, c, :]), dict(start=(c == 0), stop=(c == 3)),
                         reads=[("hy", tt % 2), "ones_bf"], writes=["pm"], sig=(c == 3))
                K.op("act", "activation", (), dict(out=ssq_st[0:1, :], in_=PM[0:1, :], func=AF.Copy),
                     reads=["pm"], writes=["ssq_st"])
                K.dma("sp", ag1_in[0:1, tsl], ssq_st[0:1, :], reads=["ssq_st"], writes=["ag1_in"])
            K.cc([ag1_in], [ag1_out], GROUPS, reads=["ag1_in"], writes=["ag1_out"])
            for tt in range(NTT):
                rstd_tile(tt)

        rstd_all = MT.ap()[:, 0:8192].bitcast(F32)

        def rstd_tile(tt):
            tsl = slice(tt * TT, (tt + 1) * TT)
            rstd = rstd_all[:, tsl]
            K.dma("sp", ssq4[0:4, :], ag1_out[0:4, tsl], reads=["ag1_out"], writes=["ssq4"])
            K.op("pe", "matmul", (PM[:, :], ones_f[0:4, :], ssq4[0:4, :]), dict(start=True, stop=True),
                 reads=["ssq4", "ones_f"], writes=["pm"])
            K.op("act", "activation", (), dict(out=rstd, in_=PM[:, :], func=AF.Sqrt, bias=epsc[:, 0:1]),
                 reads=["pm", "epsc"], writes=[("rstd", tt)])
            K.op("dve", RECIP, (rstd, rstd), reads=[("rstd", tt)], writes=[("rstd", tt)])

        epsc = sb("epsc", [128, 2], F32)
        K.op("dve", "memset", (epsc[:, 0:1], D * EPS), writes=["epsc"])
        K.op("dve", "memset", (epsc[:, 1:2], 256 * EPS), writes=["epsc"])

        def load_tile(src, tt, slot):
            if src is ag2_out:
                parts = [(ag2_out, "ag2_out", 0, 16)]
            else:
                parts = [(ag3a_out, "ag3a_out", 0, 8), (ag3r_out, "ag3r_out", 8, 8)]
            for sap, skey, k0, nk in parts:
                v = sap[tt].rearrange("(k p) t -> p k t", p=128)
                for q4 in range(nk // 4):
                    K.dma("sp", HY[:, slot, k0 + q4 * 4:k0 + (q4 + 1) * 4, :], v[:, q4 * 4:(q4 + 1) * 4, :],
                          reads=[(skey, tt)], writes=[("hy", slot)])

        def load_w(src2d, ncols, dst=None, key="W", extra=()):
            base = WB.ap() if dst is None else dst
            Wv = base[:, 0:16 * ncols].rearrange("p (k n) -> p k n", k=16)
            for q4 in range(4):
                K.dma("pool", Wv[:, q4 * 4:(q4 + 1) * 4, :],
                      src2d[q4 * 512:(q4 + 1) * 512, :].rearrange("(k p) n -> p k n", p=128), writes=[key], extra=extra)
            return Wv

        pbi = [0]

        def next_pb(n=2):
            i = pbi[0] % n
            pbi[0] += 1
            return i

        def proj_pass(l, col0, nch, evac, Wv=None, post_tile=None):
            if Wv is None:
                Wv = load_w(win_d[l, :, col0 * 128:(col0 + nch) * 128], nch * 128)
            for tt in range(NTT):
                slot = tt % 2
                load_tile(ag2_out, tt, slot)
                for ch in range(nch):
                    b = next_pb()
                    for kc in range(16):
                        K.op("pe", "matmul", (PB[b][:, :], Wv[:, kc, ch * 128:(ch + 1) * 128], HY[:, slot, kc, :]),
                             dict(start=(kc == 0), stop=(kc == 15)), reads=["W", ("hy", slot)], writes=[("pb", b)],
                             sig=(kc == 15))
                    evac(ch, tt, PB[b][:, :], ("pb", b))
                if post_tile is not None:
                    post_tile(tt)


        WBa = HY[:, 0, :, :].rearrange("p k t -> p (k t)")
        MTa = MT.ap()
        acc_o = MTa[:, 0:8192].bitcast(F32)
        acc_d = MTa[:, 8192:16384].bitcast(F32)
        VT = WBa[:, 0:4096].rearrange("p (i e) -> p i e", e=128)
        NSL = 4
        Es = [WBa[:, 4096 + 512 * i:4096 + 512 * (i + 1)].bitcast(F32) for i in range(NSL)]
        Ps = [WBa[:, 6144 + 256 * i:6144 + 256 * (i + 1)] for i in range(NSL)]
        SCALE = 128.0 ** -0.5
        PT2 = PM.ap().bitcast(BF16).rearrange("p (i e) -> p i e", e=128)
        PTS = [(PT, "pt"), (PT2, "pm")]
        PT32 = PT.ap().rearrange("p i e -> p (i e)").bitcast(F32)
        SBK = [(PB[0], ("pb", 0)), (PB[1], ("pb", 1)), (PT32, "pt"), (PM, "pm")]

        ya_stage = MTa[:, 8192:16384].rearrange("p (n t) -> p n t", t=1024)

        def attention_head(l, hh, Wv_in):
            def evac(ch, tt, pap, bk):
                tsl = slice(tt * TT, (tt + 1) * TT)
                if ch == 0:
                    K.op("act", "activation", (), dict(out=PO[:, 0, tsl], in_=pap, func=AF.Copy, scale=SCALE),
                         reads=[bk], writes=[("po", 0)])
                elif ch == 3:
                    K.op("act", "activation", (), dict(out=PO[:, 3, tsl], in_=pap, func=AF.Silu),
                         reads=[bk], writes=[("po", 3)])
                else:
                    K.op("dve", "tensor_copy", (PO[:, ch, tsl], pap), reads=[bk], writes=[("po", ch)])
            proj_pass(l, hh * 4, 4, evac, Wv=Wv_in)
            K.barrier()
            ncol0 = (hh + 1) * 4
            Wv_next = load_w(win_d[l, :, ncol0 * 128:(ncol0 + 4) * 128], 512)
            if l + 1 < L:
                mods_load(l + 1, hh, STG)
            for pi, d in enumerate(PATTERNS):
                Wm = amask[:, hh * 3 + pi, :]
                Ls = T // d
                nkt = Ls // 128
                for idx in range(32):
                    r, m = idx // nkt, idx % nkt
                    tok0 = r + d * 128 * m
                    half = (idx // 4) % 2
                    PTb, ptk = PTS[half]
                    K.op("pe", "transpose", (PTb[:, idx % 4, :], PO[:, 2, ssl(tok0, 128, d)], ident[:]),
                         reads=[("po", 2), "ident"], writes=[ptk], sig=(idx % 4 == 3))
                    if idx % 4 == 3:
                        if half == 0:
                            K.op("act", "activation", (), dict(out=VT[:, idx - 3:idx + 1, :], in_=PTb[:, 0:4, :], func=AF.Copy),
                                 reads=[ptk], writes=[("vt", idx // 4)])
                        else:
                            K.op("dve", "tensor_copy", (VT[:, idx - 3:idx + 1, :], PTb[:, 0:4, :]),
                                 reads=[ptk], writes=[("vt", idx // 4)])
                tiles = [(r, m) for r in range(d) for m in range(nkt)]

                def geom(r, m):
                    c_lo = 64 if m == 0 else 0
                    c_hi = 192 if m == nkt - 1 else 256
                    return c_lo, c_hi

                def emit_S(i):
                    r, m = tiles[i]
                    c_lo, c_hi = geom(r, m)
                    nq = c_hi - c_lo
                    tq0 = r + d * (128 * m - 64 + c_lo)
                    kt0 = r + d * 128 * m
                    sbk = i % NSL
                    Sb_, Sk_ = SBK[sbk]
                    K.op("pe", "matmul", (Sb_[:, 0:nq], PO[:, 1, ssl(kt0, 128, d)], PO[:, 0, ssl(tq0, nq, d)]),
                         dict(start=True, stop=True), reads=[("po", 0), ("po", 1)], writes=[Sk_])

                def emit_EP(i):
                    r, m = tiles[i]
                    c_lo, c_hi = geom(r, m)
                    nq = c_hi - c_lo
                    sbk = i % NSL
                    Sb_, Sk_ = SBK[sbk]
                    K.op("act", "activation", (), dict(out=Es[sbk][:, 0:nq], in_=Sb_[:, 0:nq], func=AF.Exp),
                         reads=[Sk_], writes=[("E", sbk)])
                    K.op("dve", "tensor_tensor", (Ps[sbk][:, c_lo:c_hi], Es[sbk][:, 0:nq], Wm[:, c_lo:c_hi], ALU.mult),
                         reads=[("E", sbk), "amask"], writes=[("P", sbk)])

                def emit_PV(i):
                    r, m = tiles[i]
                    c_lo, c_hi = geom(r, m)
                    sbk = i % NSL
                    vt = VT[:, r * nkt + m, :]
                    vtk = ("vt", (r * nkt + m) // 4)
                    bka = (m // 4) % 2
                    ca = (m % 4) * 128
                    for which, lhs, base in (("o", vt, 2), ("d", ones_bf[:, :], 4)):
                        K.op("pe", "matmul", (PB[base + bka][:, ca + c_lo:ca + 128], lhs, Ps[sbk][:, c_lo:128]),
                             dict(start=(m == 0), stop=True), reads=[("P", sbk), vtk, "ones_bf"],
                             writes=[("pb", base + bka)], sig=(which == "d"))
                    bkb = ((m + 1) // 4) % 2
                    cb = ((m + 1) % 4) * 128
                    for which, lhs, base in (("o", vt, 2), ("d", ones_bf[:, :], 4)):
                        K.op("pe", "matmul", (PB[base + bkb][:, cb:cb + c_hi - 128], lhs, Ps[sbk][:, 128:c_hi]),
                             dict(start=True, stop=(m == nkt - 1)), reads=[("P", sbk), vtk, "ones_bf"],
                             writes=[("pb", base + bkb)], sig=(which == "d"))
                    groups = []
                    if m % 4 == 3:
                        groups.append(m // 4)
                    if m == nkt - 1:
                        groups.append(nkt // 4)
                    for k in groups:
                        jmax = min(4 * k + 3, nkt)
                        lo = 64 if k == 0 else 0
                        hi = (jmax % 4) * 128 + (64 if jmax == nkt else 128)
                        sub0 = 128 * 4 * k - 64 + lo
                        n = hi - lo
                        t0 = r + d * sub0
                        bk = k % 2
                        for acc, base, key in ((acc_o, 2, "acco"), (acc_d, 4, "accd")):
                            dst = acc[:, ssl(t0, n, d)]
                            if pi == 0:
                                K.op("dve", "tensor_copy", (dst, PB[base + bk][:, lo:hi]), reads=[("pb", base + bk)], writes=[key])
                            else:
                                K.op("dve", "tensor_tensor", (dst, dst, PB[base + bk][:, lo:hi], ALU.add),
                                     reads=[("pb", base + bk), key], writes=[key])

                LA = NSL - 1
                for j in range(min(LA, len(tiles))):
                    emit_S(j)
                for i in range(len(tiles)):
                    emit_EP(i)
                    if i + LA < len(tiles):
                        emit_S(i + LA)
                    emit_PV(i)
            if l + 1 < L:
                mods_mm(l + 1, hh, STG)
            allev = [(e, K.cnt[e]) for e in ("pe", "act", "dve", "pool") if K.cnt.get(e, 0) > 0]
            K.wait_only("sp", allev)
            for tt in range(NTT):
                tsl = slice(tt * TT, (tt + 1) * TT)
                K.op("dve", "reciprocal", (acc_d[:, tsl], acc_d[:, tsl]), reads=["accd"], writes=[("accd", tt)])
                K.op("pool", "tensor_tensor", (acc_o[:, tsl], acc_o[:, tsl], acc_d[:, tsl], ALU.mult), reads=["acco", ("accd", tt)], writes=[("acco", tt)])
                K.op("pool", "tensor_tensor", (ya_stage[:, tt, 0:TT], acc_o[:, tsl], PO[:, 3, tsl], ALU.mult),
                     reads=[("acco", tt), ("accd", tt), ("po", 3)], writes=[("yast", tt)])
            K.dma("sp", ag3a_in[:, hh * 128:(hh + 1) * 128, :].rearrange("n p t -> p n t"),
                  ya_stage[:, :, 0:TT], reads=[("yast", tt) for tt in range(NTT)], writes=[("ag3a_in", tt) for tt in range(NTT)])
            return Wv_next

        Sb_all = MTa[:, 0:8192].rearrange("p (n e) -> p n e", e=256)
        VTr = MTa[:, 8192:16384].rearrange("p (n e) -> p n e", e=256)
        kzs = [WBa[:, 128 * i:128 * (i + 1)] for i in range(2)]
        Prs = [WBa[:, 256 + 128 * i:256 + 128 * (i + 1)] for i in range(2)]
        qxi = [WBa[:, 512 + 512 * i:512 + 512 * (i + 1)] for i in range(2)]
        qxi2 = [qxi, [WBa[:, 5120 + 512 * i:5120 + 512 * (i + 1)] for i in range(2)]]
        Sst = [WBa[:, 1536 + 512 * i:1536 + 512 * (i + 1)].bitcast(F32) for i in range(2)]
        Sfb = [WBa[:, 2560 + 256 * i:2560 + 256 * (i + 1)] for i in range(2)]
        sqs = WBa[:, 3072:4096].rearrange("p (c t) -> p c t", c=2)
        rs = WBa[:, 4096:5120].bitcast(F32)
        RC = 768
        ysr = [MTa[:, 8192 + 1024 * i:8192 + 1024 * (i + 1)].rearrange("p (c t) -> p c t", c=2) for i in range(2)]
        Wo = [None]

        def retention_head(l, Wv_b1):
            lgf = lg[:, 2 * l:2 * l + 1]
            lgb = lg[:, 2 * l + 1:2 * l + 2]
            K.op("act", "activation", (), dict(out=rM[:], in_=rconst[:, 0:128], func=AF.Exp, scale=lgf), reads=["rconst", "lg"], writes=["rM"])
            K.op("dve", "tensor_tensor", (rM[:], rM[:], rconst[:, 128:256], ALU.mult), reads=["rM", "rconst"], writes=["rM"])
            K.op("act", "activation", (), dict(out=rtmp[:], in_=rconst[:, 256:384], func=AF.Exp, scale=lgb), reads=["rconst", "lg"], writes=["rtmp"])
            K.op("dve", "tensor_tensor", (rtmp[:], rtmp[:], rconst[:, 384:512], ALU.mult), reads=["rtmp", "rconst"], writes=["rtmp"])
            K.op("dve", "tensor_tensor", (rM[:], rM[:], rtmp[:], ALU.add), reads=["rM", "rtmp"], writes=["rM"])
            K.op("act", "activation", (), dict(out=rxi[:, 0, :], in_=rconst[:, 512:640], func=AF.Exp, scale=lgf), reads=["rconst", "lg"], writes=["rxi"])
            K.op("act", "activation", (), dict(out=rxi[:, 1, :], in_=rconst[:, 640:768], func=AF.Exp, scale=lgb), reads=["rconst", "lg"], writes=["rxi"])
            K.op("act", "activation", (), dict(out=rcol[:, 0:1], in_=rconst[:, RC:RC + 1], func=AF.Exp, scale=lgf), reads=["rconst", "lg"], writes=["rcol"])
            K.op("act", "activation", (), dict(out=rcol[:, 1:2], in_=rconst[:, RC + 1:RC + 2], func=AF.Exp, scale=lgb), reads=["rconst", "lg"], writes=["rcol"])
            K.op("act", "activation", (), dict(out=rcol[:, 2:3], in_=rconst[:, RC + 2:RC + 3], func=AF.Exp, scale=lgf), reads=["rconst", "lg"], writes=["rcol"])
            K.op("act", "activation", (), dict(out=rcol[:, 3:4], in_=rconst[:, RC + 2:RC + 3], func=AF.Exp, scale=lgb), reads=["rconst", "lg"], writes=["rcol"])

            def evac(ch, tt, pap, bk):
                tsl = slice(tt * TT, (tt + 1) * TT)
                if ch == 1:
                    K.op("act", "activation", (), dict(out=PO[:, 1, tsl], in_=pap, func=AF.Copy, scale=SCALE),
                         reads=[bk], writes=[("po", 1)])
                elif ch == 0:
                    K.op("act", "activation", (), dict(out=PO[:, 0, tsl], in_=pap, func=AF.Copy), reads=[bk], writes=[("po", 0)])
                else:
                    K.op("dve", "tensor_copy", (PO[:, ch, tsl], pap), reads=[bk], writes=[("po", ch)])
            for tt in range(NTT):
                K.cc([ag3a_in[tt]], [ag3a_out[tt]], GROUPS, reads=[("ag3a_in", tt)], writes=[("ag3a_out", tt)])
            proj_pass(l, 8, 4, evac, Wv=Wv_b1)
            K.barrier()
            Wv_b2 = load_w(win_d[l, :, 12 * 128:14 * 128], 256)
            if l + 1 < L:
                mods_load(l + 1, 2, STG)
            K.op("dve", "memset", (Sst[0], 0.0), writes=["Sf"])
            K.op("dve", "memset", (Sst[1], 0.0), writes=["Sb"])
            def bwd_A(n):
                csl = slice(n * 128, (n + 1) * 128)
                half = n % 2
                PTb, ptk = PTS[half]
                for c2 in range(2):
                    K.op("pe", "transpose", (PTb[:, c2, :], PO[:, 2 + c2, csl], ident[:]),
                         reads=[("po", 2 + c2), "ident"], writes=[ptk], sig=False)
                K.op("pe", "transpose", (PTb[:, 2, :], PO[:, 1, csl], ident[:]),
                     reads=[("po", 1), "ident"], writes=[ptk], sig=True)
                K.op("act", "activation", (), dict(out=VTr[:, n, :].rearrange("p (c e) -> p c e", c=2), in_=PTb[:, 0:2, :], func=AF.Copy),
                     reads=[ptk], writes=[("vtr", n)])
                K.op("dve", "tensor_scalar", (kzs[half], PTb[:, 2, :], rcol[:, 1:2], None, ALU.mult),
                     reads=[ptk, "rcol"], writes=[("kz", half)])

            def bwd_B(n):
                half = n % 2
                K.op("act", "activation", (), dict(out=Sb_all[:, n, :], in_=Sst[1], func=AF.Copy), reads=["Sb"], writes=[("sball", n)])
                b = next_pb()
                K.op("pe", "matmul", (PB[b][:, 0:256], kzs[half], VTr[:, n, :]), dict(start=True, stop=True),
                     reads=[("kz", half), ("vtr", n)], writes=[("pb", b)])
                K.op("dve", "scalar_tensor_tensor", (Sst[1], Sst[1], rcol[:, 3:4], PB[b][:, 0:256], ALU.mult, ALU.add),
                     reads=[("pb", b), "Sb", "rcol"], writes=["Sb"])

            bwd_A(31)
            for n in range(31, -1, -1):
                if n > 0:
                    bwd_A(n - 1)
                bwd_B(n)

            def fwd_G(gq):
                qs = gq % 2
                for ci in range(4):
                    csl = slice(gq * TT + ci * 128, gq * TT + (ci + 1) * 128)
                    K.op("pool", "tensor_tensor", (qxi2[qs][0][:, ci * 128:(ci + 1) * 128], PO[:, 0, csl], rxi[:, 0, :], ALU.mult),
                         reads=[("po", 0), "rxi"], writes=[("qxi", qs, 0)])
                    K.op("pool", "tensor_tensor", (qxi2[qs][1][:, ci * 128:(ci + 1) * 128], PO[:, 0, csl], rxi[:, 1, :], ALU.mult),
                         reads=[("po", 0), "rxi"], writes=[("qxi", qs, 1)])

            def fwd_A(n):
                csl = slice(n * 128, (n + 1) * 128)
                half = n % 2
                K.op("pe", "transpose", (PT[:, 0, :], PO[:, 1, csl], ident[:]),
                     reads=[("po", 1), "ident"], writes=["pt"], sig=True)
                K.op("dve", "tensor_scalar", (kzs[half], PT[:, 0, :], rcol[:, 0:1], None, ALU.mult),
                     reads=["pt", "rcol"], writes=[("kz", half)])
                b = next_pb()
                K.op("pe", "matmul", (PB[b][:, 0:128], PO[:, 1, csl], PO[:, 0, csl]), dict(start=True, stop=True),
                     reads=[("po", 0), ("po", 1)], writes=[("pb", b)])
                K.op("dve", "tensor_tensor", (Prs[half], PB[b][:, 0:128], rM[:], ALU.mult), reads=[("pb", b), "rM"], writes=[("Pr", half)])

            def fwd_B(n):
                gq, ci = divmod(n, 4)
                qs = gq % 2
                half = n % 2
                ob = 2 + 2 * (gq % 2)
                for c2 in range(2):
                    dst = PB[ob + c2][:, ci * 128:(ci + 1) * 128]
                    terms = [(VTr[:, n, c2 * 128:(c2 + 1) * 128], Prs[half], [("vtr", n), ("Pr", half)])]
                    if n > 0:
                        terms.append((Sfb[n % 2][:, c2 * 128:(c2 + 1) * 128], qxi2[qs][0][:, ci * 128:(ci + 1) * 128], [("Sfb", n % 2), ("qxi", qs, 0)]))
                    if n < 31:
                        terms.append((Sb_all[:, n, c2 * 128:(c2 + 1) * 128], qxi2[qs][1][:, ci * 128:(ci + 1) * 128], [("sball", n), ("qxi", qs, 1)]))
                    for ti, (lhs, rhs, rk) in enumerate(terms):
                        K.op("pe", "matmul", (dst, lhs, rhs), dict(start=(ti == 0), stop=(ti == len(terms) - 1)),
                             reads=rk, writes=[("pb", ob + c2)], sig=(ti == len(terms) - 1))
                b = next_pb()
                K.op("pe", "matmul", (PB[b][:, 0:256], kzs[half], VTr[:, n, :]), dict(start=True, stop=True),
                     reads=[("kz", half), ("vtr", n)], writes=[("pb", b)])
                K.op("dve", "scalar_tensor_tensor", (Sst[0], Sst[0], rcol[:, 2:3], PB[b][:, 0:256], ALU.mult, ALU.add),
                     reads=[("pb", b), "Sf", "rcol"], writes=["Sf"])
                K.op("act", "activation", (), dict(out=Sfb[(n + 1) % 2], in_=Sst[0], func=AF.Copy), reads=["Sf"], writes=[("Sfb", (n + 1) % 2)])

            def fwd_N(gq):
                tsl = slice(gq * TT, (gq + 1) * TT)
                ob = 2 + 2 * (gq % 2)
                for c2 in range(2):
                    K.op("act", "activation", (), dict(out=sqs[:, c2, :], in_=PB[ob + c2][:, :], func=AF.Square),
                         reads=[("pb", ob + c2)], writes=[("sq", c2)])
                for c2 in range(2):
                    K.op("pe", "matmul", (PM[:, :], ones_bf[:, :], sqs[:, c2, :]), dict(start=(c2 == 0), stop=(c2 == 1)),
                         reads=[("sq", c2), "ones_bf"], writes=["pm"], sig=(c2 == 1))
                K.op("act", "activation", (), dict(out=rs, in_=PM[:, :], func=AF.Sqrt, bias=epsc[:, 1:2]),
                     reads=["pm", "epsc"], writes=["rs"])
                K.op("dve", RECIP, (rs, rs), reads=["rs"], writes=["rs"])
                for c2 in range(2):
                    K.op("dve", "scalar_tensor_tensor", (PO[:, 2 + c2, tsl], PB[ob + c2][:, :], 16.0, rs, ALU.mult, ALU.mult),
                         reads=[("pb", ob + c2), "rs"], writes=[("po", 2 + c2)])

            fwd_G(0)
            fwd_A(0)
            for n in range(32):
                if n + 1 < 32:
                    if (n + 1) % 4 == 0:
                        fwd_G((n + 1) // 4)
                    fwd_A(n + 1)
                fwd_B(n)
                if n % 4 == 3:
                    fwd_N(n // 4)
            if l + 1 < L:
                mods_mm(l + 1, 2, STG)
                mods_finish(l + 1)
            K.barrier()
            Wo[0] = load_w(wout_d[l, :, :], 512, dst=MTa, key="Wo")

            def evac2(ch, tt, pap, bk):
                tsl = slice(tt * TT, (tt + 1) * TT)
                K.op("act", "activation", (), dict(out=PO[:, ch, tsl], in_=pap, func=AF.Silu), reads=[bk], writes=[("po", ch)])

            def post(tt):
                tsl = slice(tt * TT, (tt + 1) * TT)
                ys = ysr[tt % 2]
                for c2 in range(2):
                    K.op("dve", "tensor_tensor", (ys[:, c2, :], PO[:, 2 + c2, tsl], PO[:, c2, tsl], ALU.mult),
                         reads=[("po", 2 + c2), ("po", c2)], writes=[("ysr", tt % 2)])
                K.dma("pool", ag3r_in[tt].rearrange("(c p) t -> p c t", p=128), ys, reads=[("ysr", tt % 2)], writes=[("ag3r_in", tt)])
                K.cc([ag3r_in[tt]], [ag3r_out[tt]], GROUPS, reads=[("ag3r_in", tt)], writes=[("ag3r_out", tt)])
            proj_pass(l, 12, 2, evac2, Wv=Wv_b2, post_tile=post)
            K.barrier()

        ystage = HY[:, 0, :, :].rearrange("p k t -> p (k t)")
        for l in range(L):
            A_ = modA[:, l * 4:l * 4 + 4]
            B_ = modB[:, l * 4:l * 4 + 4]
            G_ = modG[:, l * 4:l * 4 + 4]
            W_A0 = load_w(win_d[l, :, 0:512], 512)
            norm_stats()
            if stop == "n1":
                finish_raw()
                return nc
            for tt in range(NTT):
                tsl = slice(tt * TT, (tt + 1) * TT)
                slot = tt % 2
                for c in range(4):
                    K.op("dve", "scalar_tensor_tensor", (tA[:], X[:, c, tsl], A_[:, c:c + 1], rstd_all[:, tsl], ALU.mult, ALU.mult),
                         reads=[("x", c, tt), ("rstd", tt), ("modA", l)], writes=["tA"])
                    K.op("act", "activation", (), dict(out=HY[:, slot, c, :], in_=tA[:], func=AF.Identity, bias=B_[:, c:c + 1]),
                         reads=["tA", ("modB", l)], writes=[("hy", slot)])
                K.dma("sp", ag2_in[tt].rearrange("(c p) t -> p c t", p=128), HY[:, slot, 0:4, :],
                      reads=[("hy", slot)], writes=[("ag2_in", tt)])
                K.cc([ag2_in[tt]], [ag2_out[tt]], GROUPS, reads=[("ag2_in", tt)], writes=[("ag2_out", tt)])
            if dbg and l == 0:
                for tt in range(NTT):
                    load_tile(ag2_out, tt, tt % 2)
                    K.dma("sp", dbg_d["h"].rearrange("(k p) t -> p k t", p=128)[:, :, tt * TT:(tt + 1) * TT], HY[:, tt % 2, :, :],
                          reads=[("hy", tt % 2)])

            if stop == "n2":
                finish_raw()
                return nc
            if mix:
                Wn = attention_head(l, 0, W_A0)
                Wn = attention_head(l, 1, Wn)
                retention_head(l, Wn)
            if dbg and l == 0:
                for tt in range(NTT):
                    load_tile(None, tt, tt % 2)
                    K.dma("sp", dbg_d["y"].rearrange("(k p) t -> p k t", p=128)[:, :, tt * TT:(tt + 1) * TT], HY[:, tt % 2, :, :],
                          reads=[("hy", tt % 2)])

            if stop == "m":
                finish_raw()
                return nc
            Wv = Wo[0]
            for tt in range(NTT):
                tsl = slice(tt * TT, (tt + 1) * TT)
                slot = tt % 2
                load_tile(None, tt, slot)
                for c in range(4):
                    b = next_pb()
                    for kc in range(16):
                        K.op("pe", "matmul", (PB[b][:, :], Wv[:, kc, c * 128:(c + 1) * 128], HY[:, slot, kc, :]),
                             dict(start=(kc == 0), stop=(kc == 15)), reads=["Wo", ("hy", slot)], writes=[("pb", b)],
                             sig=(kc == 15))
                    K.op("dve", "scalar_tensor_tensor", (X[:, c, tsl], PB[b][:, :], G_[:, c:c + 1], X[:, c, tsl], ALU.mult, ALU.add),
                         reads=[("pb", b), ("x", c, tt), ("modG", l)], writes=[("x", c, tt)])

        if stop == "o":
            finish_raw()
            return nc
        norm_stats()
        outs = []
        for tt in range(NTT):
            tsl = slice(tt * TT, (tt + 1) * TT)
            for c in range(4):
                K.op("dve", "scalar_tensor_tensor", (tA[:], X[:, c, tsl], fgain[:, c:c + 1], rstd_all[:, tsl], ALU.mult, ALU.mult),
                     reads=[("x", c, tt), ("rstd", tt), "fgain"], writes=["tA"])
                outs.append(K.dma("sp", out_d[c * 128:(c + 1) * 128, tsl], tA[:], reads=["tA"]))
        K.wait_only("sp", outs)
        K.emit()
    return nc


def _host_consts():
    p = np.arange(128)[:, None]
    c = np.arange(256)[None, :]
    rel = np.abs(c - 64 - p).astype(np.float64)
    return rel


def prep_inputs(x, c, norm_gain, w_ada, b_ada, w_in, w_out, ret_decay_logit_f, ret_decay_logit_b, final_gain, L=NL):
    f32 = np.float32
    x = np.asarray(x, f32); c = np.asarray(c, f32)
    norm_gain = np.asarray(norm_gain, f32); w_ada = np.asarray(w_ada, f32)[:L]; b_ada = np.asarray(b_ada, f32)
    w_in = np.asarray(w_in, f32)[:L]; w_out = np.asarray(w_out, f32)[:L]
    dlf = np.asarray(ret_decay_logit_f, f32); dlb = np.asarray(ret_decay_logit_b, f32)
    final_gain = np.asarray(final_gain, f32)
    rel = _host_consts()
    ident = np.eye(128, dtype=f32)
    j = np.arange(128)[:, None].astype(np.float64)
    i = np.arange(128)[None, :].astype(np.float64)
    Rf = np.maximum(i - j, 0); Uf = (i >= j).astype(np.float64)
    Rb = np.maximum(j - i, 0); Ub = (j >= i).astype(np.float64)
    I1 = np.broadcast_to(i + 1.0, (128, 128)); I2 = np.broadcast_to(128.0 - i, (128, 128))
    cols = np.concatenate([127.0 - j, j, np.full((128, 1), 128.0), np.zeros((128, 1))], axis=1)
    rconst = np.concatenate([Rf, Uf, Rb, Ub, I1, I2, cols], axis=1).astype(f32)
    in_maps = []
    for core in range(8):
        b, g = core // 4, core % 4
        dsl = slice(512 * g, 512 * g + 512)
        m = {}
        m["xT"] = np.ascontiguousarray(x[b].T[dsl, :])
        m["cT"] = np.ascontiguousarray(c[b].reshape(16, 128).T)
        m["wada"] = np.ascontiguousarray(np.concatenate([w_ada[:, :, dsl], w_ada[:, :, 2048:4096][:, :, dsl],
                                                         w_ada[:, :, 4096:6144][:, :, dsl]], axis=2))
        ba = np.concatenate([b_ada[:, dsl], b_ada[:, 2048:4096][:, dsl], b_ada[:, 4096:6144][:, dsl]], axis=1)
        m["bada"] = np.ascontiguousarray(ba.reshape(NL, 12, 128).transpose(2, 0, 1).reshape(128, NL * 12))
        m["gain"] = np.ascontiguousarray(norm_gain[:, dsl].reshape(NL, 4, 128).transpose(2, 0, 1).reshape(128, NL * 4))
        m["fgain"] = np.ascontiguousarray(final_gain[dsl].reshape(4, 128).T)
        cols_in = []
        for hh in (2 * g, 2 * g + 1):
            for base in (0, 1024, 2048, 3072):
                cols_in.append(np.arange(base + 128 * hh, base + 128 * hh + 128))
        cols_in.append(np.arange(4096 + 128 * g, 4096 + 128 * g + 128))
        cols_in.append(np.arange(4608 + 128 * g, 4608 + 128 * g + 128))
        cols_in.append(np.arange(5120 + 256 * g, 5120 + 256 * g + 256))
        cols_in.append(np.arange(6144 + 256 * g, 6144 + 256 * g + 256))
        cols_in = np.concatenate(cols_in)
        m["win"] = np.ascontiguousarray(w_in[:, :, cols_in])
        rows = []
        for g2 in range(4):
            rows.append(np.arange(128 * (2 * g2), 128 * (2 * g2) + 128))
            rows.append(np.arange(128 * (2 * g2 + 1), 128 * (2 * g2 + 1) + 128))
            rows.append(np.arange(1024 + 256 * g2, 1024 + 256 * g2 + 256))
        rows = np.concatenate(rows)
        m["wout"] = np.ascontiguousarray(w_out[:, :, dsl])
        dl = np.stack([dlf[:, g], dlb[:, g]], axis=1).reshape(1, NL * 2)
        m["dlog"] = np.ascontiguousarray(np.broadcast_to(dl, (128, NL * 2))).astype(f32)
        m["ident"] = ident
        am = []
        for hh in (2 * g, 2 * g + 1):
            slope = 2.0 ** (-(hh + 1.0))
            for d in PATTERNS:
                am.append(np.where(rel <= 64, np.exp(-slope * d * rel), 0.0))
        m["amask"] = np.ascontiguousarray(np.concatenate(am, axis=1)).astype(f32)
        m["rconst"] = rconst
        in_maps.append(m)
    return in_maps


def assemble(results):
    out = np.empty((2, T, D), np.float32)
    for core in range(8):
        b, g = core // 4, core % 4
        out[b, :, 512 * g:512 * g + 512] = results[core]["outT"].T
    return out


_NC_CACHE = {}


def kernel(x, c, norm_gain, w_ada, b_ada, w_in, w_out, ret_decay_logit_f, ret_decay_logit_b, final_gain):
    in_maps = prep_inputs(x, c, norm_gain, w_ada, b_ada, w_in, w_out, ret_decay_logit_f, ret_decay_logit_b, final_gain)
    nc = build()
    res = run_bass_kernel_spmd(nc, in_maps, core_ids=list(range(8)))
    return assemble(res.results)
```
